# Optimizing a Trainium2 kernel written in Bass

```python
import math
import jax, jax.numpy as jnp
from jax import lax
import numpy as np

D_MODEL = 2048
BATCH = 16
SEQ = 2048
DEPTH = 2

N_META = 16
CHUNK = 128
NORM_EPS = 1e-6
HEAD_NORM_EPS = 1e-5
S5_WIDTH = D_MODEL // 2
S5_GROUP_SIZE = 16
S5_GROUPS = S5_WIDTH // S5_GROUP_SIZE
S5_STATE = 64
MLSTM_WIDTH = 3 * D_MODEL // 2
MLSTM_HEADS = 8
MLSTM_HEAD_DIM = MLSTM_WIDTH // MLSTM_HEADS
MLSTM_CONV = 4
QKV_BLOCK = 4
AB_INNER = S5_WIDTH + MLSTM_WIDTH
AB_IN = 2 * AB_INNER
SSD_INNER = 2 * D_MODEL
SSD_HEAD_DIM = 64
SSD_HEADS = SSD_INNER // SSD_HEAD_DIM
SSD_STATE = 128
SSD_GROUPS = 8
SSD_HPG = SSD_HEADS // SSD_GROUPS
SSD_CONV = 4
SSD_CONV_DIM = SSD_INNER + 2 * SSD_GROUPS * SSD_STATE
SSD_IN = SSD_INNER + SSD_CONV_DIM + SSD_HEADS
N_EVEN = (DEPTH + 1) // 2
N_ODD = DEPTH // 2

kernel_name = 'hybrid_s5_mlstm_ssd_meta'

F32 = jnp.float32


def rmsnorm(x, g):
    xf = x.astype(F32)
    y = xf * lax.rsqrt(jnp.mean(xf * xf, axis=-1, keepdims=True) + NORM_EPS)
    return (y * g.astype(F32)).astype(x.dtype)


def causal_dwconv(x, w, b):
    k, c = w.shape
    y = lax.conv_general_dilated(x, w[:, None, :].astype(x.dtype), window_strides=(1,),
                                 padding=[(k - 1, 0)], dimension_numbers=('NWC', 'WIO', 'NWC'),
                                 feature_group_count=c)
    return y + b.astype(x.dtype)


def run_chunked(step, carry, seqs):
    bsz = seqs[0].shape[0]
    carry, y_meta = step(carry, tuple(a[:, :N_META] for a in seqs))

    def to_chunks(a):
        r = a[:, N_META:]
        r = r.reshape(bsz, r.shape[1] // CHUNK, CHUNK, *r.shape[2:])
        return jnp.moveaxis(r, 1, 0)

    _, y_real = lax.scan(step, carry, tuple(to_chunks(a) for a in seqs))
    y_real = jnp.moveaxis(y_real, 0, 1)
    y_real = y_real.reshape(bsz, -1, *y_real.shape[3:])
    return jnp.concatenate([y_meta, y_real], axis=1)


def cmul(ar, ai, br, bi):
    return ar * br - ai * bi, ar * bi + ai * br


def s5_mixer(u, lam_re, lam_im, log_dt, b_re, b_im, c_re, c_im, d, glu_w, glu_b):
    bsz, t_len, _ = u.shape
    uf = u.astype(F32).reshape(bsz, t_len, S5_GROUPS, S5_GROUP_SIZE)
    dt = jnp.exp(log_dt.astype(F32))[:, None]
    lr, li = lam_re.astype(F32), lam_im.astype(F32)
    mag = jnp.exp(lr * dt)
    ar, ai = mag * jnp.cos(li * dt), mag * jnp.sin(li * dt)
    den = lr * lr + li * li
    qr = ((ar - 1.0) * lr + ai * li) / den
    qi = (ai * lr - (ar - 1.0) * li) / den
    bbr, bbi = cmul(qr[..., None], qi[..., None], b_re.astype(F32), b_im.astype(F32))
    cr, ci = c_re.astype(F32), c_im.astype(F32)

    def combine(e1, e2):
        a1r, a1i, s1r, s1i = e1
        a2r, a2i, s2r, s2i = e2
        pr, pi_ = cmul(a1r, a1i, a2r, a2i)
        tr, ti = cmul(a2r, a2i, s1r, s1i)
        return pr, pi_, tr + s2r, ti + s2i

    def step(carry, inp):
        sr, si = carry
        (uc,) = inp
        bur = jnp.einsum('bcgh,gph->bcgp', uc, bbr)
        bui = jnp.einsum('bcgh,gph->bcgp', uc, bbi)
        shp = bur.shape
        pr, pi_, xr, xi = lax.associative_scan(
            combine, (jnp.broadcast_to(ar, shp), jnp.broadcast_to(ai, shp), bur, bui), axis=1)
        hr, hi = cmul(pr, pi_, sr[:, None], si[:, None])
        xr, xi = xr + hr, xi + hi
        y = jnp.einsum('ghp,bcgp->bcgh', cr, xr) - jnp.einsum('ghp,bcgp->bcgh', ci, xi)
        return (xr[:, -1], xi[:, -1]), y

    zeros = jnp.zeros((bsz, S5_GROUPS, S5_STATE), F32)
    y = run_chunked(step, (zeros, zeros), (uf,))
    y = (y + d.astype(F32).reshape(S5_GROUPS, S5_GROUP_SIZE) * uf).reshape(bsz, t_len, S5_WIDTH)
    g = jax.nn.gelu(y)
    return g * jax.nn.sigmoid(g @ glu_w.astype(F32) + glu_b.astype(F32))


def mlstm_mixer(xm, conv_w, conv_b, wq, wk, wv, w_gate, b_gate, norm_w, skip):
    bsz, t_len, _ = xm.shape
    xf = xm.astype(F32)
    xc = jax.nn.silu(causal_dwconv(xf, conv_w.astype(F32), conv_b.astype(F32)))

    def headwise(a, w):
        nb = w.shape[0]
        return jnp.einsum('btni,nio->btno', a.reshape(bsz, t_len, nb, QKV_BLOCK),
                          w.astype(F32)).reshape(bsz, t_len, MLSTM_WIDTH)

    q = headwise(xc, wq)
    k = headwise(xc, wk)
    v = headwise(xf, wv)
    wg = w_gate.astype(F32)
    gates = (q @ wg[:MLSTM_WIDTH] + k @ wg[MLSTM_WIDTH:2 * MLSTM_WIDTH]
             + v @ wg[2 * MLSTM_WIDTH:] + b_gate.astype(F32))
    ig = gates[..., :MLSTM_HEADS]
    lf = jax.nn.log_sigmoid(gates[..., MLSTM_HEADS:])
    hs = (bsz, t_len, MLSTM_HEADS, MLSTM_HEAD_DIM)
    qh = q.reshape(hs) * (MLSTM_HEAD_DIM ** -0.5)
    kh = k.reshape(hs)
    vh = v.reshape(hs)

    def step(carry, inp):
        cmat, nvec, m = carry
        qc, kc, vc, igc, lfc = inp
        c = qc.shape[1]
        bcum = jnp.cumsum(lfc, axis=1)
        causal = jnp.tril(jnp.ones((c, c), bool))[None, :, :, None]
        dmat = bcum[:, :, None, :] - bcum[:, None, :, :] + igc[:, None, :, :]
        dmat = jnp.where(causal, dmat, -jnp.inf)
        inter = bcum + m[:, None, :]
        mt = jnp.maximum(inter, dmat.max(axis=2))
        wt = jnp.exp(dmat - mt[:, :, None, :])
        w_prev = jnp.exp(inter - mt)
        s = jnp.einsum('bthd,bshd->btsh', qc, kc) * wt
        num = (jnp.einsum('btsh,bshe->bthe', s, vc)
               + w_prev[..., None] * jnp.einsum('bthd,bhde->bthe', qc, cmat))
        den = s.sum(axis=2) + w_prev * jnp.einsum('bthd,bhd->bth', qc, nvec)
        h = num / jnp.maximum(jnp.abs(den), jnp.exp(-mt))[..., None]
        blast = bcum[:, -1]
        g = blast[:, None, :] - bcum + igc
        m_new = jnp.maximum(blast + m, g.max(axis=1))
        decay = jnp.exp(blast + m - m_new)
        wkc = jnp.exp(g - m_new[:, None, :])[..., None] * kc
        c_new = decay[..., None, None] * cmat + jnp.einsum('bshd,bshe->bhde', wkc, vc)
        n_new = decay[..., None] * nvec + wkc.sum(axis=1)
        return (c_new, n_new, m_new), h

    carry0 = (jnp.zeros((bsz, MLSTM_HEADS, MLSTM_HEAD_DIM, MLSTM_HEAD_DIM), F32),
              jnp.zeros((bsz, MLSTM_HEADS, MLSTM_HEAD_DIM), F32),
              jnp.zeros((bsz, MLSTM_HEADS), F32))
    h = run_chunked(step, carry0, (qh, kh, vh, ig, lf))
    mu = jnp.mean(h, axis=-1, keepdims=True)
    var = jnp.mean(jnp.square(h - mu), axis=-1, keepdims=True)
    hn = (h - mu) * lax.rsqrt(var + HEAD_NORM_EPS) * norm_w.astype(F32).reshape(MLSTM_HEADS, MLSTM_HEAD_DIM)
    return hn.reshape(bsz, t_len, MLSTM_WIDTH) + skip.astype(F32) * xc


def ab_layer(x, norm_g, w_in, s5_lambda_re, s5_lambda_im, s5_log_dt, s5_b_re, s5_b_im, s5_c_re,
             s5_c_im, s5_d, s5_glu_w, s5_glu_b, ml_conv_w, ml_conv_b, ml_wq, ml_wk, ml_wv,
             ml_w_gate, ml_b_gate, ml_norm, ml_skip, w_out):
    p = rmsnorm(x, norm_g) @ w_in
    u_a, z_a, x_b, z_b = jnp.split(p, [S5_WIDTH, 2 * S5_WIDTH, 2 * S5_WIDTH + MLSTM_WIDTH], axis=-1)
    y_a = s5_mixer(u_a, s5_lambda_re, s5_lambda_im, s5_log_dt, s5_b_re, s5_b_im, s5_c_re, s5_c_im,
                   s5_d, s5_glu_w, s5_glu_b) * jax.nn.silu(z_a.astype(F32))
    y_b = mlstm_mixer(x_b, ml_conv_w, ml_conv_b, ml_wq, ml_wk, ml_wv, ml_w_gate, ml_b_gate,
                      ml_norm, ml_skip) * jax.nn.silu(z_b.astype(F32))
    y = jnp.concatenate([y_a, y_b], axis=-1).astype(x.dtype)
    return x + y @ w_out


def ssd_layer(x, norm_g, w_in, conv_w, conv_b, dt_bias, a_log, d, gnorm, w_out):
    bsz, t_len, _ = x.shape
    p = (rmsnorm(x, norm_g) @ w_in).astype(F32)
    z, xbc, dt = jnp.split(p, [SSD_INNER, SSD_INNER + SSD_CONV_DIM], axis=-1)
    xbc = jax.nn.silu(causal_dwconv(xbc, conv_w.astype(F32), conv_b.astype(F32)))
    xs, bm, cm = jnp.split(xbc, [SSD_INNER, SSD_INNER + SSD_GROUPS * SSD_STATE], axis=-1)
    xs = xs.reshape(bsz, t_len, SSD_GROUPS, SSD_HPG, SSD_HEAD_DIM)
    bm = bm.reshape(bsz, t_len, SSD_GROUPS, SSD_STATE)
    cm = cm.reshape(bsz, t_len, SSD_GROUPS, SSD_STATE)
    dt = jax.nn.softplus(dt + dt_bias.astype(F32)).reshape(bsz, t_len, SSD_GROUPS, SSD_HPG)
    a = -jnp.exp(a_log.astype(F32)).reshape(SSD_GROUPS, SSD_HPG)

    def step(state, inp):
        xc, dtc, bc, cc = inp
        c = xc.shape[1]
        cum = jnp.cumsum(dtc * a, axis=1)
        causal = jnp.tril(jnp.ones((c, c), bool))[None, :, :, None, None]
        seg = jnp.exp(jnp.where(causal, cum[:, :, None] - cum[:, None], -jnp.inf))
        cb = jnp.einsum('btgn,bsgn->btsg', cc, bc)
        w = cb[..., None] * seg * dtc[:, None]
        y = (jnp.einsum('btsgr,bsgrp->btgrp', w, xc)
             + jnp.exp(cum)[..., None] * jnp.einsum('btgn,bgrpn->btgrp', cc, state))
        last = cum[:, -1]
        dec = jnp.exp(last[:, None] - cum) * dtc
        state = (jnp.exp(last)[..., None, None] * state
                 + jnp.einsum('bsgr,bsgn,bsgrp->bgrpn', dec, bc, xc))
        return state, y

    state0 = jnp.zeros((bsz, SSD_GROUPS, SSD_HPG, SSD_HEAD_DIM, SSD_STATE), F32)
    y = run_chunked(step, state0, (xs, dt, bm, cm))
    y = y + d.astype(F32).reshape(SSD_GROUPS, SSD_HPG, 1) * xs
    y = y.reshape(bsz, t_len, SSD_INNER) * jax.nn.silu(z)
    yg = y.reshape(bsz, t_len, SSD_GROUPS, -1)
    yg = yg * lax.rsqrt(jnp.mean(yg * yg, axis=-1, keepdims=True) + NORM_EPS)
    y = yg.reshape(bsz, t_len, SSD_INNER) * gnorm.astype(F32)
    return x + y.astype(x.dtype) @ w_out


def setup_inputs(seed: int = 0) -> dict:
    key = jax.random.key(seed)
    ks = iter(jax.random.split(key, 48))
    nrm = lambda shape, s: jax.random.normal(next(ks), shape, F32) * s
    ne, no = N_EVEN, N_ODD
    lam_im = jnp.pi * jnp.arange(S5_STATE, dtype=F32)
    gate_b = jnp.concatenate([
        nrm((ne, MLSTM_HEADS), 0.1),
        jnp.linspace(3.0, 6.0, MLSTM_HEADS, dtype=F32)[None] + nrm((ne, MLSTM_HEADS), 0.01)], axis=-1)
    dt0 = jnp.exp(jax.random.uniform(next(ks), (no, SSD_HEADS), F32, math.log(1e-3), math.log(1e-1)))
    return {
        'x': nrm((BATCH, SEQ, D_MODEL), 1.0),
        'meta_tokens': nrm((N_META, D_MODEL), 1.0),
        'ab_norm': 1.0 + nrm((ne, D_MODEL), 0.02),
        'ab_w_in': nrm((ne, D_MODEL, AB_IN), D_MODEL ** -0.5),
        's5_lambda_re': -0.5 + nrm((ne, S5_GROUPS, S5_STATE), 0.01),
        's5_lambda_im': lam_im + nrm((ne, S5_GROUPS, S5_STATE), 0.01),
        's5_log_dt': jax.random.uniform(next(ks), (ne, S5_GROUPS), F32, math.log(1e-3), math.log(1e-1)),
        's5_b_re': nrm((ne, S5_GROUPS, S5_STATE, S5_GROUP_SIZE), (2 * S5_GROUP_SIZE) ** -0.5),
        's5_b_im': nrm((ne, S5_GROUPS, S5_STATE, S5_GROUP_SIZE), (2 * S5_GROUP_SIZE) ** -0.5),
        's5_c_re': nrm((ne, S5_GROUPS, S5_GROUP_SIZE, S5_STATE), (2 * S5_STATE) ** -0.5),
        's5_c_im': nrm((ne, S5_GROUPS, S5_GROUP_SIZE, S5_STATE), (2 * S5_STATE) ** -0.5),
        's5_d': nrm((ne, S5_WIDTH), 1.0),
        's5_glu_w': nrm((ne, S5_WIDTH, S5_WIDTH), S5_WIDTH ** -0.5),
        's5_glu_b': nrm((ne, S5_WIDTH), 0.01),
        'ml_conv_w': nrm((ne, MLSTM_CONV, MLSTM_WIDTH), MLSTM_CONV ** -0.5),
        'ml_conv_b': nrm((ne, MLSTM_WIDTH), 0.01),
        'ml_wq': nrm((ne, MLSTM_WIDTH // QKV_BLOCK, QKV_BLOCK, QKV_BLOCK), QKV_BLOCK ** -0.5),
        'ml_wk': nrm((ne, MLSTM_WIDTH // QKV_BLOCK, QKV_BLOCK, QKV_BLOCK), QKV_BLOCK ** -0.5),
        'ml_wv': nrm((ne, MLSTM_WIDTH // QKV_BLOCK, QKV_BLOCK, QKV_BLOCK), QKV_BLOCK ** -0.5),
        'ml_w_gate': nrm((ne, 3 * MLSTM_WIDTH, 2 * MLSTM_HEADS), (3 * MLSTM_WIDTH) ** -0.5),
        'ml_b_gate': gate_b,
        'ml_norm': 1.0 + nrm((ne, MLSTM_WIDTH), 0.02),
        'ml_skip': 1.0 + nrm((ne, MLSTM_WIDTH), 0.02),
        'ab_w_out': nrm((ne, AB_INNER, D_MODEL), AB_INNER ** -0.5),
        'ssd_norm': 1.0 + nrm((no, D_MODEL), 0.02),
        'ssd_w_in': nrm((no, D_MODEL, SSD_IN), D_MODEL ** -0.5),
        'ssd_conv_w': nrm((no, SSD_CONV, SSD_CONV_DIM), SSD_CONV ** -0.5),
        'ssd_conv_b': nrm((no, SSD_CONV_DIM), 0.01),
        'ssd_dt_bias': dt0 + jnp.log(-jnp.expm1(-dt0)),
        'ssd_a_log': jnp.log(jax.random.uniform(next(ks), (no, SSD_HEADS), F32, 1.0, 16.0)),
        'ssd_d': 1.0 + nrm((no, SSD_HEADS), 0.02),
        'ssd_gnorm': 1.0 + nrm((no, SSD_INNER), 0.02),
        'ssd_w_out': nrm((no, SSD_INNER, D_MODEL), SSD_INNER ** -0.5),
        'final_norm': 1.0 + nrm((D_MODEL,), 0.02),
    }


def reference(x, meta_tokens, ab_norm, ab_w_in, s5_lambda_re, s5_lambda_im, s5_log_dt, s5_b_re,
              s5_b_im, s5_c_re, s5_c_im, s5_d, s5_glu_w, s5_glu_b, ml_conv_w, ml_conv_b, ml_wq,
              ml_wk, ml_wv, ml_w_gate, ml_b_gate, ml_norm, ml_skip, ab_w_out, ssd_norm, ssd_w_in,
              ssd_conv_w, ssd_conv_b, ssd_dt_bias, ssd_a_log, ssd_d, ssd_gnorm, ssd_w_out, final_norm):
    bsz = x.shape[0]
    meta = jnp.broadcast_to(meta_tokens[None].astype(x.dtype), (bsz, N_META, x.shape[-1]))
    h = jnp.concatenate([meta, x], axis=1)
    for layer in range(DEPTH):
        i = layer // 2
        if layer % 2 == 0:
            h = ab_layer(h, ab_norm[i], ab_w_in[i], s5_lambda_re[i], s5_lambda_im[i], s5_log_dt[i],
                         s5_b_re[i], s5_b_im[i], s5_c_re[i], s5_c_im[i], s5_d[i], s5_glu_w[i],
                         s5_glu_b[i], ml_conv_w[i], ml_conv_b[i], ml_wq[i], ml_wk[i], ml_wv[i],
                         ml_w_gate[i], ml_b_gate[i], ml_norm[i], ml_skip[i], ab_w_out[i])
        else:
            h = ssd_layer(h, ssd_norm[i], ssd_w_in[i], ssd_conv_w[i], ssd_conv_b[i], ssd_dt_bias[i],
                          ssd_a_log[i], ssd_d[i], ssd_gnorm[i], ssd_w_out[i])
    return rmsnorm(h, final_norm)[:, N_META:]
```

```python
import contextlib
import numpy as np
import concourse.bass as bass
import concourse.mybir as mybir
from concourse.bass_utils import run_bass_kernel_spmd

F32 = mybir.dt.float32
BF16 = mybir.dt.bfloat16
AF = mybir.ActivationFunctionType
ALU = mybir.AluOpType
AX = mybir.AxisListType


class Eng:
    def __init__(self, name, h, sem, is_pe=False):
        self.name, self.h, self.sem, self.is_pe = name, h, sem, is_pe
        self.count = 0
        self.pending = False
        self.known = {}


class T:
    __slots__ = ("name", "w", "r", "dsem", "dcount")

    def __init__(self, name):
        self.name = name
        self.w = None
        self.r = {}
        self.dsem = None
        self.dcount = 0


class FW:
    def __init__(self, nc, stack):
        self.nc, self.stack = nc, stack
        self.root = stack
        self.nsem = 0
        self.free_dsems = []
        self.consts = {}
        self.pe = Eng("pe", nc.tensor, self.sem("pe"), is_pe=True)
        self.dve = Eng("dve", nc.vector, self.sem("dve"))
        self.act = Eng("act", nc.scalar, self.sem("act"))
        self.pool = Eng("pool", nc.gpsimd, self.sem("pool"))
        self.sp = Eng("sp", nc.sync, self.sem("sp"))
        self.engs = [self.pe, self.dve, self.act, self.pool, self.sp]
        self.ninstr = 0
        self.dma_owners = []
        self.ident_dram = None

    def sem(self, name):
        self.nsem += 1
        self.uid = getattr(self, "uid", 0) + 1
        return self.root.enter_context(self.nc.semaphore("%s_u%d" % (name, self.uid)))

    def sb(self, name, shape, dt):
        self.uid = getattr(self, "uid", 0) + 1
        return self.stack.enter_context(self.nc.sbuf_tensor("%s_u%d" % (name, self.uid), list(shape), dt))

    def ps(self, name, shape, dt):
        self.uid = getattr(self, "uid", 0) + 1
        return self.stack.enter_context(self.nc.psum_tensor("%s_u%d" % (name, self.uid), list(shape), dt))

    def _wait(self, eng, tok):
        if tok is None:
            return
        sem, val, src = tok
        if src is eng and eng.is_pe:
            return
        k = id(sem)
        if eng.known.get(k, 0) >= val:
            return
        eng.h.wait_ge(sem, val)
        eng.known[k] = val
        self.ninstr += 1

    def _deps(self, eng, reads, writes):
        for t in reads:
            self._wait(eng, t.w)
        for t in writes:
            self._wait(eng, t.w)
            for tok in t.r.values():
                self._wait(eng, tok)

    def _record(self, tok, reads, writes):
        for t in reads:
            t.r[id(tok[0])] = tok
        for t in writes:
            t.w = tok
            t.r = {}

    def op(self, eng, fn, reads=(), writes=(), signal=True):
        self._deps(eng, reads, writes)
        ins = fn()
        self.ninstr += 1
        if signal:
            eng.count += 1
            ins.then_inc(eng.sem, 1)
            tok = (eng.sem, eng.count, eng)
        else:
            assert eng.is_pe
            tok = (eng.sem, eng.count + 1, eng)
        self._record(tok, reads, writes)
        return ins

    def dma(self, q, out, in_, reads=(), writes=(), owner=None, **kw):
        self._deps(q, reads, writes)
        if owner is None:
            owner = writes[0] if writes else reads[0]
        if owner.dsem is None:
            if self.free_dsems:
                owner.dsem, owner.dcount = self.free_dsems.pop()
            else:
                owner.dsem = self.sem("d_" + owner.name)
            self.dma_owners.append(owner)
        ins = q.h.dma_start(out=out, in_=in_, **kw)
        owner.dcount += 16
        ins.then_inc(owner.dsem, 16)
        self.ninstr += 1
        tok = (owner.dsem, owner.dcount, None)
        self._record(tok, reads, writes)
        return ins

    def finish(self, eng, trackers):
        for t in trackers:
            self._wait(eng, t.w)
            for tok in t.r.values():
                self._wait(eng, tok)

    def barrier(self):
        for e in self.engs:
            for e2 in self.engs:
                if e2 is e or e2.count == 0:
                    continue
                self._wait(e, (e2.sem, e2.count, e2))
            for t in self.dma_owners:
                if t.dcount:
                    self._wait(e, (t.dsem, t.dcount, None))
        for t in self.dma_owners:
            self.free_dsems.append((t.dsem, t.dcount))
            t.dsem = None
            t.dcount = 0
        self.dma_owners = []
        for e in self.engs:
            if e.count > 0:
                e.sem = self.sem(e.name)
                e.count = 0


D = 2048
KC = D // 128
NSEQ = 2
NMETA = 16
SEQ = 2048
TSEQ = SEQ + NMETA
TS = 86
NCH = TSEQ // TS
NB = 3 * TS
NTB = TSEQ // NB
TTOT = NSEQ * TSEQ
EPS = 1e-6


def cast_weights(fw, dst, src, rows, tr):
    nc = fw.nc
    for r0 in range(0, rows, 256):
        r1 = min(rows, r0 + 256)
        fw.dma(fw.pool, dst[r0:r1, :], src[r0:r1, :], writes=[tr], owner=tr)


def phase_in_proj(fw, load_tok, g_vec, w_bf, t_w, F, outT, t_out, dt_out=None):
    nc = fw.nc
    with contextlib.ExitStack() as st:
        old = fw.stack
        fw.stack = st
        xnT = fw.sb("xnT", [128, KC, TSEQ], BF16)
        t_xnT = [T("xnT%d" % i) for i in range(NCH)]
        gbc = fw.sb("gbc", [128, D], F32); t_gbc = T("gbc")
        ident = fw.sb("identb", [128, 128], BF16); t_ident = T("identb")
        identf = fw.sb("identf", [128, 128], F32); t_identf = T("identf")
        xt = [fw.sb("xt%d" % i, [128, D], F32) for i in range(2)]
        t_xt = [T("xt%d" % i) for i in range(2)]
        xs = [fw.sb("xs%d" % i, [128, D], BF16) for i in range(2)]
        t_xs = [T("xs%d" % i) for i in range(2)]
        junk = fw.sb("junk", [128, D], BF16); t_junk = T("junk")
        st_ = [fw.sb("stat%d" % i, [128, 4], F32) for i in range(2)]
        t_st = [T("stat%d" % i) for i in range(2)]
        WS = 512
        wsl = [fw.sb("wsl%d" % i, [128, KC, WS], BF16) for i in range(2)]
        t_wsl = [T("wsl%d" % i) for i in range(2)]
        ob = [fw.sb("ob%d" % i, [128, TSEQ], BF16) for i in range(2)]
        t_ob = [T("ob%d" % i) for i in range(2)]
        obf = fw.sb("obf", [128, TSEQ], F32); t_obf = T("obf")
        pst = [fw.ps("pst%d" % i, [128, 8, TS], BF16) for i in range(2)]
        t_pst = [T("pst%d" % i) for i in range(2)]
        pmm = [fw.ps("pmm%d" % i, [128, 512], F32) for i in range(4)]
        t_pmm = [T("pmm%d" % i) for i in range(4)]

        fw.dma(fw.sp, gbc[:], g_vec.partition_broadcast(128), writes=[t_gbc])
        fw.dma(fw.sp, identf[:], fw.ident_dram, writes=[t_identf])
        fw.op(fw.dve, lambda: nc.vector.tensor_copy(out=ident[:], in_=identf[:]), reads=[t_identf], writes=[t_ident])

        nslab = (F + WS - 1) // WS
        ev = 0
        ftc = 0
        for seq in range(NSEQ):
            for tt in range(NCH):
                s = tt % 2
                load_tok(seq, tt, xt[s], t_xt[s])
                fw.op(fw.act, lambda: nc.scalar.activation(out=junk[:TS, :], in_=xt[s][:TS, :], func=AF.Square,
                                                          accum_out=st_[s][:TS, 0:1]),
                      reads=[t_xt[s]], writes=[t_junk, t_st[s]])
                fw.op(fw.act, lambda: nc.scalar.activation(out=st_[s][:TS, 1:2], in_=st_[s][:TS, 0:1], func=AF.Ln,
                                                          scale=1.0 / D, bias=EPS),
                      reads=[t_st[s]], writes=[t_st[s]])
                fw.op(fw.act, lambda: nc.scalar.activation(out=st_[s][:TS, 2:3], in_=st_[s][:TS, 1:2], func=AF.Exp,
                                                          scale=-0.5),
                      reads=[t_st[s]], writes=[t_st[s]])
                fw.op(fw.dve, lambda: nc.vector.scalar_tensor_tensor(out=xs[s][:TS, :], in0=xt[s][:TS, :],
                                                                     scalar=st_[s][:TS, 2:3], in1=gbc[:TS, :],
                                                                     op0=ALU.mult, op1=ALU.mult),
                      reads=[t_xt[s], t_st[s], t_gbc], writes=[t_xs[s]])
                for half in range(2):
                    p = pst[half]
                    for j in range(8):
                        kc = half * 8 + j
                        fw.op(fw.pe, lambda: nc.tensor.transpose(p[:, j, :], xs[s][:TS, kc * 128:(kc + 1) * 128],
                                                                 ident[:TS, :TS]),
                              reads=[t_xs[s], t_ident], writes=[t_pst[half]], signal=(j == 7))
                    dst = xnT[:, half * 8:(half + 1) * 8, tt * TS:(tt + 1) * TS]
                    if half == 0:
                        fw.op(fw.act, lambda: nc.scalar.copy(out=dst, in_=p[:]), reads=[t_pst[half]], writes=[t_xnT[tt]])
                    else:
                        fw.op(fw.dve, lambda: nc.vector.tensor_copy(out=dst, in_=p[:]), reads=[t_pst[half]],
                              writes=[t_xnT[tt]])
            for sl in range(nslab):
                f0 = sl * WS
                fw_ = min(WS, F - f0)
                ws = sl % 2
                src = w_bf[:, f0:f0 + fw_].rearrange("(kc p) f -> p kc f", p=128)
                h = KC // 2
                fw.dma(fw.sp, wsl[ws][:, 0:h, 0:fw_], src[:, 0:h, :], reads=[t_w], writes=[t_wsl[ws]])
                fw.dma(fw.sp, wsl[ws][:, h:KC, 0:fw_], src[:, h:KC, :], reads=[t_w], writes=[t_wsl[ws]])
                for fi in range(0, fw_, 128):
                    m = min(128, fw_ - fi)
                    o = ftc % 2
                    ftc += 1
                    for tb in range(NTB):
                        pb = ev % 4
                        pm = pmm[pb]
                        for kc in range(KC):
                            fw.op(fw.pe, lambda: nc.tensor.matmul(pm[:m, 0:NB], lhsT=wsl[ws][:, kc, fi:fi + m],
                                                                  rhs=xnT[:, kc, tb * NB:(tb + 1) * NB],
                                                                  start=(kc == 0), stop=(kc == KC - 1)),
                                  reads=[t_wsl[ws]] + t_xnT[tb * 3:(tb + 1) * 3], writes=[t_pmm[pb]],
                                  signal=(kc == KC - 1))
                        dst = ob[o][:m, tb * NB:(tb + 1) * NB]
                        if dt_out is not None and f0 + fi >= dt_out[1]:
                            fw.op(fw.dve, lambda: nc.vector.tensor_copy(out=obf[:m, tb * NB:(tb + 1) * NB],
                                                                        in_=pm[:m, 0:NB]),
                                  reads=[t_pmm[pb]], writes=[t_obf])
                            fw.op(fw.dve, lambda: nc.vector.tensor_copy(out=dst, in_=obf[:m, tb * NB:(tb + 1) * NB]),
                                  reads=[t_obf], writes=[t_ob[o]])
                        elif ev % 2 == 0:
                            fw.op(fw.act, lambda: nc.scalar.copy(out=dst, in_=pm[:m, 0:NB]), reads=[t_pmm[pb]],
                                  writes=[t_ob[o]])
                        else:
                            fw.op(fw.dve, lambda: nc.vector.tensor_copy(out=dst, in_=pm[:m, 0:NB]), reads=[t_pmm[pb]],
                                  writes=[t_ob[o]])
                        ev += 1
                    fw.dma(fw.act, outT[f0 + fi:f0 + fi + m, seq * TSEQ:(seq + 1) * TSEQ], ob[o][:m, :],
                           reads=[t_ob[o]], writes=[t_out], owner=t_ob[o])
                    if dt_out is not None and f0 + fi >= dt_out[1]:
                        fw.dma(fw.act, dt_out[0][0:m, seq * TSEQ:(seq + 1) * TSEQ], obf[:m, :],
                               reads=[t_obf], writes=[t_out], owner=t_obf)
        fw.barrier()
        fw.stack = old


def phase_out_proj(fw, yT, t_y, w_bf, t_w, load_res, store, final_g=None):
    nc = fw.nc
    FI = 4096
    FC = FI // 128
    with contextlib.ExitStack() as st:
        old = fw.stack
        fw.stack = st
        wres = fw.sb("wres", [128, FC, D], BF16); t_wres = T("wres")
        yb = [fw.sb("yb%d" % i, [128, FC, NB], BF16) for i in range(2)]
        t_yb = [T("yb%d" % i) for i in range(2)]
        xt = [fw.sb("rt%d" % i, [128, D], F32) for i in range(2)]
        t_xt = [T("rt%d" % i) for i in range(2)]
        po = [fw.ps("po%d" % i, [128, 512], F32) for i in range(8)]
        t_po = [T("po%d" % i) for i in range(8)]
        if final_g is not None:
            gbc = fw.sb("gbcf", [128, D], F32); t_gbc = T("gbcf")
            junk = fw.sb("junkf", [128, D], BF16); t_junk = T("junkf")
            st_ = [fw.sb("statf%d" % i, [128, 4], F32) for i in range(2)]
            t_st = [T("statf%d" % i) for i in range(2)]
            fw.dma(fw.sp, gbc[:], final_g.partition_broadcast(128), writes=[t_gbc])
        src = w_bf.rearrange("(fc p) d -> p fc d", p=128)
        for q4 in range(4):
            fw.dma(fw.sp, wres[:, q4 * 8:(q4 + 1) * 8, :], src[:, q4 * 8:(q4 + 1) * 8, :], reads=[t_w], writes=[t_wres])
        ysrc = yT.rearrange("(fc p) t -> p fc t", p=128)
        pc = 0
        for seq in range(NSEQ):
            for tb in range(NTB):
                ys = (seq * NTB + tb) % 2
                tok0 = seq * TSEQ + tb * NB
                fw.dma(fw.sp, yb[ys][:, 0:FC // 2, :], ysrc[:, 0:FC // 2, tok0:tok0 + NB], reads=[t_y], writes=[t_yb[ys]])
                fw.dma(fw.sp, yb[ys][:, FC // 2:FC, :], ysrc[:, FC // 2:FC, tok0:tok0 + NB], reads=[t_y], writes=[t_yb[ys]])
                for j in range(3):
                    tt = tb * 3 + j
                    s = tt % 2
                    load_res(seq, tt, xt[s], t_xt[s])
                    for db in range(4):
                        pb = pc % 8
                        pc += 1
                        for fc in range(FC):
                            fw.op(fw.pe, lambda: nc.tensor.matmul(po[pb][:TS, :], lhsT=yb[ys][:, fc, j * TS:(j + 1) * TS],
                                                                  rhs=wres[:, fc, db * 512:(db + 1) * 512],
                                                                  start=(fc == 0), stop=(fc == FC - 1)),
                                  reads=[t_yb[ys], t_wres], writes=[t_po[pb]], signal=(fc == FC - 1))
                        fw.op(fw.dve, lambda: nc.vector.tensor_tensor(out=xt[s][:TS, db * 512:(db + 1) * 512],
                                                                      in0=po[pb][:TS, :],
                                                                      in1=xt[s][:TS, db * 512:(db + 1) * 512], op=ALU.add),
                              reads=[t_po[pb], t_xt[s]], writes=[t_xt[s]])
                    if final_g is not None:
                        fw.op(fw.act, lambda: nc.scalar.activation(out=junk[:TS, :], in_=xt[s][:TS, :], func=AF.Square,
                                                                  accum_out=st_[s][:TS, 0:1]),
                              reads=[t_xt[s]], writes=[t_junk, t_st[s]])
                        fw.op(fw.act, lambda: nc.scalar.activation(out=st_[s][:TS, 1:2], in_=st_[s][:TS, 0:1], func=AF.Ln,
                                                                  scale=1.0 / D, bias=EPS),
                              reads=[t_st[s]], writes=[t_st[s]])
                        fw.op(fw.act, lambda: nc.scalar.activation(out=st_[s][:TS, 2:3], in_=st_[s][:TS, 1:2], func=AF.Exp,
                                                                  scale=-0.5),
                              reads=[t_st[s]], writes=[t_st[s]])
                        fw.op(fw.dve, lambda: nc.vector.scalar_tensor_tensor(out=xt[s][:TS, :], in0=xt[s][:TS, :],
                                                                             scalar=st_[s][:TS, 2:3], in1=gbc[:TS, :],
                                                                             op0=ALU.mult, op1=ALU.mult),
                              reads=[t_xt[s], t_st[s], t_gbc], writes=[t_xt[s]])
                    store(seq, tt, xt[s], t_xt[s])
        fw.barrier()
        fw.stack = old


def host_consts():
    c = {}
    c["ident"] = np.eye(128, dtype=np.float32)
    s = np.arange(TS)
    c["tri"] = (s[:, None] <= s[None, :]).astype(np.float32)
    p = np.arange(128)
    c["mask8"] = (p[:, None] // 16 == np.arange(8)[None, :]).astype(np.float32)
    mc = np.zeros((128, 4, 128), np.float32)
    for kl in range(4):
        mc[:, kl, :] = ((p[None, :] // 16) == (2 * kl + p[:, None] // 64)).astype(np.float32)
    c["maskC"] = mc
    c["bd4"] = (p[:, None] // 4 == p[None, :] // 4).astype(np.float32)
    return c


CONST_SHAPES = {"ident": [128, 128], "tri": [TS, TS], "mask8": [128, 8], "maskC": [128, 4, 128], "bd4": [128, 128]}


def cmul_ops(fw, nc, out_r, out_i, ar, ai, br, bi, tmp, tr_out, tr_in, tr_tmp, neg_imag_b=False):
    t1, t2 = tmp
    V = nc.vector
    fw.op(fw.dve, lambda: V.tensor_tensor(out=t1, in0=ar, in1=br, op=ALU.mult), reads=tr_in, writes=tr_tmp)
    fw.op(fw.dve, lambda: V.tensor_tensor(out=t2, in0=ai, in1=bi, op=ALU.mult), reads=tr_in, writes=tr_tmp)
    fw.op(fw.dve, lambda: V.tensor_tensor(out=out_r, in0=t1, in1=t2, op=ALU.subtract), reads=tr_tmp, writes=tr_out)
    fw.op(fw.dve, lambda: V.tensor_tensor(out=t1, in0=ar, in1=bi, op=ALU.mult), reads=tr_in + tr_out, writes=tr_tmp)
    fw.op(fw.dve, lambda: V.tensor_tensor(out=t2, in0=ai, in1=br, op=ALU.mult), reads=tr_in, writes=tr_tmp)
    fw.op(fw.dve, lambda: V.tensor_tensor(out=out_i, in0=t1, in1=t2, op=ALU.add), reads=tr_tmp, writes=tr_out)


def phase_s5(fw, p0T, t_p0, prm, glu_bf, t_glu, y0T, t_y0):
    nc = fw.nc
    V, A, P = nc.vector, nc.scalar, nc.tensor
    L = TS
    with contextlib.ExitStack() as st:
        old = fw.stack
        fw.stack = st
        En_r = fw.sb("En_r", [128, 32, 128], F32)
        En_i = fw.sb("En_i", [128, 32, 128], F32)
        Ep_r = fw.sb("Ep_r", [128, 32, L], F32)
        Ep_i = fw.sb("Ep_i", [128, 32, L], F32)
        Bblk = fw.sb("Bblk", [128, 8, 1024], BF16)
        Cblk = fw.sb("Cblk", [128, 32, 2, 128], BF16)
        tri = fw.sb("trib", [128, L], BF16)
        glu = fw.sb("gluw", [128, 8, 1024], BF16)
        dvec = fw.sb("s5d", [128, 8], F32)
        hb = fw.sb("s5hb", [128, 8], F32)
        car_r = fw.sb("car_r", [128, 32], F32)
        car_i = fw.sb("car_i", [128, 32], F32)
        t_tab = T("s5tab")
        t_car = [T("car%d" % i) for i in range(8)]
        ps = [fw.ps("s5ps%d" % i, [128, 512], F32) for i in range(8)]
        t_ps = [T("s5ps%d" % i) for i in range(8)]

        with contextlib.ExitStack() as st2:
            fw.stack = st2
            t_s = T("s5setup")
            identf = fw.sb("identf", [128, 128], F32)
            trif = fw.sb("trif", [128, L], F32)
            mask8 = fw.sb("mask8", [128, 8], F32)
            maskC = fw.sb("maskC", [128, 4, 128], F32)
            lr = fw.sb("lr", [128, 32], F32); li = fw.sb("li", [128, 32], F32); ldt = fw.sb("ldt", [128, 32], F32)
            w = [fw.sb("s5w%d" % i, [128, 32], F32) for i in range(12)]
            Es_r = fw.sb("Es_r", [128, 32, L], F32); Es_i = fw.sb("Es_i", [128, 32, L], F32)
            tA = fw.sb("tA", [128, 32, 64], F32); tB = fw.sb("tB", [128, 32, 64], F32)
            Bp = [fw.sb("Bp%d" % i, [64, 64, 16], F32) for i in range(2)]
            Cn = [fw.sb("Cn%d" % i, [128, 8, 64], F32) for i in range(2)]
            glub = fw.sb("glub", [128, 8], F32)
            Cn2 = fw.sb("Cn2", [128, 2, 64], F32)
            S = [t_s]
            fw.dma(fw.sp, identf[:], fw.consts["ident"], writes=S)
            fw.dma(fw.sp, trif[:L, :], fw.consts["tri"], writes=S)
            fw.dma(fw.sp, mask8[:], fw.consts["mask8"], writes=S)
            fw.dma(fw.sp, maskC[:], fw.consts["maskC"], writes=S)
            fw.dma(fw.sp, lr[:], prm["s5_lambda_re"].rearrange("g p -> (g p)").rearrange("(k q) -> q k", q=128), writes=S,
                   allow_slow_non_contiguous=True)
            fw.dma(fw.sp, li[:], prm["s5_lambda_im"].rearrange("g p -> (g p)").rearrange("(k q) -> q k", q=128), writes=S,
                   allow_slow_non_contiguous=True)
            ldv = prm["s5_log_dt"].rearrange("(k gl) -> gl k", gl=2)
            for gl in range(2):
                fw.dma(fw.sp, ldt[gl * 64:(gl + 1) * 64, :], ldv[gl, :].partition_broadcast(64), writes=S,
                       allow_slow_non_contiguous=True)
            fw.dma(fw.sp, Bp[0][:], prm["s5_b_re"].rearrange("g p h -> p g h"), writes=S)
            fw.dma(fw.sp, Bp[1][:], prm["s5_b_im"].rearrange("g p h -> p g h"), writes=S)
            fw.dma(fw.sp, Cn[0][:], prm["s5_c_re"].rearrange("(ct gl) h p -> (gl h) ct p", gl=8), writes=S)
            fw.dma(fw.sp, Cn[1][:], prm["s5_c_im"].rearrange("(ct gl) h p -> (gl h) ct p", gl=8), writes=S)
            fw.dma(fw.sp, dvec[:], prm["s5_d"].rearrange("(ct c) -> c ct", c=128), writes=S, allow_slow_non_contiguous=True)
            fw.dma(fw.sp, glub[:], prm["s5_glu_b"].rearrange("(ct c) -> c ct", c=128), writes=S, allow_slow_non_contiguous=True)
            gsrc = glu_bf.rearrange("(ci p) co -> p ci co", p=128)
            fw.dma(fw.sp, glu[:], gsrc, reads=[t_glu], writes=[t_tab])
            o = lambda eng, fn: fw.op(eng, fn, reads=S, writes=S)
            o(fw.dve, lambda: V.tensor_copy(out=tri[:L, :], in_=trif[:L, :]))
            o(fw.dve, lambda: V.tensor_scalar(out=hb[:], in0=glub[:], scalar1=0.5, scalar2=None, op0=ALU.mult))
            dt_, lrd, lid, magp, magn, cs, sn, bpr, bpi, bnr, bni, tq = w
            o(fw.act, lambda: A.activation(out=dt_[:], in_=ldt[:], func=AF.Exp))
            o(fw.dve, lambda: V.scalar_tensor_tensor(out=lrd[:], in0=lr[:], scalar=1.0 / 16, in1=dt_[:], op0=ALU.mult, op1=ALU.mult))
            o(fw.dve, lambda: V.scalar_tensor_tensor(out=lid[:], in0=li[:], scalar=1.0 / 16, in1=dt_[:], op0=ALU.mult, op1=ALU.mult))
            o(fw.act, lambda: A.activation(out=magp[:], in_=lrd[:], func=AF.Exp))
            o(fw.act, lambda: A.activation(out=magn[:], in_=lrd[:], func=AF.Exp, scale=-1.0))
            o(fw.act, lambda: A.activation(out=sn[:], in_=lid[:], func=AF.Sin))
            o(fw.dve, lambda: V.tensor_scalar(out=tq[:], in0=lid[:], scalar1=float(np.pi / 2), scalar2=None, op0=ALU.add))
            o(fw.act, lambda: A.activation(out=cs[:], in_=tq[:], func=AF.Sin))
            o(fw.dve, lambda: V.tensor_tensor(out=bpr[:], in0=magp[:], in1=cs[:], op=ALU.mult))
            o(fw.dve, lambda: V.tensor_tensor(out=bpi[:], in0=magp[:], in1=sn[:], op=ALU.mult))
            o(fw.dve, lambda: V.tensor_tensor(out=bnr[:], in0=magn[:], in1=cs[:], op=ALU.mult))
            o(fw.dve, lambda: V.scalar_tensor_tensor(out=bni[:], in0=magn[:], scalar=-1.0, in1=sn[:], op0=ALU.mult, op1=ALU.mult))

            def csq(r, i, t1, t2):
                o(fw.dve, lambda: V.tensor_tensor(out=t1, in0=r, in1=r, op=ALU.mult))
                o(fw.dve, lambda: V.tensor_tensor(out=t2, in0=i, in1=i, op=ALU.mult))
                o(fw.dve, lambda: V.scalar_tensor_tensor(out=i, in0=r, scalar=2.0, in1=i, op0=ALU.mult, op1=ALU.mult))
                o(fw.dve, lambda: V.tensor_tensor(out=r, in0=t1, in1=t2, op=ALU.subtract))

            for _ in range(4):
                csq(bpr[:], bpi[:], magp[:], magn[:])
            for _ in range(4):
                csq(bnr[:], bni[:], magp[:], magn[:])
            am1, den, qr, qi = cs, sn, dt_, lrd
            o(fw.dve, lambda: V.tensor_scalar(out=am1[:], in0=bpr[:], scalar1=-1.0, scalar2=None, op0=ALU.add))
            o(fw.dve, lambda: V.tensor_tensor(out=magp[:], in0=lr[:], in1=lr[:], op=ALU.mult))
            o(fw.dve, lambda: V.tensor_tensor(out=magn[:], in0=li[:], in1=li[:], op=ALU.mult))
            o(fw.dve, lambda: V.tensor_tensor(out=den[:], in0=magp[:], in1=magn[:], op=ALU.add))
            o(fw.dve, lambda: V.reciprocal(out=den[:], in_=den[:]))
            o(fw.dve, lambda: V.tensor_tensor(out=magp[:], in0=am1[:], in1=lr[:], op=ALU.mult))
            o(fw.dve, lambda: V.tensor_tensor(out=magn[:], in0=bpi[:], in1=li[:], op=ALU.mult))
            o(fw.dve, lambda: V.tensor_tensor(out=qr[:], in0=magp[:], in1=magn[:], op=ALU.add))
            o(fw.dve, lambda: V.tensor_tensor(out=qr[:], in0=qr[:], in1=den[:], op=ALU.mult))
            o(fw.dve, lambda: V.tensor_tensor(out=magp[:], in0=bpi[:], in1=lr[:], op=ALU.mult))
            o(fw.dve, lambda: V.tensor_tensor(out=magn[:], in0=am1[:], in1=li[:], op=ALU.mult))
            o(fw.dve, lambda: V.tensor_tensor(out=qi[:], in0=magp[:], in1=magn[:], op=ALU.subtract))
            o(fw.dve, lambda: V.tensor_tensor(out=qi[:], in0=qi[:], in1=den[:], op=ALU.mult))

            def build_pow(Er, Ei, br, bi):
                o(fw.dve, lambda: V.tensor_copy(out=Er[:, :, 0], in_=br))
                o(fw.dve, lambda: V.tensor_copy(out=Ei[:, :, 0], in_=bi))
                n = 1
                while n < L:
                    m = min(n, L - n)
                    cb_r = br.unsqueeze(2).to_broadcast([128, 32, m])
                    cb_i = bi.unsqueeze(2).to_broadcast([128, 32, m])
                    cmul_ops(fw, nc, Er[:, :, n:n + m], Ei[:, :, n:n + m], Er[:, :, 0:m], Ei[:, :, 0:m], cb_r, cb_i,
                             (tA[:, :, 0:m], tB[:, :, 0:m]), S, S, S)
                    n += m
                    if n < L:
                        csq(br, bi, magp[:], magn[:])

            build_pow(Ep_r, Ep_i, bpr[:], bpi[:])
            build_pow(Es_r, Es_i, bnr[:], bni[:])
            qb_r = qr[:].unsqueeze(2).to_broadcast([128, 32, L])
            qb_i = qi[:].unsqueeze(2).to_broadcast([128, 32, L])
            for h0 in (0, 43):
                sl = slice(h0, h0 + 43)
                t1, t2 = tA[:, :, 0:43], tB[:, :, 0:43]
                qr_b = qr[:].unsqueeze(2).to_broadcast([128, 32, 43])
                qi_b = qi[:].unsqueeze(2).to_broadcast([128, 32, 43])
                o(fw.dve, lambda: V.tensor_tensor(out=t1, in0=Es_r[:, :, sl], in1=qi_b, op=ALU.mult))
                o(fw.dve, lambda: V.tensor_tensor(out=t2, in0=Es_i[:, :, sl], in1=qi_b, op=ALU.mult))
                o(fw.dve, lambda: V.tensor_tensor(out=Es_r[:, :, sl], in0=Es_r[:, :, sl], in1=qr_b, op=ALU.mult))
                o(fw.dve, lambda: V.tensor_tensor(out=Es_i[:, :, sl], in0=Es_i[:, :, sl], in1=qr_b, op=ALU.mult))
                o(fw.dve, lambda: V.tensor_tensor(out=Es_r[:, :, sl], in0=Es_r[:, :, sl], in1=t2, op=ALU.subtract))
                o(fw.dve, lambda: V.tensor_tensor(out=Es_i[:, :, sl], in0=Es_i[:, :, sl], in1=t1, op=ALU.add))
            for k in range(32):
                for ri, (src, dst) in enumerate(((Es_r, En_r), (Es_i, En_i))):
                    pb = ps[(2 * k + ri) % 8]
                    o(fw.pe, lambda: P.transpose(pb[:L, 0:128], src[:, k, :], identf[:, :]))
                    o(fw.act, lambda: A.copy(out=dst[:L, k, :], in_=pb[:L, 0:128]))
            for ct in range(8):
                pb = ps[ct % 8]
                for ri in range(2):
                    o(fw.pe, lambda: P.transpose(pb[:, ri * 64:(ri + 1) * 64],
                                                 Bp[ri][:, ct * 8:(ct + 1) * 8, :].rearrange("p g h -> p (g h)"),
                                                 identf[:64, :64]))
                src = pb[:, 0:128].rearrange("c (ri p) -> c ri p", ri=2).unsqueeze(2).to_broadcast([128, 2, 8, 64])
                msk = mask8[:].unsqueeze(1).unsqueeze(3).to_broadcast([128, 2, 8, 64])
                o(fw.dve, lambda: V.tensor_tensor(out=Bblk[:, ct, :].rearrange("c (ri g p) -> c ri g p", ri=2, g=8),
                                                  in0=src, in1=msk, op=ALU.mult))
            for ct in range(8):
                for ri in range(2):
                    pb = ps[(2 * ct + ri) % 8]
                    o(fw.dve, lambda: V.tensor_copy(out=Cn2[:], in_=Cn[ri][:, ct, :].unsqueeze(1).to_broadcast([128, 2, 64])))
                    o(fw.pe, lambda: P.transpose(pb[:, 0:128], Cn2[:].rearrange("c a p -> c (a p)"), identf[:, :]))
                    for kl in range(4):
                        k = ct * 4 + kl
                        o(fw.dve, lambda: V.scalar_tensor_tensor(out=Cblk[:, k, ri, :], in0=pb[:, 0:128],
                                                                 scalar=(1.0 if ri == 0 else -1.0), in1=maskC[:, kl, :],
                                                                 op0=ALU.mult, op1=ALU.mult))
            o(fw.dve, lambda: V.memset(car_r[:], 0.0))
            fw.op(fw.dve, lambda: V.memset(car_i[:], 0.0), reads=S, writes=S + [t_tab] + t_car)
            fw.barrier()
        fw.stack = st

        ub = [fw.sb("s5ub%d" % i, [128, 16, NB], BF16) for i in range(2)]
        t_ub = [T("s5ub%d" % i) for i in range(2)]
        tmp = [fw.sb("s5t%d" % i, [128, 512], F32) for i in range(4)]
        t_tmp = T("s5tmp")
        zin = [fw.sb("zin%d" % i, [128, 1024], BF16) for i in range(2)]
        t_zin = [T("zin%d" % i) for i in range(2)]
        zc = fw.sb("zc", [128, 2, 4, L], F32); t_zc = T("zc")
        rt = [fw.sb("s5rt%d" % i, [128, 4, L], F32) for i in range(4)]
        t_rt = T("s5rt")
        xb = [fw.sb("s5xb%d" % i, [128, 2, 4, L], BF16) for i in range(2)]
        t_xb = [T("s5xb%d" % i) for i in range(2)]
        yv = fw.sb("s5yv", [128, L], F32); t_yv = T("s5yv")
        e1 = fw.sb("s5e1", [128, L], F32); e2 = fw.sb("s5e2", [128, L], F32); t_e = T("s5e")
        g2 = [fw.sb("s5g2%d" % i, [128, 8, L], BF16) for i in range(2)]
        t_g2 = [T("s5g2%d" % i) for i in range(2)]
        ob = [fw.sb("s5ob%d" % i, [128, 8, NB], BF16) for i in range(2)]
        t_ob = [T("s5ob%d" % i) for i in range(2)]
        usrc = p0T[0:2048, :].rearrange("(j c) t -> c j t", c=128)
        ydst = y0T[0:1024, :].rearrange("(j c) t -> c j t", c=128)
        ci = 0
        for seq in range(NSEQ):
            if seq > 0:
                fw.op(fw.dve, lambda: V.memset(car_r[:], 0.0), writes=t_car)
                fw.op(fw.dve, lambda: V.memset(car_i[:], 0.0), writes=t_car)
            for tb in range(NTB):
                bs = (seq * NTB + tb) % 2
                tok0 = seq * TSEQ + tb * NB
                fw.dma(fw.sp, ub[bs][:], usrc[:, :, tok0:tok0 + NB], reads=[t_p0], writes=[t_ub[bs]])
                for j in range(3):
                    cs_ = slice(j * L, (j + 1) * L)
                    gs = ci % 2
                    ci += 1
                    for ct in range(8):
                        zs = ct % 2
                        pA, pB, pZr, pZi, pY = ps[0], ps[1], ps[2 + 2 * zs], ps[3 + 2 * zs], ps[6 + zs]
                        tA_, tB_, tZr, tZi, tY = t_ps[0], t_ps[1], t_ps[2 + 2 * zs], t_ps[3 + 2 * zs], t_ps[6 + zs]
                        u_ct = ub[bs][:, ct, cs_]
                        fw.op(fw.pe, lambda: P.matmul(pA[:L, :], lhsT=u_ct, rhs=Bblk[:, ct, 0:512], start=True, stop=True),
                              reads=[t_ub[bs], t_tab], writes=[tA_])
                        fw.op(fw.pe, lambda: P.matmul(pB[:L, :], lhsT=u_ct, rhs=Bblk[:, ct, 512:1024], start=True, stop=True),
                              reads=[t_ub[bs], t_tab], writes=[tB_])
                        enr = En_r[:L, ct * 4:(ct + 1) * 4, :].rearrange("s k q -> s (k q)")
                        eni = En_i[:L, ct * 4:(ct + 1) * 4, :].rearrange("s k q -> s (k q)")
                        cmul_ops(fw, nc, zin[zs][:L, 0:512], zin[zs][:L, 512:1024], pA[:L, :], pB[:L, :], enr, eni,
                                 (tmp[0][:L, :], tmp[1][:L, :]), [t_zin[zs]], [tA_, tB_, t_tab], [t_tmp])
                        for cb in range(8):
                            pz = pZr if cb < 4 else pZi
                            tz = tZr if cb < 4 else tZi
                            fw.op(fw.pe, lambda: P.matmul(pz[:, (cb % 4) * L:(cb % 4 + 1) * L],
                                                          lhsT=zin[zs][:L, cb * 128:(cb + 1) * 128], rhs=tri[:L, :],
                                                          start=True, stop=True),
                                  reads=[t_zin[zs], t_tab], writes=[tz], signal=(cb % 4 == 3))
                        k4 = slice(ct * 4, ct * 4 + 4)
                        fw.op(fw.dve, lambda: V.tensor_tensor(out=zc[:, 0, :, :],
                                                              in0=pZr[:, 0:4 * L].rearrange("q (k t) -> q k t", k=4),
                                                              in1=car_r[:, k4].unsqueeze(2).to_broadcast([128, 4, L]),
                                                              op=ALU.add),
                              reads=[tZr, t_car[ct]], writes=[t_zc])
                        fw.op(fw.dve, lambda: V.tensor_tensor(out=zc[:, 1, :, :],
                                                              in0=pZi[:, 0:4 * L].rearrange("q (k t) -> q k t", k=4),
                                                              in1=car_i[:, k4].unsqueeze(2).to_broadcast([128, 4, L]),
                                                              op=ALU.add),
                              reads=[tZi, t_car[ct]], writes=[t_zc])
                        xs_ = ct % 2
                        epr, epi = Ep_r[:, k4, :], Ep_i[:, k4, :]
                        RT = [t_rt]
                        fw.op(fw.dve, lambda: V.tensor_tensor(out=rt[0][:], in0=zc[:, 0], in1=epr, op=ALU.mult), reads=[t_zc, t_tab], writes=RT)
                        fw.op(fw.dve, lambda: V.tensor_tensor(out=rt[1][:], in0=zc[:, 1], in1=epi, op=ALU.mult), reads=[t_zc, t_tab], writes=RT)
                        fw.op(fw.dve, lambda: V.tensor_tensor(out=rt[2][:], in0=zc[:, 0], in1=epi, op=ALU.mult), reads=[t_zc, t_tab], writes=RT)
                        fw.op(fw.dve, lambda: V.tensor_tensor(out=rt[3][:], in0=zc[:, 1], in1=epr, op=ALU.mult), reads=[t_zc, t_tab], writes=RT)
                        fw.op(fw.dve, lambda: V.tensor_tensor(out=xb[xs_][:, 0], in0=rt[0][:], in1=rt[1][:], op=ALU.subtract), reads=RT, writes=[t_xb[xs_]])
                        fw.op(fw.dve, lambda: V.tensor_tensor(out=xb[xs_][:, 1], in0=rt[2][:], in1=rt[3][:], op=ALU.add), reads=RT, writes=[t_xb[xs_]])
                        fw.op(fw.dve, lambda: V.tensor_tensor(out=car_r[:, k4], in0=rt[0][:, :, L - 1], in1=rt[1][:, :, L - 1], op=ALU.subtract), reads=RT, writes=[t_car[ct]])
                        fw.op(fw.dve, lambda: V.tensor_tensor(out=car_i[:, k4], in0=rt[2][:, :, L - 1], in1=rt[3][:, :, L - 1], op=ALU.add), reads=RT, writes=[t_car[ct]])
                        for kl in range(4):
                            for ri in range(2):
                                fw.op(fw.pe, lambda: P.matmul(pY[:, 0:L], lhsT=Cblk[:, ct * 4 + kl, ri, :], rhs=xb[xs_][:, ri, kl, :],
                                                              start=(kl == 0 and ri == 0), stop=(kl == 3 and ri == 1)),
                                      reads=[t_xb[xs_], t_tab], writes=[tY], signal=(kl == 3 and ri == 1))
                        fw.op(fw.dve, lambda: V.scalar_tensor_tensor(out=yv[:], in0=u_ct, scalar=dvec[:, ct:ct + 1], in1=pY[:, 0:L],
                                                                     op0=ALU.mult, op1=ALU.add),
                              reads=[t_ub[bs], tY, t_tab], writes=[t_yv])
                        fw.op(fw.act, lambda: A.activation(out=e1[:], in_=yv[:], func=AF.Square), reads=[t_yv], writes=[t_e])
                        fw.op(fw.dve, lambda: V.tensor_scalar(out=e1[:], in0=e1[:], scalar1=0.044715, scalar2=1.0, op0=ALU.mult, op1=ALU.add),
                              reads=[t_e], writes=[t_e])
                        fw.op(fw.dve, lambda: V.tensor_tensor(out=e1[:], in0=e1[:], in1=yv[:], op=ALU.mult), reads=[t_e, t_yv], writes=[t_e])
                        fw.op(fw.act, lambda: A.activation(out=e2[:], in_=e1[:], func=AF.Tanh, scale=0.7978845608), reads=[t_e], writes=[t_e])
                        fw.op(fw.dve, lambda: V.scalar_tensor_tensor(out=g2[gs][:, ct, :], in0=e2[:], scalar=1.0, in1=yv[:],
                                                                     op0=ALU.add, op1=ALU.mult),
                              reads=[t_e, t_yv], writes=[t_g2[gs]])
                    for co in range(8):
                        pg = ps[6 + co % 2]
                        tg = t_ps[6 + co % 2]
                        for ci_ in range(8):
                            fw.op(fw.pe, lambda: P.matmul(pg[:, 0:L], lhsT=glu[:, ci_, co * 128:(co + 1) * 128], rhs=g2[gs][:, ci_, :],
                                                          start=(ci_ == 0), stop=(ci_ == 7)),
                                  reads=[t_g2[gs], t_tab], writes=[tg], signal=(ci_ == 7))
                        fw.op(fw.act, lambda: A.activation(out=e1[:], in_=pg[:, 0:L], func=AF.Tanh, scale=0.25, bias=hb[:, co:co + 1]),
                              reads=[tg, t_tab], writes=[t_e])
                        fw.op(fw.act, lambda: A.activation(out=e2[:], in_=ub[bs][:, 8 + co, cs_], func=AF.Tanh, scale=0.5),
                              reads=[t_ub[bs]], writes=[t_e])
                        fw.op(fw.dve, lambda: V.scalar_tensor_tensor(out=e1[:], in0=e1[:], scalar=1.0, in1=g2[gs][:, co, :],
                                                                     op0=ALU.add, op1=ALU.mult), reads=[t_e, t_g2[gs]], writes=[t_e])
                        fw.op(fw.dve, lambda: V.scalar_tensor_tensor(out=e2[:], in0=e2[:], scalar=1.0, in1=ub[bs][:, 8 + co, cs_],
                                                                     op0=ALU.add, op1=ALU.mult), reads=[t_e, t_ub[bs]], writes=[t_e])
                        fw.op(fw.dve, lambda: V.scalar_tensor_tensor(out=ob[bs][:, co, cs_], in0=e1[:], scalar=0.125, in1=e2[:],
                                                                     op0=ALU.mult, op1=ALU.mult), reads=[t_e], writes=[t_ob[bs]])
                fw.dma(fw.act, ydst[:, :, tok0:tok0 + NB], ob[bs][:], reads=[t_ob[bs]], writes=[t_y0], owner=t_ob[bs])
        fw.barrier()
        fw.stack = old


NH = 8
DH = 384
DBG = {}


class StopEmit(Exception):
    pass


def stage(n):
    if DBG.get("stage", 999) <= n:
        raise StopEmit()
HEAD_EPS = 1e-5


def phase_mlstm(fw, p0T, t_p0, prm, y0T, t_y0):
    nc = fw.nc
    V, A, P = nc.vector, nc.scalar, nc.tensor
    L = TS
    NT = 24
    with contextlib.ExitStack() as st:
        old = fw.stack
        fw.stack = st
        Wq = fw.sb("Wq", [128, NT, 128], BF16)
        Wk = fw.sb("Wk", [128, NT, 128], BF16)
        Wv = fw.sb("Wv", [128, NT, 128], BF16)
        Gqk = fw.sb("Gqk", [128, NT, 16], BF16)
        Gv = fw.sb("Gv", [128, NT, 16], BF16)
        cw = fw.sb("cw", [128, 4, NT], F32)
        cb = fw.sb("cbias", [128, NT], F32)
        nw = fw.sb("nw", [128, NT], F32)
        sk = fw.sb("skp", [128, NT], F32)
        bg = fw.sb("bg", [1, 16], BF16)
        ones_b = fw.sb("ones_b", [128, 128], BF16)
        ones_f = fw.sb("ones_f", [128, 128], F32)
        identb = fw.sb("identb", [128, 128], BF16)
        identf = fw.sb("identf", [128, 128], F32)
        trif = fw.sb("trif", [128, L], F32)
        Cst = fw.sb("Cst", [128, NH, 3, DH + 1], F32)
        Cbf = fw.sb("Cbf", [128, NH, 3, DH + 1], BF16)
        m_b = fw.sb("m_b", [128, NH], F32)
        m_c = fw.sb("m_c", [NH, 1], F32)
        t_tab = T("mltab")
        t_C = [T("C%d" % h) for h in range(NH)]
        t_Cb = [T("Cb%d" % h) for h in range(NH)]
        t_m = T("m")
        ps = [fw.ps("mlps%d" % i, [128, 512], F32) for i in range(7)]
        t_ps = [T("mlps%d" % i) for i in range(7)]
        pTb = fw.ps("mlpT", [128, 1024], BF16)
        t_pTb = [T("mlpT%d" % i) for i in range(3)]
        t_psk = T("mlpsk")

        with contextlib.ExitStack() as st2:
            fw.stack = st2
            S = [T("mlsetup")]
            o = lambda eng, fn: fw.op(eng, fn, reads=S, writes=S)
            bd4 = fw.sb("bd4", [128, 128], F32)
            wexp = [fw.sb("wexp%d" % i, [128, NT, 4], F32) for i in range(3)]
            Wraw = fw.sb("Wraw", [128, 128], BF16)
            wgf = fw.sb("wgf", [128, 3, NT, 16], F32)
            wgb = fw.sb("wgb", [128, 3, NT, 16], BF16)
            WT = [fw.sb("WT%d" % i, [128, 128], BF16) for i in range(3)]
            bgf = fw.sb("bgf", [1, 16], F32)
            skf = fw.sb("skf", [128, NT], F32)
            fw.dma(fw.sp, bd4[:], fw.consts["bd4"], writes=S)
            fw.dma(fw.sp, identf[:], fw.consts["ident"], writes=S)
            fw.dma(fw.sp, trif[:L, :], fw.consts["tri"], writes=S)
            for i, nm in enumerate(("ml_wq", "ml_wk", "ml_wv")):
                fw.dma(fw.sp, wexp[i][:], prm[nm].rearrange("n i o -> (n i) o").rearrange("(t p) o -> p t o", p=128), writes=S)
            fw.dma(fw.sp, wgf[:], prm["ml_w_gate"].rearrange("(a t p) g -> p a t g", a=3, p=128), writes=S)
            for k in range(4):
                fw.dma(fw.sp, cw[:, k, :], prm["ml_conv_w"][k, :].rearrange("(t p) -> p t", p=128), writes=S, allow_slow_non_contiguous=True)
            fw.dma(fw.sp, cb[:], prm["ml_conv_b"].rearrange("(t p) -> p t", p=128), writes=S, allow_slow_non_contiguous=True)
            fw.dma(fw.sp, nw[:], prm["ml_norm"].rearrange("(t p) -> p t", p=128), writes=S, allow_slow_non_contiguous=True)
            fw.dma(fw.sp, skf[:], prm["ml_skip"].rearrange("(t p) -> p t", p=128), writes=S, allow_slow_non_contiguous=True)
            fw.dma(fw.sp, bgf[:], prm["ml_b_gate"].rearrange("(a g) -> a g", a=1), writes=S)
            o(fw.dve, lambda: V.tensor_copy(out=bg[:], in_=bgf[:]))
            o(fw.dve, lambda: V.tensor_scalar(out=sk[:], in0=skf[:], scalar1=0.5, scalar2=None, op0=ALU.mult))
            o(fw.dve, lambda: V.tensor_copy(out=wgb[:], in_=wgf[:]))
            o(fw.dve, lambda: V.tensor_copy(out=identb[:], in_=identf[:]))
            o(fw.dve, lambda: V.memset(ones_b[:], 1.0))
            o(fw.dve, lambda: V.memset(ones_f[:], 1.0))
            bdv = bd4[:].rearrange("p (n o) -> p n o", o=4)
            scl = (0.5 * DH ** -0.5, 0.5, 1.0)
            for t in range(NT):
                for i, Wd in enumerate((Wq, Wk, Wv)):
                    o(fw.dve, lambda: V.scalar_tensor_tensor(out=Wd[:, t, :].rearrange("p (n o) -> p n o", o=4),
                                                             in0=wexp[i][:, t, :].unsqueeze(1).to_broadcast([128, 32, 4]),
                                                             scalar=scl[i], in1=bdv, op0=ALU.mult, op1=ALU.mult))
                    o(fw.dve, lambda: V.tensor_tensor(out=Wraw[:].rearrange("p (n o) -> p n o", o=4),
                                                      in0=wexp[i][:, t, :].unsqueeze(1).to_broadcast([128, 32, 4]),
                                                      in1=bdv, op=ALU.mult))
                    o(fw.pe, lambda: P.transpose(pTb[:, 0:128], Wraw[:], identb[:]))
                    o(fw.act, lambda: A.copy(out=WT[i][:], in_=pTb[:, 0:128]))
                pq = ps[t % 4]
                o(fw.pe, lambda: P.matmul(pq[:, 0:16], lhsT=WT[0][:], rhs=wgb[:, 0, t, :], start=True, stop=False))
                o(fw.pe, lambda: P.matmul(pq[:, 0:16], lhsT=WT[1][:], rhs=wgb[:, 1, t, :], start=False, stop=True))
                o(fw.pe, lambda: P.matmul(pq[:, 16:32], lhsT=WT[2][:], rhs=wgb[:, 2, t, :], start=True, stop=True))
                o(fw.act, lambda: A.mul(out=Gqk[:, t, :], in_=pq[:, 0:16], mul=0.5))
                o(fw.act, lambda: A.copy(out=Gv[:, t, :], in_=pq[:, 16:32]))
            fw.op(fw.dve, lambda: V.memset(m_b[:], 0.0), reads=S, writes=S + [t_tab, t_m])
            fw.barrier()
        fw.stack = st

        xblk = [fw.sb("xblk%d" % i, [128, NT, NB + 3], BF16) for i in range(2)]
        zblk = [fw.sb("zblk%d" % i, [128, NT, NB], BF16) for i in range(2)]
        t_xblk = [T("xblk%d" % i) for i in range(2)]
        t_zblk = [T("zblk%d" % i) for i in range(2)]
        acc = fw.sb("cacc", [128, L], F32); t_acc = T("cacc")
        th = fw.sb("cth", [128, L], F32); t_th = T("cth")
        xc2 = fw.sb("xc2", [128, NT, L], BF16); t_xc2 = T("xc2")
        qT = fw.sb("qT", [128, NT, L], BF16); t_qT = T("qT")
        kT = fw.sb("kT", [128, NT, L], BF16); t_kT = T("kT")
        ktok = fw.sb("ktok", [128, NH, DH], BF16); t_ktok = [T("ktok%d" % h) for h in range(NH)]
        vext = fw.sb("vext", [128, NH, DH + 1], BF16); t_vext = [T("vext%d" % h) for h in range(NH)]
        gts = fw.sb("gts", [128, 64], F32); t_g = T("gts")
        g8 = fw.sb("g8", [8, 256], F32); t_g8 = T("g8")
        Mcb = fw.sb("Mcb", [128, 8], F32); dlt = fw.sb("dlt", [128, 8], F32); t_Mc = T("Mcb")
        Sp = [fw.sb("Sp%d" % i, [128, L], BF16) for i in range(2)]; t_Sp = [T("Sp%d" % i) for i in range(2)]
        hh = fw.sb("hh", [128, DH], F32); t_hh = T("hh")
        hst = fw.sb("hst", [128, 16], F32); t_hst = T("hst")
        hn = [fw.sb("hn%d" % i, [128, DH], BF16) for i in range(2)]; t_hn = [T("hn%d" % i) for i in range(2)]
        hjunk = fw.sb("hjunk", [128, DH], BF16); t_hjunk = T("hjunk")
        e1 = fw.sb("mle1", [128, L], F32); e2 = fw.sb("mle2", [128, L], F32); e3 = fw.sb("mle3", [128, L], F32)
        t_e = T("mle")
        ob = [fw.sb("mlob%d" % i, [128, NT, NB], BF16) for i in range(2)]; t_ob = [T("mlob%d" % i) for i in range(2)]
        fw.op(fw.dve, lambda: V.memset(vext[:, :, DH:DH + 1], 1.0), writes=t_vext)
        xsrc = p0T[2048:5120, :].rearrange("(j c) t -> c j t", c=128)
        zsrc = p0T[5120:8192, :].rearrange("(j c) t -> c j t", c=128)
        ydst = y0T[1024:4096, :].rearrange("(j c) t -> c j t", c=128)
        try:
            for seq in range(NSEQ):
                if DBG.get("stage", 999) < 999 and seq > 0:
                    break
                fw.op(fw.dve, lambda: V.memset(Cst[:], 0.0), writes=t_C)
                fw.op(fw.dve, lambda: V.memset(m_b[:], 0.0), writes=[t_m])
                fw.op(fw.dve, lambda: V.memset(m_c[:], 0.0), writes=[t_m])
                for tb in range(NTB):
                    if seq * NTB + tb >= DBG.get("ml_max_blocks", 99):
                        break
                    bs = (seq * NTB + tb) % 2
                    tok0 = seq * TSEQ + tb * NB
                    if tb == 0:
                        fw.op(fw.dve, lambda: V.memset(xblk[bs][:, :, 0:3], 0.0), writes=[t_xblk[bs]])
                        fw.dma(fw.sp, xblk[bs][:, :, 3:NB + 3], xsrc[:, :, tok0:tok0 + NB], reads=[t_p0], writes=[t_xblk[bs]])
                    else:
                        fw.dma(fw.sp, xblk[bs][:, :, :], xsrc[:, :, tok0 - 3:tok0 + NB], reads=[t_p0], writes=[t_xblk[bs]])
                    fw.dma(fw.sp, zblk[bs][:], zsrc[:, :, tok0:tok0 + NB], reads=[t_p0], writes=[t_zblk[bs]])
                    for j in range(3):
                        c0 = j * L
                        XB, ZB = xblk[bs], zblk[bs]
                        tXB, tZB = t_xblk[bs], t_zblk[bs]
                        for t in range(NT):
                            fw.op(fw.dve, lambda: V.tensor_scalar(out=acc[:], in0=XB[:, t, c0:c0 + L], scalar1=cw[:, 0, t:t + 1],
                                                                  scalar2=cb[:, t:t + 1], op0=ALU.mult, op1=ALU.add),
                                  reads=[tXB, t_tab], writes=[t_acc])
                            for k in range(1, 4):
                                fw.op(fw.dve, lambda: V.scalar_tensor_tensor(out=acc[:], in0=XB[:, t, c0 + k:c0 + k + L],
                                                                             scalar=cw[:, k, t:t + 1], in1=acc[:],
                                                                             op0=ALU.mult, op1=ALU.add),
                                      reads=[tXB, t_tab, t_acc], writes=[t_acc])
                            fw.op(fw.act, lambda: A.activation(out=th[:], in_=acc[:], func=AF.Tanh, scale=0.5), reads=[t_acc], writes=[t_th])
                            fw.op(fw.dve, lambda: V.scalar_tensor_tensor(out=xc2[:, t, :], in0=th[:], scalar=1.0, in1=acc[:],
                                                                         op0=ALU.add, op1=ALU.mult),
                                  reads=[t_th, t_acc], writes=[t_xc2])
                        try_stage = stage(1)
                        for t in range(NT):
                            pq = ps[0]
                            fw.op(fw.pe, lambda: P.matmul(pq[:, 0:L], lhsT=Wq[:, t, :], rhs=xc2[:, t, :], start=True, stop=True),
                                  reads=[t_xc2, t_tab], writes=[t_ps[0]])
                            fw.op(fw.act, lambda: A.copy(out=qT[:, t, :], in_=pq[:, 0:L]), reads=[t_ps[0]], writes=[t_qT])
                            pk = ps[1]
                            fw.op(fw.pe, lambda: P.matmul(pk[:, 0:L], lhsT=Wk[:, t, :], rhs=xc2[:, t, :], start=True, stop=True),
                                  reads=[t_xc2, t_tab], writes=[t_ps[1]])
                            fw.op(fw.dve, lambda: V.tensor_copy(out=kT[:, t, :], in_=pk[:, 0:L]), reads=[t_ps[1]], writes=[t_kT])
                        try_stage = stage(2)
                        pg = ps[3]
                        tg = t_ps[3]
                        for t in range(NT):
                            fw.op(fw.pe, lambda: P.matmul(pg[:L, 0:16], lhsT=xc2[:, t, :], rhs=Gqk[:, t, :], start=(t == 0), stop=False),
                                  reads=[t_xc2, t_tab], writes=[tg], signal=False)
                            fw.op(fw.pe, lambda: P.matmul(pg[:L, 0:16], lhsT=XB[:, t, c0 + 3:c0 + 3 + L], rhs=Gv[:, t, :], start=False, stop=False),
                                  reads=[tXB, t_tab], writes=[tg], signal=False)
                        fw.op(fw.pe, lambda: P.matmul(pg[:L, 0:16], lhsT=ones_b[0:1, :L], rhs=bg[0:1, :], start=False, stop=True),
                              reads=[t_tab], writes=[tg])
                        try_stage = stage(3)
                        G = [t_g]
                        fw.op(fw.act, lambda: A.activation(out=gts[:L, 8:16], in_=pg[:L, 8:16], func=AF.Exp, scale=-1.0), reads=[tg], writes=G)
                        fw.op(fw.act, lambda: A.activation(out=gts[:L, 8:16], in_=gts[:L, 8:16], func=AF.Ln, bias=1.0), reads=G, writes=G)
                        fw.op(fw.dve, lambda: V.tensor_scalar(out=gts[:L, 8:16], in0=gts[:L, 8:16], scalar1=-1.0, scalar2=None, op0=ALU.mult), reads=G, writes=G)
                        fw.op(fw.dve, lambda: V.tensor_copy(out=gts[:L, 0:8], in_=pg[:L, 0:8]), reads=[tg], writes=G)
                        try_stage = stage(4)
                        fw.op(fw.pe, lambda: P.matmul(pg[:L, 16:24], lhsT=trif[:L, :L], rhs=gts[:L, 8:16], start=True, stop=True), reads=G + [t_tab], writes=[tg])
                        fw.op(fw.pe, lambda: P.matmul(pg[:, 24:32], lhsT=ones_f[:L, :], rhs=gts[:L, 8:16], start=True, stop=True), reads=G + [t_tab], writes=[tg])
                        fw.op(fw.pe, lambda: P.matmul(pg[0:8, 32:33], lhsT=gts[:L, 8:16], rhs=ones_f[:L, 0:1], start=True, stop=True), reads=G + [t_tab], writes=[tg])
                        fw.op(fw.dve, lambda: V.tensor_copy(out=gts[:L, 40:48], in_=pg[:L, 16:24]), reads=[tg], writes=G)
                        fw.op(fw.dve, lambda: V.tensor_tensor(out=gts[:L, 16:24], in0=gts[:L, 0:8], in1=gts[:L, 40:48], op=ALU.subtract), reads=G, writes=G)
                        try_stage = stage(5)
                        fw.op(fw.pe, lambda: P.transpose(pg[0:8, 64:64 + L], gts[:L, 16:24], identf[:L, :L]), reads=G + [t_tab], writes=[tg])
                        G8 = [t_g8]
                        fw.op(fw.dve, lambda: V.reduce_max(out=g8[:, 0:1], in_=pg[0:8, 64:64 + L], axis=AX.X), reads=[tg], writes=G8)
                        fw.op(fw.dve, lambda: V.tensor_tensor(out=g8[:, 1:2], in0=g8[:, 0:1], in1=m_c[:, 0:1], op=ALU.max), reads=G8 + [t_m], writes=G8)
                        fw.op(fw.dve, lambda: V.tensor_scalar(out=g8[:, 8:16], in0=identf[0:8, 0:8], scalar1=g8[:, 1:2], scalar2=None, op0=ALU.mult), reads=G8 + [t_tab], writes=G8)
                        fw.op(fw.pe, lambda: P.matmul(pg[:, 40:48], lhsT=ones_f[0:8, :], rhs=g8[:, 8:16], start=True, stop=True), reads=G8 + [t_tab], writes=[tg])
                        try_stage = stage(6)
                        MC = [t_Mc]
                        fw.op(fw.dve, lambda: V.tensor_copy(out=Mcb[:], in_=pg[:, 40:48]), reads=[tg], writes=MC)
                        fw.op(fw.dve, lambda: V.tensor_tensor(out=dlt[:], in0=m_b[:], in1=Mcb[:], op=ALU.subtract), reads=MC + [t_m], writes=MC)
                        fw.op(fw.act, lambda: A.activation(out=dlt[:], in_=dlt[:], func=AF.Exp), reads=MC, writes=MC)
                        fw.op(fw.dve, lambda: V.tensor_tensor(out=m_b[:], in0=pg[:, 24:32], in1=Mcb[:], op=ALU.add), reads=MC + [tg], writes=[t_m])
                        fw.op(fw.dve, lambda: V.tensor_tensor(out=m_c[:, 0:1], in0=pg[0:8, 32:33], in1=g8[:, 1:2], op=ALU.add), reads=G8 + [tg], writes=[t_m])
                        fw.op(fw.dve, lambda: V.tensor_tensor(out=gts[:L, 24:32], in0=gts[:L, 16:24], in1=Mcb[:L, :], op=ALU.subtract), reads=G + MC, writes=G)
                        fw.op(fw.act, lambda: A.activation(out=gts[:L, 24:32], in_=gts[:L, 24:32], func=AF.Exp), reads=G, writes=G)
                        fw.op(fw.dve, lambda: V.tensor_tensor(out=gts[:L, 32:40], in0=gts[:L, 40:48], in1=Mcb[:L, :], op=ALU.add), reads=G + MC, writes=G)
                        fw.op(fw.act, lambda: A.activation(out=gts[:L, 32:40], in_=gts[:L, 32:40], func=AF.Exp, scale=-1.0), reads=G, writes=G)
                        try_stage = stage(7)
                        for h in range(NH):
                            pkv = ps[1]
                            tkv = t_ps[1]
                            for jj in range(3):
                                t = 3 * h + jj
                                fw.op(fw.pe, lambda: P.matmul(pkv[:L, jj * 128:(jj + 1) * 128], lhsT=xc2[:, t, :], rhs=Wk[:, t, :], start=True, stop=True),
                                      reads=[t_xc2, t_tab], writes=[tkv], signal=(jj == 2))
                            fw.op(fw.act, lambda: A.activation(out=ktok[:L, h, :], in_=pkv[:L, 0:DH], func=AF.Copy, scale=gts[:L, 24 + h:25 + h]),
                                  reads=[tkv] + G, writes=[t_ktok[h]])
                            pv = ps[2]
                            tv = t_ps[2]
                            for jj in range(3):
                                t = 3 * h + jj
                                fw.op(fw.pe, lambda: P.matmul(pv[:L, jj * 128:(jj + 1) * 128], lhsT=XB[:, t, c0 + 3:c0 + 3 + L], rhs=Wv[:, t, :], start=True, stop=True),
                                      reads=[tXB, t_tab], writes=[tv], signal=(jj == 2))
                            fw.op(fw.dve, lambda: V.tensor_copy(out=vext[:L, h, 0:DH], in_=pv[:L, 0:DH]), reads=[tv], writes=[t_vext[h]])
                            try_stage = stage(8)
                            pS = ps[3]
                            tS = t_ps[3]
                            for jj in range(3):
                                t = 3 * h + jj
                                fw.op(fw.pe, lambda: P.matmul(pS[:L, 0:L], lhsT=kT[:, t, :], rhs=qT[:, t, :], start=(jj == 0), stop=(jj == 2)),
                                      reads=[t_kT, t_qT], writes=[tS], signal=(jj == 2))
                            sp = h % 2
                            fw.op(fw.dve, lambda: V.scalar_tensor_tensor(out=Sp[sp][:L, :], in0=pS[:L, 0:L], scalar=gts[:L, 24 + h:25 + h],
                                                                         in1=trif[:L, :], op0=ALU.mult, op1=ALU.mult),
                                  reads=[tS, t_tab] + G, writes=[t_Sp[sp]])
                            try_stage = stage(9)
                            for jj in range(3):
                                fw.op(fw.act, lambda: A.activation(out=Cbf[:, h, jj, :], in_=Cst[:, h, jj, :], func=AF.Copy, scale=dlt[:, h:h + 1]),
                                      reads=[t_C[h]] + MC, writes=[t_Cb[h]])
                            pX = ps[4]
                            tX = t_ps[4]
                            fw.op(fw.pe, lambda: P.matmul(pX[:L, 0:DH + 1], lhsT=Sp[sp][:L, :], rhs=vext[:L, h, :], start=True, stop=False),
                                  reads=[t_Sp[sp], t_vext[h]], writes=[tX], signal=False)
                            for jj in range(3):
                                fw.op(fw.pe, lambda: P.matmul(pX[:L, 0:DH + 1], lhsT=qT[:, 3 * h + jj, :], rhs=Cbf[:, h, jj, :], start=False, stop=(jj == 2)),
                                      reads=[t_qT, t_Cb[h]], writes=[tX], signal=(jj == 2))
                            try_stage = stage(10)
                            for jj in range(3):
                                pc = ps[5 + jj % 2]
                                tc = t_ps[5 + jj % 2]
                                fw.op(fw.pe, lambda: P.matmul(pc[:, 0:DH + 1], lhsT=ktok[:L, h, jj * 128:(jj + 1) * 128], rhs=vext[:L, h, :], start=True, stop=True),
                                      reads=[t_ktok[h], t_vext[h]], writes=[tc])
                                fw.op(fw.dve, lambda: V.scalar_tensor_tensor(out=Cst[:, h, jj, :], in0=Cst[:, h, jj, :], scalar=dlt[:, h:h + 1],
                                                                             in1=pc[:, 0:DH + 1], op0=ALU.mult, op1=ALU.add),
                                      reads=[tc, t_C[h], t_Cb[h]] + MC, writes=[t_C[h]])
                            try_stage = stage(11)
                            H = [t_hst]
                            fw.op(fw.act, lambda: A.activation(out=hst[:L, 0:1], in_=pX[:L, DH:DH + 1], func=AF.Abs), reads=[tX], writes=H)
                            fw.op(fw.dve, lambda: V.tensor_tensor(out=hst[:L, 0:1], in0=hst[:L, 0:1], in1=gts[:L, 32 + h:33 + h], op=ALU.max),
                                  reads=H + G, writes=H)
                            fw.op(fw.dve, lambda: V.reciprocal(out=hst[:L, 1:2], in_=hst[:L, 0:1]), reads=H, writes=H)
                            fw.op(fw.dve, lambda: V.tensor_scalar(out=hh[:L, :], in0=pX[:L, 0:DH], scalar1=hst[:L, 1:2], scalar2=None, op0=ALU.mult),
                                  reads=[tX] + H, writes=[t_hh])
                            try_stage = stage(12)
                            fw.op(fw.dve, lambda: V.reduce_sum(out=hst[:L, 2:3], in_=hh[:L, :], axis=AX.X), reads=[t_hh], writes=H)
                            fw.op(fw.act, lambda: A.activation(out=hjunk[:L, :], in_=hh[:L, :], func=AF.Square,
                                                              accum_out=hst[:L, 3:4]), reads=[t_hh], writes=H + [t_hjunk])
                            fw.op(fw.dve, lambda: V.tensor_scalar(out=hst[:L, 4:5], in0=hst[:L, 2:3], scalar1=1.0 / DH, scalar2=None, op0=ALU.mult), reads=H, writes=H)
                            fw.op(fw.dve, lambda: V.tensor_tensor(out=hst[:L, 5:6], in0=hst[:L, 4:5], in1=hst[:L, 4:5], op=ALU.mult), reads=H, writes=H)
                            fw.op(fw.dve, lambda: V.scalar_tensor_tensor(out=hst[:L, 6:7], in0=hst[:L, 3:4], scalar=1.0 / DH, in1=hst[:L, 5:6],
                                                                         op0=ALU.mult, op1=ALU.subtract), reads=H, writes=H)
                            fw.op(fw.act, lambda: A.activation(out=hst[:L, 7:8], in_=hst[:L, 6:7], func=AF.Ln, bias=HEAD_EPS), reads=H, writes=H)
                            fw.op(fw.act, lambda: A.activation(out=hst[:L, 8:9], in_=hst[:L, 7:8], func=AF.Exp, scale=-0.5), reads=H, writes=H)
                            fw.op(fw.dve, lambda: V.tensor_scalar(out=hn[h % 2][:L, :], in0=hh[:L, :], scalar1=hst[:L, 4:5], scalar2=hst[:L, 8:9],
                                                                  op0=ALU.subtract, op1=ALU.mult), reads=[t_hh] + H, writes=[t_hn[h % 2]])
                            try_stage = stage(13)
                            for jj in range(3):
                                t = 3 * h + jj
                                pT = pTb[:, jj * 128:jj * 128 + L]
                                tT = t_pTb[0]
                                fw.op(fw.pe, lambda: P.transpose(pT, hn[h % 2][:L, jj * 128:(jj + 1) * 128], identb[:L, :L]),
                                      reads=[t_hn[h % 2], t_tab], writes=[tT])
                                E = [t_e]
                                fw.op(fw.dve, lambda: V.tensor_scalar(out=e1[:], in0=xc2[:, t, :], scalar1=sk[:, t:t + 1], scalar2=None, op0=ALU.mult),
                                      reads=[t_xc2, t_tab], writes=E)
                                fw.op(fw.dve, lambda: V.scalar_tensor_tensor(out=e1[:], in0=pT, scalar=nw[:, t:t + 1], in1=e1[:],
                                                                             op0=ALU.mult, op1=ALU.add), reads=[tT, t_tab] + E, writes=E)
                                fw.op(fw.act, lambda: A.activation(out=e2[:], in_=ZB[:, t, c0:c0 + L], func=AF.Tanh, scale=0.5), reads=[tZB], writes=E)
                                fw.op(fw.dve, lambda: V.scalar_tensor_tensor(out=e2[:], in0=e2[:], scalar=1.0, in1=ZB[:, t, c0:c0 + L],
                                                                             op0=ALU.add, op1=ALU.mult), reads=E + [tZB], writes=E)
                                fw.op(fw.dve, lambda: V.scalar_tensor_tensor(out=ob[bs][:, t, c0:c0 + L], in0=e1[:], scalar=0.5, in1=e2[:],
                                                                             op0=ALU.mult, op1=ALU.mult), reads=E, writes=[t_ob[bs]])
                    fw.dma(fw.act, ydst[:, :, tok0:tok0 + NB], ob[bs][:], reads=[t_ob[bs]], writes=[t_y0], owner=t_ob[bs])
        except StopEmit:
            pass
        fw.barrier()
        fw.stack = old


def phase_ssd(fw, p1T, t_p1, dtT_d, prm, y1T, t_y1):
    nc = fw.nc
    V, A, P = nc.vector, nc.scalar, nc.tensor
    L = TS
    NX = 48
    HP = 64
    with contextlib.ExitStack() as st:
        old = fw.stack
        fw.stack = st
        cw = fw.sb("scw", [128, 4, NX], F32)
        cb = fw.sb("scb", [128, NX], F32)
        dtb = fw.sb("dtb", [64, 1], F32)
        aneg = fw.sb("aneg", [64, 1], F32)
        D_b = fw.sb("D_b", [128, 64], F32)
        gn = fw.sb("gn", [128, 32], F32)
        Sel = fw.sb("Sel", [64, 64, L], F32)
        maskb = fw.sb("maskb", [128, L], F32)
        trif = fw.sb("trif", [128, L], F32)
        identf = fw.sb("identf", [128, 128], F32)
        identb = fw.sb("identb", [128, 128], BF16)
        ones_f = fw.sb("ones_f", [128, 128], F32)
        stT = fw.sb("stT", [128, 64, HP], F32)
        stb = fw.sb("stb", [128, 64, HP], BF16)
        t_tab = T("ssdtab")
        t_st = [T("st%d" % g) for g in range(8)]
        t_stb = [T("stb%d" % g) for g in range(8)]
        pf = [fw.ps("sdps%d" % i, [128, 512], F32) for i in range(6)]
        t_pf = [T("sdps%d" % i) for i in range(6)]
        pb = [fw.ps("sdpb%d" % i, [128, 1024], BF16) for i in range(2)]
        t_pb = [T("sdpb%d" % i) for i in range(2)]
        S = [T("ssdsetup")]
        o = lambda eng, fn: fw.op(eng, fn, reads=S, writes=S)
        with contextlib.ExitStack() as st2:
            fw.stack = st2
            tmpf = fw.sb("stmpf", [128, NX], F32)
            alog = fw.sb("alog", [64, 1], F32)
            fw.dma(fw.sp, identf[:], fw.consts["ident"], writes=S)
            fw.dma(fw.sp, trif[:L, :], fw.consts["tri"], writes=S)
            for k in range(4):
                fw.dma(fw.sp, cw[:, k, :], prm["ssd_conv_w"][k, :].rearrange("(t p) -> p t", p=128), writes=S, allow_slow_non_contiguous=True)
            fw.dma(fw.sp, cb[:], prm["ssd_conv_b"].rearrange("(t p) -> p t", p=128), writes=S, allow_slow_non_contiguous=True)
            fw.dma(fw.sp, gn[:], prm["ssd_gnorm"].rearrange("(t p) -> p t", p=128), writes=S, allow_slow_non_contiguous=True)
            fw.dma(fw.sp, dtb[:], prm["ssd_dt_bias"].rearrange("(h a) -> h a", a=1), writes=S)
            fw.dma(fw.sp, alog[:], prm["ssd_a_log"].rearrange("(h a) -> h a", a=1), writes=S)
            fw.dma(fw.sp, D_b[:], prm["ssd_d"].partition_broadcast(128), writes=S)
            o(fw.dve, lambda: V.tensor_scalar(out=cw[:], in0=cw[:], scalar1=0.5, scalar2=None, op0=ALU.mult))
            o(fw.dve, lambda: V.tensor_scalar(out=cb[:], in0=cb[:], scalar1=0.5, scalar2=None, op0=ALU.mult))
            o(fw.act, lambda: A.activation(out=aneg[:], in_=alog[:], func=AF.Exp))
            o(fw.dve, lambda: V.tensor_scalar(out=aneg[:], in0=aneg[:], scalar1=-1.0, scalar2=None, op0=ALU.mult))
            o(fw.dve, lambda: V.tensor_copy(out=identb[:], in_=identf[:]))
            o(fw.dve, lambda: V.memset(ones_f[:], 1.0))
            o(fw.dve, lambda: V.tensor_copy(out=Sel[:], in_=identf[0:64, 0:64].unsqueeze(2).to_broadcast([64, 64, L])))
            o(fw.dve, lambda: V.tensor_scalar(out=maskb[:L, :], in0=trif[:L, :], scalar1=-1.0, scalar2=30000.0, op0=ALU.add, op1=ALU.mult))
            fw.op(fw.dve, lambda: V.memset(stT[:], 0.0), reads=S, writes=S + [t_tab] + t_st)
            fw.barrier()
        fw.stack = st

        xblk = fw.sb("sxblk", [128, NX, NB + 3], BF16); t_xblk = T("sxblk")
        zblk = fw.sb("szblk", [128, 32, NB], BF16); t_zblk = T("szblk")
        dblk = fw.sb("sdblk", [64, NB], F32); t_dblk = T("sdblk")
        ob = fw.sb("sob", [128, 32, NB], BF16); t_ob = T("sob")
        acc = fw.sb("sacc", [128, L], F32); t_acc = T("sacc")
        th = fw.sb("sth", [128, L], F32); t_th = T("sth")
        xc = fw.sb("sxc", [128, NX, L], BF16); t_xc = T("sxc")
        xtok = fw.sb("xtok", [128, 40 * 128], BF16); t_xtok = T("xtok")
        fT = fw.sb("fT", [64, 4, L], F32); t_fT = T("fT")
        tk = fw.sb("tk", [128, 5, 64], F32); t_tk = T("tk")
        dg = fw.sb("sdg", [64, 64], F32); t_dg = T("sdg")
        elb = fw.sb("elb", [128, 64], F32); lastb = fw.sb("lastb", [128, 64], F32); t_el = T("elb")
        cbT = fw.sb("cbT", [128, L], F32); t_cbT = T("cbT")
        Eh = [fw.sb("Eh%d" % i, [128, L], F32) for i in range(2)]; t_Eh = [T("Eh%d" % i) for i in range(2)]
        Wh = [fw.sb("Wh%d" % i, [128, L], BF16) for i in range(2)]; t_Wh = [T("Wh%d" % i) for i in range(2)]
        ys = fw.sb("sys", [128, 512], F32); xd = fw.sb("sxd", [128, 512], F32); t_ys = T("sys")
        xsd = fw.sb("xsd", [128, 512], BF16); t_xsd = T("xsd")
        gz = fw.sb("sgz", [128, 512], F32); t_gz = T("sgz")
        yz = fw.sb("syz", [128, 8, 512], BF16); t_yz = T("syz")
        sq = fw.sb("ssq", [128, 512], BF16); t_sq = T("ssq")
        nst = fw.sb("snst", [128, 32], F32); t_nst = T("snst")
        yn = fw.sb("syn", [128, 512], BF16); t_yn = T("syn")
        xsrc = p1T[4096:10240, :].rearrange("(j c) t -> c j t", c=128)
        zsrc = p1T[0:4096, :].rearrange("(j c) t -> c j t", c=128)
        ydst = y1T.rearrange("(j c) t -> c j t", c=128)
        try:
            for seq in range(NSEQ):
                if seq > 0:
                    fw.op(fw.dve, lambda: V.memset(stT[:], 0.0), writes=t_st)
                fw.op(fw.dve, lambda: V.memset(stb[:], 0.0), writes=t_stb)
                for tb in range(NTB):
                    if seq * NTB + tb >= DBG.get("ssd_max_blocks", 99):
                        raise StopEmit()
                    tok0 = seq * TSEQ + tb * NB
                    if tb == 0:
                        fw.op(fw.dve, lambda: V.memset(xblk[:, :, 0:3], 0.0), writes=[t_xblk])
                        fw.dma(fw.sp, xblk[:, 0:24, 3:NB + 3], xsrc[:, 0:24, tok0:tok0 + NB], reads=[t_p1], writes=[t_xblk])
                        fw.dma(fw.sp, xblk[:, 24:48, 3:NB + 3], xsrc[:, 24:48, tok0:tok0 + NB], reads=[t_p1], writes=[t_xblk])
                    else:
                        fw.dma(fw.sp, xblk[:, 0:24, :], xsrc[:, 0:24, tok0 - 3:tok0 + NB], reads=[t_p1], writes=[t_xblk])
                        fw.dma(fw.sp, xblk[:, 24:48, :], xsrc[:, 24:48, tok0 - 3:tok0 + NB], reads=[t_p1], writes=[t_xblk])
                    fw.dma(fw.sp, zblk[:], zsrc[:, :, tok0:tok0 + NB], reads=[t_p1], writes=[t_zblk])
                    fw.dma(fw.sp, dblk[:], dtT_d[:, tok0:tok0 + NB], reads=[t_p1], writes=[t_dblk])
                    for j in range(3):
                        c0 = j * L
                        for t in range(NX):
                            fw.op(fw.dve, lambda: V.tensor_scalar(out=acc[:], in0=xblk[:, t, c0:c0 + L], scalar1=cw[:, 0, t:t + 1],
                                                                  scalar2=cb[:, t:t + 1], op0=ALU.mult, op1=ALU.add),
                                  reads=[t_xblk, t_tab], writes=[t_acc])
                            for k in range(1, 4):
                                fw.op(fw.dve, lambda: V.scalar_tensor_tensor(out=acc[:], in0=xblk[:, t, c0 + k:c0 + k + L],
                                                                             scalar=cw[:, k, t:t + 1], in1=acc[:], op0=ALU.mult, op1=ALU.add),
                                      reads=[t_xblk, t_tab, t_acc], writes=[t_acc])
                            fw.op(fw.act, lambda: A.activation(out=th[:], in_=acc[:], func=AF.Tanh), reads=[t_acc], writes=[t_th])
                            fw.op(fw.dve, lambda: V.scalar_tensor_tensor(out=xc[:, t, :], in0=th[:], scalar=1.0, in1=acc[:], op0=ALU.add, op1=ALU.mult),
                                  reads=[t_th, t_acc], writes=[t_xc])
                        stage(21)
                        F_ = [t_fT]
                        fw.op(fw.act, lambda: A.activation(out=fT[:, 0, :], in_=dblk[:, c0:c0 + L], func=AF.Exp, bias=dtb[:, 0:1]), reads=[t_dblk, t_tab], writes=F_)
                        fw.op(fw.act, lambda: A.activation(out=fT[:, 0, :], in_=fT[:, 0, :], func=AF.Ln, bias=1.0), reads=F_, writes=F_)
                        fw.op(fw.dve, lambda: V.tensor_scalar(out=fT[:, 1, :], in0=fT[:, 0, :], scalar1=aneg[:, 0:1], scalar2=None, op0=ALU.mult), reads=F_ + [t_tab], writes=F_)
                        fw.op(fw.dve, lambda: V.tensor_tensor_scan(out=fT[:, 2, :], data0=ones_f[0:64, 0:L], data1=fT[:, 1, :], initial=0.0, op0=ALU.mult, op1=ALU.add),
                              reads=F_ + [t_tab], writes=F_)
                        pm = pf[5]; tm = t_pf[5]
                        fw.op(fw.pe, lambda: P.transpose(pm[:L, 0:64], fT[:, 0, :], identf[0:64, 0:64]), reads=F_ + [t_tab], writes=[tm])
                        fw.op(fw.pe, lambda: P.transpose(pm[:L, 64:128], fT[:, 2, :], identf[0:64, 0:64]), reads=F_ + [t_tab], writes=[tm])
                        fw.op(fw.dve, lambda: V.tensor_scalar(out=dg[:], in0=identf[0:64, 0:64], scalar1=fT[:, 2, L - 1:L], scalar2=None, op0=ALU.mult), reads=F_ + [t_tab], writes=[t_dg])
                        fw.op(fw.pe, lambda: P.matmul(pm[:, 128:192], lhsT=ones_f[0:64, :], rhs=dg[:], start=True, stop=True), reads=[t_dg, t_tab], writes=[tm])
                        K_ = [t_tk]
                        fw.op(fw.dve, lambda: V.tensor_copy(out=tk[:L, 0, :], in_=pm[:L, 0:64]), reads=[tm], writes=K_)
                        fw.op(fw.dve, lambda: V.tensor_copy(out=tk[:L, 4, :], in_=pm[:L, 64:128]), reads=[tm], writes=K_)
                        fw.op(fw.dve, lambda: V.tensor_scalar(out=tk[:L, 1, :], in0=pm[:L, 64:128], scalar1=-1.0, scalar2=None, op0=ALU.mult), reads=[tm], writes=K_)
                        fw.op(fw.act, lambda: A.activation(out=tk[:L, 2, :], in_=pm[:L, 64:128], func=AF.Exp), reads=[tm], writes=K_)
                        fw.op(fw.dve, lambda: V.tensor_copy(out=lastb[:], in_=pm[:, 128:192]), reads=[tm] + K_, writes=[t_el])
                        fw.op(fw.act, lambda: A.activation(out=elb[:], in_=lastb[:], func=AF.Exp), reads=[t_el], writes=[t_el])
                        fw.op(fw.dve, lambda: V.tensor_tensor(out=tk[:L, 3, :], in0=lastb[:L, :], in1=tk[:L, 4, :], op=ALU.subtract), reads=[t_el] + K_, writes=K_)
                        fw.op(fw.act, lambda: A.activation(out=tk[:L, 3, :], in_=tk[:L, 3, :], func=AF.Exp), reads=K_, writes=K_)
                        fw.op(fw.dve, lambda: V.tensor_tensor(out=tk[:L, 3, :], in0=tk[:L, 3, :], in1=tk[:L, 0, :], op=ALU.mult), reads=K_, writes=K_)
                        stage(22)
                        for r5 in range(5):
                            pbb = pb[r5 % 2]; tpb = t_pb[r5 % 2]
                            for i8 in range(8):
                                t = r5 * 8 + i8
                                fw.op(fw.pe, lambda: P.transpose(pbb[:L, i8 * 128:(i8 + 1) * 128], xc[:, t, :], identb[:, :]),
                                      reads=[t_xc, t_tab], writes=[tpb], signal=(i8 == 7))
                            if r5 % 2 == 0:
                                fw.op(fw.act, lambda: A.copy(out=xtok[:L, r5 * 1024:(r5 + 1) * 1024], in_=pbb[:L, :]), reads=[tpb], writes=[t_xtok])
                            else:
                                fw.op(fw.dve, lambda: V.tensor_copy(out=xtok[:L, r5 * 1024:(r5 + 1) * 1024], in_=pbb[:L, :]), reads=[tpb], writes=[t_xtok])
                        stage(23)
                        for g in range(8):
                            bmf = xc[:, 32 + g, :]
                            cmf = xc[:, 40 + g, :]
                            fw.op(fw.pe, lambda: P.matmul(pf[0][:L, 0:L], lhsT=bmf, rhs=cmf, start=True, stop=True), reads=[t_xc], writes=[t_pf[0]])
                            fw.op(fw.act, lambda: A.copy(out=cbT[:L, :], in_=pf[0][:L, 0:L]), reads=[t_pf[0]], writes=[t_cbT])
                            fw.op(fw.pe, lambda: P.matmul(pf[3][:L, :], lhsT=cmf, rhs=stb[:, 8 * g:8 * g + 8, :].rearrange("n h p -> n (h p)"), start=True, stop=True),
                                  reads=[t_xc, t_stb[g]], writes=[t_pf[3]])
                            for hl in range(8):
                                hh_ = 8 * g + hl
                                e = hl % 2
                                fw.op(fw.pe, lambda: P.matmul(pf[1][:L, 0:L], lhsT=Sel[:, hh_, :], rhs=fT[:, 2, :], start=True, stop=False),
                                      reads=F_ + [t_tab], writes=[t_pf[1]], signal=False)
                                fw.op(fw.pe, lambda: P.matmul(pf[1][:L, 0:L], lhsT=identf[:L, :L], rhs=maskb[:L, :], start=False, stop=True),
                                      reads=[t_tab], writes=[t_pf[1]])
                                fw.op(fw.act, lambda: A.activation(out=Eh[e][:L, :], in_=pf[1][:L, 0:L], func=AF.Exp, bias=tk[:L, 1, hh_:hh_ + 1]),
                                      reads=[t_pf[1]] + K_, writes=[t_Eh[e]])
                                fw.op(fw.dve, lambda: V.scalar_tensor_tensor(out=Wh[e][:L, :], in0=Eh[e][:L, :], scalar=tk[:L, 0, hh_:hh_ + 1], in1=cbT[:L, :],
                                                                             op0=ALU.mult, op1=ALU.mult), reads=[t_Eh[e], t_cbT] + K_, writes=[t_Wh[e]])
                                fw.op(fw.pe, lambda: P.matmul(pf[2][:L, hl * HP:(hl + 1) * HP], lhsT=Wh[e][:L, :], rhs=xtok[:L, hh_ * HP:(hh_ + 1) * HP], start=True, stop=True),
                                      reads=[t_Wh[e], t_xtok], writes=[t_pf[2]], signal=(hl == 7))
                            g8 = slice(8 * g, 8 * g + 8)
                            xg = xtok[:L, g * 512:(g + 1) * 512].rearrange("s (h p) -> s h p", p=HP)
                            Y = [t_ys]
                            fw.op(fw.dve, lambda: V.tensor_tensor(out=ys[:L, :].rearrange("s (h p) -> s h p", p=HP), in0=pf[3][:L, :].rearrange("s (h p) -> s h p", p=HP),
                                                                  in1=tk[:L, 2, g8].unsqueeze(2).to_broadcast([L, 8, HP]), op=ALU.mult), reads=[t_pf[3]] + K_, writes=Y)
                            fw.op(fw.dve, lambda: V.tensor_tensor(out=xd[:L, :].rearrange("s (h p) -> s h p", p=HP), in0=xg,
                                                                  in1=D_b[:L, g8].unsqueeze(2).to_broadcast([L, 8, HP]), op=ALU.mult), reads=[t_xtok, t_tab], writes=Y)
                            fw.op(fw.dve, lambda: V.tensor_tensor(out=ys[:L, :], in0=ys[:L, :], in1=xd[:L, :], op=ALU.add), reads=Y, writes=Y)
                            fw.op(fw.dve, lambda: V.tensor_tensor(out=ys[:L, :], in0=pf[2][:L, :], in1=ys[:L, :], op=ALU.add), reads=Y + [t_pf[2]], writes=Y)
                            fw.op(fw.dve, lambda: V.tensor_tensor(out=xsd[:L, :].rearrange("s (h p) -> s h p", p=HP), in0=xg,
                                                                  in1=tk[:L, 3, g8].unsqueeze(2).to_broadcast([L, 8, HP]), op=ALU.mult), reads=[t_xtok] + K_, writes=[t_xsd])
                            fw.op(fw.pe, lambda: P.matmul(pf[4][:, :], lhsT=xtok[:L, 4096 + g * 128:4096 + (g + 1) * 128], rhs=xsd[:L, :], start=True, stop=True),
                                  reads=[t_xtok, t_xsd], writes=[t_pf[4]])
                            fw.op(fw.dve, lambda: V.tensor_tensor(out=stT[:, g8, :], in0=stT[:, g8, :], in1=elb[:, g8].unsqueeze(2).to_broadcast([128, 8, HP]), op=ALU.mult),
                                  reads=[t_st[g], t_el], writes=[t_st[g]])
                            fw.op(fw.dve, lambda: V.tensor_tensor(out=stT[:, g8, :].rearrange("n h p -> n (h p)"), in0=stT[:, g8, :].rearrange("n h p -> n (h p)"), in1=pf[4][:, :], op=ALU.add),
                                  reads=[t_st[g], t_pf[4]], writes=[t_st[g]])
                            fw.op(fw.act, lambda: A.copy(out=stb[:, g8, :], in_=stT[:, g8, :]), reads=[t_st[g], t_pf[3]], writes=[t_stb[g]])
                            pz = pb[g % 2]; tpz = t_pb[g % 2]
                            for i4 in range(4):
                                fw.op(fw.pe, lambda: P.transpose(pz[:L, i4 * 128:(i4 + 1) * 128], zblk[:, 4 * g + i4, c0:c0 + L], identb[:, :]),
                                      reads=[t_zblk, t_tab], writes=[tpz], signal=(i4 == 3))
                            fw.op(fw.act, lambda: A.activation(out=gz[:L, :], in_=pz[:L, 0:512], func=AF.Tanh, scale=0.5), reads=[tpz], writes=[t_gz])
                            fw.op(fw.dve, lambda: V.scalar_tensor_tensor(out=gz[:L, :], in0=gz[:L, :], scalar=1.0, in1=pz[:L, 0:512], op0=ALU.add, op1=ALU.mult),
                                  reads=[t_gz, tpz], writes=[t_gz])
                            fw.op(fw.dve, lambda: V.scalar_tensor_tensor(out=yz[:L, g, :], in0=ys[:L, :], scalar=0.5, in1=gz[:L, :], op0=ALU.mult, op1=ALU.mult),
                                  reads=Y + [t_gz], writes=[t_yz])
                            fw.op(fw.act, lambda: A.activation(out=sq[:L, :], in_=yz[:L, g, :], func=AF.Square, accum_out=nst[:L, g:g + 1]),
                                  reads=[t_yz], writes=[t_sq, t_nst])
                        stage(24)
                        N_ = [t_nst]
                        fw.op(fw.act, lambda: A.activation(out=nst[:L, 8:16], in_=nst[:L, 0:8], func=AF.Ln, scale=1.0 / 512, bias=EPS), reads=N_, writes=N_)
                        fw.op(fw.act, lambda: A.activation(out=nst[:L, 16:24], in_=nst[:L, 8:16], func=AF.Exp, scale=-0.5), reads=N_, writes=N_)
                        for g in range(8):
                            fw.op(fw.dve, lambda: V.tensor_scalar(out=yn[:L, :], in0=yz[:L, g, :], scalar1=nst[:L, 16 + g:17 + g], scalar2=None, op0=ALU.mult),
                                  reads=[t_yz] + N_, writes=[t_yn])
                            py = pb[g % 2]; tpy = t_pb[g % 2]
                            for i4 in range(4):
                                fw.op(fw.pe, lambda: P.transpose(py[:, i4 * L:(i4 + 1) * L], yn[:L, i4 * 128:(i4 + 1) * 128], identb[:L, :L]),
                                      reads=[t_yn, t_tab], writes=[tpy], signal=(i4 == 3))
                            for i4 in range(4):
                                t = 4 * g + i4
                                fw.op(fw.act, lambda: A.activation(out=ob[:, t, c0:c0 + L], in_=py[:, i4 * L:(i4 + 1) * L], func=AF.Copy, scale=gn[:, t:t + 1]),
                                      reads=[tpy, t_tab], writes=[t_ob])
                    fw.dma(fw.act, ydst[:, :, tok0:tok0 + NB], ob[:], reads=[t_ob], writes=[t_y1], owner=t_ob)
        except StopEmit:
            pass
        fw.barrier()
        fw.stack = old


PARAM_SHAPES = {
    "ab_norm": [D], "ab_w_in": [D, 8192],
    "s5_lambda_re": [64, 64], "s5_lambda_im": [64, 64], "s5_log_dt": [64], "s5_b_re": [64, 64, 16], "s5_b_im": [64, 64, 16],
    "s5_c_re": [64, 16, 64], "s5_c_im": [64, 16, 64], "s5_d": [1024], "s5_glu_w": [1024, 1024], "s5_glu_b": [1024],
    "ml_conv_w": [4, 3072], "ml_conv_b": [3072], "ml_wq": [768, 4, 4], "ml_wk": [768, 4, 4], "ml_wv": [768, 4, 4],
    "ml_w_gate": [9216, 16], "ml_b_gate": [16], "ml_norm": [3072], "ml_skip": [3072], "ab_w_out": [4096, D],
    "ssd_norm": [D], "ssd_w_in": [D, 10304], "ssd_conv_w": [4, 6144], "ssd_conv_b": [6144], "ssd_dt_bias": [64],
    "ssd_a_log": [64], "ssd_d": [64], "ssd_gnorm": [4096], "ssd_w_out": [4096, D], "final_norm": [D],
}


def build_program():
    nc = bass.Bass("TRN2", target_bir_lowering=False)
    dI = lambda n, s, dt=F32: nc.dram_tensor(n, list(s), dt, kind="ExternalInput").ap()
    dS = lambda n, s, dt: nc.dram_tensor(n, list(s), dt, kind="Internal").ap()
    x = dI("x", [NSEQ, SEQ, D])
    meta = dI("meta_tokens", [NMETA, D])
    prm = {k: dI(k, v) for k, v in PARAM_SHAPES.items()}
    cst = {k: dI("c_" + k, v) for k, v in CONST_SHAPES.items()}
    out = nc.dram_tensor("out", [NSEQ, SEQ, D], F32, kind="ExternalOutput").ap()
    wb0 = dS("wb0", [D, 8192], BF16); glub = dS("glub", [1024, 1024], BF16); wo0 = dS("wo0", [4096, D], BF16)
    wb1 = dS("wb1", [D, 10304], BF16); wo1 = dS("wo1", [4096, D], BF16)
    p0T = dS("p0T", [8192, TTOT], BF16); y0T = dS("y0T", [4096, TTOT], BF16)
    h1 = dS("h1", [TTOT, D], F32)
    p1T = dS("p1T", [10304, TTOT], BF16); dtT = dS("dtT", [64, TTOT], F32); y1T = dS("y1T", [4096, TTOT], BF16)
    with contextlib.ExitStack() as st:
        fw = FW(nc, st)
        fw.consts = cst
        fw.ident_dram = cst["ident"]
        t_wb0, t_glub, t_wo0, t_wb1, t_wo1 = T("wb0"), T("glub"), T("wo0"), T("wb1"), T("wo1")
        t_p0, t_y0, t_h1, t_p1, t_y1, t_out = T("p0T"), T("y0T"), T("h1"), T("p1T"), T("y1T"), T("out")
        cast_weights(fw, wb0, prm["ab_w_in"], D, t_wb0)
        cast_weights(fw, glub, prm["s5_glu_w"], 1024, t_glub)
        cast_weights(fw, wo0, prm["ab_w_out"], 4096, t_wo0)
        cast_weights(fw, wb1, prm["ssd_w_in"], D, t_wb1)
        cast_weights(fw, wo1, prm["ssd_w_out"], 4096, t_wo1)

        def load_h0(seq, tt, dst, tr):
            t0 = tt * TS
            if tt == 0:
                fw.dma(fw.sp, dst[0:NMETA, :], meta, writes=[tr])
                fw.dma(fw.sp, dst[NMETA:TS, :], x[seq, 0:TS - NMETA, :], writes=[tr])
            else:
                fw.dma(fw.sp, dst[0:TS, :], x[seq, t0 - NMETA:t0 - NMETA + TS, :], writes=[tr])

        def load_h1(seq, tt, dst, tr):
            r0 = seq * TSEQ + tt * TS
            fw.dma(fw.sp, dst[0:TS, :], h1[r0:r0 + TS, :], reads=[t_h1], writes=[tr])

        def store_h1(seq, tt, src, tr):
            r0 = seq * TSEQ + tt * TS
            fw.dma(fw.act, h1[r0:r0 + TS, :], src[0:TS, :], reads=[tr], writes=[t_h1], owner=tr)

        def store_out(seq, tt, src, tr):
            t0 = tt * TS
            if tt == 0:
                fw.dma(fw.act, out[seq, 0:TS - NMETA, :], src[NMETA:TS, :], reads=[tr], writes=[t_out], owner=tr)
            else:
                fw.dma(fw.act, out[seq, t0 - NMETA:t0 - NMETA + TS, :], src[0:TS, :], reads=[tr], writes=[t_out], owner=tr)

        phases = [
            lambda: phase_in_proj(fw, load_h0, prm["ab_norm"], wb0, t_wb0, 8192, p0T, t_p0),
            lambda: phase_s5(fw, p0T, t_p0, prm, glub, t_glub, y0T, t_y0),
            lambda: phase_mlstm(fw, p0T, t_p0, prm, y0T, t_y0),
            lambda: phase_out_proj(fw, y0T, t_y0, wo0, t_wo0, load_h0, store_h1),
            lambda: phase_in_proj(fw, load_h1, prm["ssd_norm"], wb1, t_wb1, 10304, p1T, t_p1, dt_out=(dtT, 10240)),
            lambda: phase_ssd(fw, p1T, t_p1, dtT, prm, y1T, t_y1),
            lambda: phase_out_proj(fw, y1T, t_y1, wo1, t_wo1, load_h1, store_out, final_g=prm["final_norm"]),
        ]
        for i, ph in enumerate(phases):
            if i in DBG.get("skip_phases", ()):
                continue
            ph()
        fw.barrier()
        fw.finish(fw.sp, [t_out])
    return nc


_NC_CACHE = {}


def kernel(**inputs):
    n_cores = 8
    if "nc" not in _NC_CACHE:
        _NC_CACHE["nc"] = build_program()
    nc = _NC_CACHE["nc"]
    consts = host_consts()
    shared = {}
    for k in PARAM_SHAPES:
        a = np.asarray(inputs[k], dtype=np.float32)
        if k != "final_norm":
            a = a[0]
        shared[k] = np.ascontiguousarray(a)
    shared["meta_tokens"] = np.ascontiguousarray(np.asarray(inputs["meta_tokens"], dtype=np.float32))
    for k, v in consts.items():
        shared["c_" + k] = v
    xin = np.asarray(inputs["x"], dtype=np.float32)
    in_maps = []
    for c in range(n_cores):
        m = dict(shared)
        m["x"] = np.ascontiguousarray(xin[c * NSEQ:(c + 1) * NSEQ])
        in_maps.append(m)
    res = run_bass_kernel_spmd(nc, in_maps, core_ids=list(range(n_cores)))
    return np.concatenate([np.asarray(r["out"], dtype=np.float32) for r in res.results], axis=0)
```

```python
import contextlib
import numpy as np
import concourse.bass as bass
import concourse.mybir as mybir
from concourse.bass_utils import run_bass_kernel_spmd

F32 = mybir.dt.float32
BF16 = mybir.dt.bfloat16
AF = mybir.ActivationFunctionType
ALU = mybir.AluOpType
AX = mybir.AxisListType


class Eng:
    def __init__(self, name, h, sem, is_pe=False):
        self.name, self.h, self.sem, self.is_pe = name, h, sem, is_pe
        self.count = 0
        self.pending = False
        self.known = {}


class T:
    __slots__ = ("name", "w", "r", "dsem", "dcount")

    def __init__(self, name):
        self.name = name
        self.w = None
        self.r = {}
        self.dsem = None
        self.dcount = 0


class FW:
    def __init__(self, nc, stack):
        self.nc, self.stack = nc, stack
        self.root = stack
        self.nsem = 0
        self.free_dsems = []
        self.consts = {}
        self.pe = Eng("pe", nc.tensor, self.sem("pe"), is_pe=True)
        self.dve = Eng("dve", nc.vector, self.sem("dve"))
        self.act = Eng("act", nc.scalar, self.sem("act"))
        self.pool = Eng("pool", nc.gpsimd, self.sem("pool"))
        self.sp = Eng("sp", nc.sync, self.sem("sp"))
        self.engs = [self.pe, self.dve, self.act, self.pool, self.sp]
        self.ninstr = 0
        self.dma_owners = []
        self.ident_dram = None

    def sem(self, name):
        self.nsem += 1
        self.uid = getattr(self, "uid", 0) + 1
        return self.root.enter_context(self.nc.semaphore("%s_u%d" % (name, self.uid)))

    def sb(self, name, shape, dt):
        self.uid = getattr(self, "uid", 0) + 1
        return self.stack.enter_context(self.nc.sbuf_tensor("%s_u%d" % (name, self.uid), list(shape), dt))

    def ps(self, name, shape, dt):
        self.uid = getattr(self, "uid", 0) + 1
        return self.stack.enter_context(self.nc.psum_tensor("%s_u%d" % (name, self.uid), list(shape), dt))

    def _wait(self, eng, tok):
        if tok is None:
            return
        sem, val, src = tok
        if src is eng and eng.is_pe:
            return
        k = id(sem)
        if eng.known.get(k, 0) >= val:
            return
        eng.h.wait_ge(sem, val)
        eng.known[k] = val
        self.ninstr += 1

    def _deps(self, eng, reads, writes):
        for t in reads:
            self._wait(eng, t.w)
        for t in writes:
            self._wait(eng, t.w)
            for tok in t.r.values():
                self._wait(eng, tok)

    def _record(self, tok, reads, writes):
        for t in reads:
            t.r[id(tok[0])] = tok
        for t in writes:
            t.w = tok
            t.r = {}

    def op(self, eng, fn, reads=(), writes=(), signal=True):
        self._deps(eng, reads, writes)
        ins = fn()
        self.ninstr += 1
        if signal:
            eng.count += 1
            ins.then_inc(eng.sem, 1)
            tok = (eng.sem, eng.count, eng)
        else:
            assert eng.is_pe
            tok = (eng.sem, eng.count + 1, eng)
        self._record(tok, reads, writes)
        return ins

    def dma(self, q, out, in_, reads=(), writes=(), owner=None, **kw):
        self._deps(q, reads, writes)
        if owner is None:
            owner = writes[0] if writes else reads[0]
        if owner.dsem is None:
            if self.free_dsems:
                owner.dsem, owner.dcount = self.free_dsems.pop()
            else:
                owner.dsem = self.sem("d_" + owner.name)
            self.dma_owners.append(owner)
        ins = q.h.dma_start(out=out, in_=in_, **kw)
        owner.dcount += 16
        ins.then_inc(owner.dsem, 16)
        self.ninstr += 1
        tok = (owner.dsem, owner.dcount, None)
        self._record(tok, reads, writes)
        return ins

    def finish(self, eng, trackers):
        for t in trackers:
            self._wait(eng, t.w)
            for tok in t.r.values():
                self._wait(eng, tok)

    def barrier(self):
        for e in self.engs:
            for e2 in self.engs:
                if e2 is e or e2.count == 0:
                    continue
                self._wait(e, (e2.sem, e2.count, e2))
            for t in self.dma_owners:
                if t.dcount:
                    self._wait(e, (t.dsem, t.dcount, None))
        for t in self.dma_owners:
            self.free_dsems.append((t.dsem, t.dcount))
            t.dsem = None
            t.dcount = 0
        self.dma_owners = []
        for e in self.engs:
            if e.count > 0:
                e.sem = self.sem(e.name)
                e.count = 0


D = 2048
KC = D // 128
NSEQ = 2
NMETA = 16
SEQ = 2048
TSEQ = SEQ + NMETA
TS = 86
NCH = TSEQ // TS
NB = 3 * TS
NTB = TSEQ // NB
TTOT = NSEQ * TSEQ
EPS = 1e-6


def cast_weights(fw, dst, src, rows, tr):
    nc = fw.nc
    for r0 in range(0, rows, 256):
        r1 = min(rows, r0 + 256)
        fw.dma(fw.pool, dst[r0:r1, :], src[r0:r1, :], writes=[tr], owner=tr)


def phase_in_proj(fw, load_tok, g_vec, w_bf, t_w, F, outT, t_out, dt_out=None):
    nc = fw.nc
    with contextlib.ExitStack() as st:
        old = fw.stack
        fw.stack = st
        xnT = fw.sb("xnT", [128, KC, TSEQ], BF16)
        t_xnT = [T("xnT%d" % i) for i in range(NCH)]
        gbc = fw.sb("gbc", [128, D], F32); t_gbc = T("gbc")
        ident = fw.sb("identb", [128, 128], BF16); t_ident = T("identb")
        identf = fw.sb("identf", [128, 128], F32); t_identf = T("identf")
        xt = [fw.sb("xt%d" % i, [128, D], F32) for i in range(2)]
        t_xt = [T("xt%d" % i) for i in range(2)]
        xs = [fw.sb("xs%d" % i, [128, D], BF16) for i in range(2)]
        t_xs = [T("xs%d" % i) for i in range(2)]
        junk = fw.sb("junk", [128, D], BF16); t_junk = T("junk")
        st_ = [fw.sb("stat%d" % i, [128, 4], F32) for i in range(2)]
        t_st = [T("stat%d" % i) for i in range(2)]
        WS = 512
        wsl = [fw.sb("wsl%d" % i, [128, KC, WS], BF16) for i in range(2)]
        t_wsl = [T("wsl%d" % i) for i in range(2)]
        ob = [fw.sb("ob%d" % i, [128, TSEQ], BF16) for i in range(2)]
        t_ob = [T("ob%d" % i) for i in range(2)]
        obf = fw.sb("obf", [128, TSEQ], F32); t_obf = T("obf")
        pst = [fw.ps("pst%d" % i, [128, 8, TS], BF16) for i in range(2)]
        t_pst = [T("pst%d" % i) for i in range(2)]
        pmm = [fw.ps("pmm%d" % i, [128, 512], F32) for i in range(4)]
        t_pmm = [T("pmm%d" % i) for i in range(4)]

        fw.dma(fw.sp, gbc[:], g_vec.partition_broadcast(128), writes=[t_gbc])
        fw.dma(fw.sp, identf[:], fw.ident_dram, writes=[t_identf])
        fw.op(fw.dve, lambda: nc.vector.tensor_copy(out=ident[:], in_=identf[:]), reads=[t_identf], writes=[t_ident])

        nslab = (F + WS - 1) // WS
        ev = 0
        ftc = 0
        for seq in range(NSEQ):
            for tt in range(NCH):
                s = tt % 2
                load_tok(seq, tt, xt[s], t_xt[s])
                fw.op(fw.act, lambda: nc.scalar.activation(out=junk[:TS, :], in_=xt[s][:TS, :], func=AF.Square,
                                                          accum_out=st_[s][:TS, 0:1]),
                      reads=[t_xt[s]], writes=[t_junk, t_st[s]])
                fw.op(fw.act, lambda: nc.scalar.activation(out=st_[s][:TS, 1:2], in_=st_[s][:TS, 0:1], func=AF.Ln,
                                                          scale=1.0 / D, bias=EPS),
                      reads=[t_st[s]], writes=[t_st[s]])
                fw.op(fw.act, lambda: nc.scalar.activation(out=st_[s][:TS, 2:3], in_=st_[s][:TS, 1:2], func=AF.Exp,
                                                          scale=-0.5),
                      reads=[t_st[s]], writes=[t_st[s]])
                fw.op(fw.dve, lambda: nc.vector.scalar_tensor_tensor(out=xs[s][:TS, :], in0=xt[s][:TS, :],
                                                                     scalar=st_[s][:TS, 2:3], in1=gbc[:TS, :],
                                                                     op0=ALU.mult, op1=ALU.mult),
                      reads=[t_xt[s], t_st[s], t_gbc], writes=[t_xs[s]])
                for half in range(2):
                    p = pst[half]
                    for j in range(8):
                        kc = half * 8 + j
                        fw.op(fw.pe, lambda: nc.tensor.transpose(p[:, j, :], xs[s][:TS, kc * 128:(kc + 1) * 128],
                                                                 ident[:TS, :TS]),
                              reads=[t_xs[s], t_ident], writes=[t_pst[half]], signal=(j == 7))
                    dst = xnT[:, half * 8:(half + 1) * 8, tt * TS:(tt + 1) * TS]
                    if half == 0:
                        fw.op(fw.act, lambda: nc.scalar.copy(out=dst, in_=p[:]), reads=[t_pst[half]], writes=[t_xnT[tt]])
                    else:
                        fw.op(fw.dve, lambda: nc.vector.tensor_copy(out=dst, in_=p[:]), reads=[t_pst[half]],
                              writes=[t_xnT[tt]])
            for sl in range(nslab):
                f0 = sl * WS
                fw_ = min(WS, F - f0)
                ws = sl % 2
                src = w_bf[:, f0:f0 + fw_].rearrange("(kc p) f -> p kc f", p=128)
                h = KC // 2
                fw.dma(fw.sp, wsl[ws][:, 0:h, 0:fw_], src[:, 0:h, :], reads=[t_w], writes=[t_wsl[ws]])
                fw.dma(fw.sp, wsl[ws][:, h:KC, 0:fw_], src[:, h:KC, :], reads=[t_w], writes=[t_wsl[ws]])
                for fi in range(0, fw_, 128):
                    m = min(128, fw_ - fi)
                    o = ftc % 2
                    ftc += 1
                    for tb in range(NTB):
                        pb = ev % 4
                        pm = pmm[pb]
                        for kc in range(KC):
                            fw.op(fw.pe, lambda: nc.tensor.matmul(pm[:m, 0:NB], lhsT=wsl[ws][:, kc, fi:fi + m],
                                                                  rhs=xnT[:, kc, tb * NB:(tb + 1) * NB],
                                                                  start=(kc == 0), stop=(kc == KC - 1)),
                                  reads=[t_wsl[ws]] + t_xnT[tb * 3:(tb + 1) * 3], writes=[t_pmm[pb]],
                                  signal=(kc == KC - 1))
                        dst = ob[o][:m, tb * NB:(tb + 1) * NB]
                        if dt_out is not None and f0 + fi >= dt_out[1]:
                            fw.op(fw.dve, lambda: nc.vector.tensor_copy(out=obf[:m, tb * NB:(tb + 1) * NB],
                                                                        in_=pm[:m, 0:NB]),
                                  reads=[t_pmm[pb]], writes=[t_obf])
                            fw.op(fw.dve, lambda: nc.vector.tensor_copy(out=dst, in_=obf[:m, tb * NB:(tb + 1) * NB]),
                                  reads=[t_obf], writes=[t_ob[o]])
                        elif ev % 2 == 0:
                            fw.op(fw.act, lambda: nc.scalar.copy(out=dst, in_=pm[:m, 0:NB]), reads=[t_pmm[pb]],
                                  writes=[t_ob[o]])
                        else:
                            fw.op(fw.dve, lambda: nc.vector.tensor_copy(out=dst, in_=pm[:m, 0:NB]), reads=[t_pmm[pb]],
                                  writes=[t_ob[o]])
                        ev += 1
                    fw.dma(fw.act, outT[f0 + fi:f0 + fi + m, seq * TSEQ:(seq + 1) * TSEQ], ob[o][:m, :],
                           reads=[t_ob[o]], writes=[t_out], owner=t_ob[o])
                    if dt_out is not None and f0 + fi >= dt_out[1]:
                        fw.dma(fw.act, dt_out[0][0:m, seq * TSEQ:(seq + 1) * TSEQ], obf[:m, :],
                               reads=[t_obf], writes=[t_out], owner=t_obf)
        fw.barrier()
        fw.stack = old


def phase_out_proj(fw, yT, t_y, w_bf, t_w, load_res, store, final_g=None):
    nc = fw.nc
    FI = 4096
    FC = FI // 128
    with contextlib.ExitStack() as st:
        old = fw.stack
        fw.stack = st
        wres = fw.sb("wres", [128, FC, D], BF16); t_wres = T("wres")
        yb = [fw.sb("yb%d" % i, [128, FC, NB], BF16) for i in range(2)]
        t_yb = [T("yb%d" % i) for i in range(2)]
        xt = [fw.sb("rt%d" % i, [128, D], F32) for i in range(2)]
        t_xt = [T("rt%d" % i) for i in range(2)]
        po = [fw.ps("po%d" % i, [128, 512], F32) for i in range(8)]
        t_po = [T("po%d" % i) for i in range(8)]
        if final_g is not None:
            gbc = fw.sb("gbcf", [128, D], F32); t_gbc = T("gbcf")
            junk = fw.sb("junkf", [128, D], BF16); t_junk = T("junkf")
            st_ = [fw.sb("statf%d" % i, [128, 4], F32) for i in range(2)]
            t_st = [T("statf%d" % i) for i in range(2)]
            fw.dma(fw.sp, gbc[:], final_g.partition_broadcast(128), writes=[t_gbc])
        src = w_bf.rearrange("(fc p) d -> p fc d", p=128)
        for q4 in range(4):
            fw.dma(fw.sp, wres[:, q4 * 8:(q4 + 1) * 8, :], src[:, q4 * 8:(q4 + 1) * 8, :], reads=[t_w], writes=[t_wres])
        ysrc = yT.rearrange("(fc p) t -> p fc t", p=128)
        pc = 0
        for seq in range(NSEQ):
            for tb in range(NTB):
                ys = (seq * NTB + tb) % 2
                tok0 = seq * TSEQ + tb * NB
                fw.dma(fw.sp, yb[ys][:, 0:FC // 2, :], ysrc[:, 0:FC // 2, tok0:tok0 + NB], reads=[t_y], writes=[t_yb[ys]])
                fw.dma(fw.sp, yb[ys][:, FC // 2:FC, :], ysrc[:, FC // 2:FC, tok0:tok0 + NB], reads=[t_y], writes=[t_yb[ys]])
                for j in range(3):
                    tt = tb * 3 + j
                    s = tt % 2
                    load_res(seq, tt, xt[s], t_xt[s])
                    for db in range(4):
                        pb = pc % 8
                        pc += 1
                        for fc in range(FC):
                            fw.op(fw.pe, lambda: nc.tensor.matmul(po[pb][:TS, :], lhsT=yb[ys][:, fc, j * TS:(j + 1) * TS],
                                                                  rhs=wres[:, fc, db * 512:(db + 1) * 512],
                                                                  start=(fc == 0), stop=(fc == FC - 1)),
                                  reads=[t_yb[ys], t_wres], writes=[t_po[pb]], signal=(fc == FC - 1))
                        fw.op(fw.dve, lambda: nc.vector.tensor_tensor(out=xt[s][:TS, db * 512:(db + 1) * 512],
                                                                      in0=po[pb][:TS, :],
                                                                      in1=xt[s][:TS, db * 512:(db + 1) * 512], op=ALU.add),
                              reads=[t_po[pb], t_xt[s]], writes=[t_xt[s]])
                    if final_g is not None:
                        fw.op(fw.act, lambda: nc.scalar.activation(out=junk[:TS, :], in_=xt[s][:TS, :], func=AF.Square,
                                                                  accum_out=st_[s][:TS, 0:1]),
                              reads=[t_xt[s]], writes=[t_junk, t_st[s]])
                        fw.op(fw.act, lambda: nc.scalar.activation(out=st_[s][:TS, 1:2], in_=st_[s][:TS, 0:1], func=AF.Ln,
                                                                  scale=1.0 / D, bias=EPS),
                              reads=[t_st[s]], writes=[t_st[s]])
                        fw.op(fw.act, lambda: nc.scalar.activation(out=st_[s][:TS, 2:3], in_=st_[s][:TS, 1:2], func=AF.Exp,
                                                                  scale=-0.5),
                              reads=[t_st[s]], writes=[t_st[s]])
                        fw.op(fw.dve, lambda: nc.vector.scalar_tensor_tensor(out=xt[s][:TS, :], in0=xt[s][:TS, :],
                                                                             scalar=st_[s][:TS, 2:3], in1=gbc[:TS, :],
                                                                             op0=ALU.mult, op1=ALU.mult),
                              reads=[t_xt[s], t_st[s], t_gbc], writes=[t_xt[s]])
                    store(seq, tt, xt[s], t_xt[s])
        fw.barrier()
        fw.stack = old


def host_consts():
    c = {}
    c["ident"] = np.eye(128, dtype=np.float32)
    s = np.arange(TS)
    c["tri"] = (s[:, None] <= s[None, :]).astype(np.float32)
    p = np.arange(128)
    c["mask8"] = (p[:, None] // 16 == np.arange(8)[None, :]).astype(np.float32)
    mc = np.zeros((128, 4, 128), np.float32)
    for kl in range(4):
        mc[:, kl, :] = ((p[None, :] // 16) == (2 * kl + p[:, None] // 64)).astype(np.float32)
    c["maskC"] = mc
    c["bd4"] = (p[:, None] // 4 == p[None, :] // 4).astype(np.float32)
    return c


CONST_SHAPES = {"ident": [128, 128], "tri": [TS, TS], "mask8": [128, 8], "maskC": [128, 4, 128], "bd4": [128, 128]}


def cmul_ops(fw, nc, out_r, out_i, ar, ai, br, bi, tmp, tr_out, tr_in, tr_tmp, neg_imag_b=False):
    t1, t2 = tmp
    V = nc.vector
    fw.op(fw.dve, lambda: V.tensor_tensor(out=t1, in0=ar, in1=br, op=ALU.mult), reads=tr_in, writes=tr_tmp)
    fw.op(fw.dve, lambda: V.tensor_tensor(out=t2, in0=ai, in1=bi, op=ALU.mult), reads=tr_in, writes=tr_tmp)
    fw.op(fw.dve, lambda: V.tensor_tensor(out=out_r, in0=t1, in1=t2, op=ALU.subtract), reads=tr_tmp, writes=tr_out)
    fw.op(fw.dve, lambda: V.tensor_tensor(out=t1, in0=ar, in1=bi, op=ALU.mult), reads=tr_in + tr_out, writes=tr_tmp)
    fw.op(fw.dve, lambda: V.tensor_tensor(out=t2, in0=ai, in1=br, op=ALU.mult), reads=tr_in, writes=tr_tmp)
    fw.op(fw.dve, lambda: V.tensor_tensor(out=out_i, in0=t1, in1=t2, op=ALU.add), reads=tr_tmp, writes=tr_out)


def phase_s5(fw, p0T, t_p0, prm, glu_bf, t_glu, y0T, t_y0):
    nc = fw.nc
    V, A, P = nc.vector, nc.scalar, nc.tensor
    L = TS
    with contextlib.ExitStack() as st:
        old = fw.stack
        fw.stack = st
        En_r = fw.sb("En_r", [128, 32, 128], F32)
        En_i = fw.sb("En_i", [128, 32, 128], F32)
        Ep_r = fw.sb("Ep_r", [128, 32, L], F32)
        Ep_i = fw.sb("Ep_i", [128, 32, L], F32)
        Bblk = fw.sb("Bblk", [128, 8, 1024], BF16)
        Cblk = fw.sb("Cblk", [128, 32, 2, 128], BF16)
        tri = fw.sb("trib", [128, L], BF16)
        glu = fw.sb("gluw", [128, 8, 1024], BF16)
        dvec = fw.sb("s5d", [128, 8], F32)
        hb = fw.sb("s5hb", [128, 8], F32)
        car_r = fw.sb("car_r", [128, 32], F32)
        car_i = fw.sb("car_i", [128, 32], F32)
        t_tab = T("s5tab")
        t_car = [T("car%d" % i) for i in range(8)]
        ps = [fw.ps("s5ps%d" % i, [128, 512], F32) for i in range(8)]
        t_ps = [T("s5ps%d" % i) for i in range(8)]

        with contextlib.ExitStack() as st2:
            fw.stack = st2
            t_s = T("s5setup")
            identf = fw.sb("identf", [128, 128], F32)
            trif = fw.sb("trif", [128, L], F32)
            mask8 = fw.sb("mask8", [128, 8], F32)
            maskC = fw.sb("maskC", [128, 4, 128], F32)
            lr = fw.sb("lr", [128, 32], F32); li = fw.sb("li", [128, 32], F32); ldt = fw.sb("ldt", [128, 32], F32)
            w = [fw.sb("s5w%d" % i, [128, 32], F32) for i in range(12)]
            Es_r = fw.sb("Es_r", [128, 32, L], F32); Es_i = fw.sb("Es_i", [128, 32, L], F32)
            tA = fw.sb("tA", [128, 32, 64], F32); tB = fw.sb("tB", [128, 32, 64], F32)
            Bp = [fw.sb("Bp%d" % i, [64, 64, 16], F32) for i in range(2)]
            Cn = [fw.sb("Cn%d" % i, [128, 8, 64], F32) for i in range(2)]
            glub = fw.sb("glub", [128, 8], F32)
            Cn2 = fw.sb("Cn2", [128, 2, 64], F32)
            S = [t_s]
            fw.dma(fw.sp, identf[:], fw.consts["ident"], writes=S)
            fw.dma(fw.sp, trif[:L, :], fw.consts["tri"], writes=S)
            fw.dma(fw.sp, mask8[:], fw.consts["mask8"], writes=S)
            fw.dma(fw.sp, maskC[:], fw.consts["maskC"], writes=S)
            fw.dma(fw.sp, lr[:], prm["s5_lambda_re"].rearrange("g p -> (g p)").rearrange("(k q) -> q k", q=128), writes=S,
                   allow_slow_non_contiguous=True)
            fw.dma(fw.sp, li[:], prm["s5_lambda_im"].rearrange("g p -> (g p)").rearrange("(k q) -> q k", q=128), writes=S,
                   allow_slow_non_contiguous=True)
            ldv = prm["s5_log_dt"].rearrange("(k gl) -> gl k", gl=2)
            for gl in range(2):
                fw.dma(fw.sp, ldt[gl * 64:(gl + 1) * 64, :], ldv[gl, :].partition_broadcast(64), writes=S,
                       allow_slow_non_contiguous=True)
            fw.dma(fw.sp, Bp[0][:], prm["s5_b_re"].rearrange("g p h -> p g h"), writes=S)
            fw.dma(fw.sp, Bp[1][:], prm["s5_b_im"].rearrange("g p h -> p g h"), writes=S)
            fw.dma(fw.sp, Cn[0][:], prm["s5_c_re"].rearrange("(ct gl) h p -> (gl h) ct p", gl=8), writes=S)
            fw.dma(fw.sp, Cn[1][:], prm["s5_c_im"].rearrange("(ct gl) h p -> (gl h) ct p", gl=8), writes=S)
            fw.dma(fw.sp, dvec[:], prm["s5_d"].rearrange("(ct c) -> c ct", c=128), writes=S, allow_slow_non_contiguous=True)
            fw.dma(fw.sp, glub[:], prm["s5_glu_b"].rearrange("(ct c) -> c ct", c=128), writes=S, allow_slow_non_contiguous=True)
            gsrc = glu_bf.rearrange("(ci p) co -> p ci co", p=128)
            fw.dma(fw.sp, glu[:], gsrc, reads=[t_glu], writes=[t_tab])
            o = lambda eng, fn: fw.op(eng, fn, reads=S, writes=S)
            o(fw.dve, lambda: V.tensor_copy(out=tri[:L, :], in_=trif[:L, :]))
            o(fw.dve, lambda: V.tensor_scalar(out=hb[:], in0=glub[:], scalar1=0.5, scalar2=None, op0=ALU.mult))
            dt_, lrd, lid, magp, magn, cs, sn, bpr, bpi, bnr, bni, tq = w
            o(fw.act, lambda: A.activation(out=dt_[:], in_=ldt[:], func=AF.Exp))
            o(fw.dve, lambda: V.scalar_tensor_tensor(out=lrd[:], in0=lr[:], scalar=1.0 / 16, in1=dt_[:], op0=ALU.mult, op1=ALU.mult))
            o(fw.dve, lambda: V.scalar_tensor_tensor(out=lid[:], in0=li[:], scalar=1.0 / 16, in1=dt_[:], op0=ALU.mult, op1=ALU.mult))
            o(fw.act, lambda: A.activation(out=magp[:], in_=lrd[:], func=AF.Exp))
            o(fw.act, lambda: A.activation(out=magn[:], in_=lrd[:], func=AF.Exp, scale=-1.0))
            o(fw.act, lambda: A.activation(out=sn[:], in_=lid[:], func=AF.Sin))
            o(fw.dve, lambda: V.tensor_scalar(out=tq[:], in0=lid[:], scalar1=float(np.pi / 2), scalar2=None, op0=ALU.add))
            o(fw.act, lambda: A.activation(out=cs[:], in_=tq[:], func=AF.Sin))
            o(fw.dve, lambda: V.tensor_tensor(out=bpr[:], in0=magp[:], in1=cs[:], op=ALU.mult))
            o(fw.dve, lambda: V.tensor_tensor(out=bpi[:], in0=magp[:], in1=sn[:], op=ALU.mult))
            o(fw.dve, lambda: V.tensor_tensor(out=bnr[:], in0=magn[:], in1=cs[:], op=ALU.mult))
            o(fw.dve, lambda: V.scalar_tensor_tensor(out=bni[:], in0=magn[:], scalar=-1.0, in1=sn[:], op0=ALU.mult, op1=ALU.mult))

            def csq(r, i, t1, t2):
                o(fw.dve, lambda: V.tensor_tensor(out=t1, in0=r, in1=r, op=ALU.mult))
                o(fw.dve, lambda: V.tensor_tensor(out=t2, in0=i, in1=i, op=ALU.mult))
                o(fw.dve, lambda: V.scalar_tensor_tensor(out=i, in0=r, scalar=2.0, in1=i, op0=ALU.mult, op1=ALU.mult))
                o(fw.dve, lambda: V.tensor_tensor(out=r, in0=t1, in1=t2, op=ALU.subtract))

            for _ in range(4):
                csq(bpr[:], bpi[:], magp[:], magn[:])
            for _ in range(4):
                csq(bnr[:], bni[:], magp[:], magn[:])
            am1, den, qr, qi = cs, sn, dt_, lrd
            o(fw.dve, lambda: V.tensor_scalar(out=am1[:], in0=bpr[:], scalar1=-1.0, scalar2=None, op0=ALU.add))
            o(fw.dve, lambda: V.tensor_tensor(out=magp[:], in0=lr[:], in1=lr[:], op=ALU.mult))
            o(fw.dve, lambda: V.tensor_tensor(out=magn[:], in0=li[:], in1=li[:], op=ALU.mult))
            o(fw.dve, lambda: V.tensor_tensor(out=den[:], in0=magp[:], in1=magn[:], op=ALU.add))
            o(fw.dve, lambda: V.reciprocal(out=den[:], in_=den[:]))
            o(fw.dve, lambda: V.tensor_tensor(out=magp[:], in0=am1[:], in1=lr[:], op=ALU.mult))
            o(fw.dve, lambda: V.tensor_tensor(out=magn[:], in0=bpi[:], in1=li[:], op=ALU.mult))
            o(fw.dve, lambda: V.tensor_tensor(out=qr[:], in0=magp[:], in1=magn[:], op=ALU.add))
            o(fw.dve, lambda: V.tensor_tensor(out=qr[:], in0=qr[:], in1=den[:], op=ALU.mult))
            o(fw.dve, lambda: V.tensor_tensor(out=magp[:], in0=bpi[:], in1=lr[:], op=ALU.mult))
            o(fw.dve, lambda: V.tensor_tensor(out=magn[:], in0=am1[:], in1=li[:], op=ALU.mult))
            o(fw.dve, lambda: V.tensor_tensor(out=qi[:], in0=magp[:], in1=magn[:], op=ALU.subtract))
            o(fw.dve, lambda: V.tensor_tensor(out=qi[:], in0=qi[:], in1=den[:], op=ALU.mult))

            def build_pow(Er, Ei, br, bi):
                o(fw.dve, lambda: V.tensor_copy(out=Er[:, :, 0], in_=br))
                o(fw.dve, lambda: V.tensor_copy(out=Ei[:, :, 0], in_=bi))
                n = 1
                while n < L:
                    m = min(n, L - n)
                    cb_r = br.unsqueeze(2).to_broadcast([128, 32, m])
                    cb_i = bi.unsqueeze(2).to_broadcast([128, 32, m])
                    cmul_ops(fw, nc, Er[:, :, n:n + m], Ei[:, :, n:n + m], Er[:, :, 0:m], Ei[:, :, 0:m], cb_r, cb_i,
                             (tA[:, :, 0:m], tB[:, :, 0:m]), S, S, S)
                    n += m
                    if n < L:
                        csq(br, bi, magp[:], magn[:])

            build_pow(Ep_r, Ep_i, bpr[:], bpi[:])
            build_pow(Es_r, Es_i, bnr[:], bni[:])
            qb_r = qr[:].unsqueeze(2).to_broadcast([128, 32, L])
            qb_i = qi[:].unsqueeze(2).to_broadcast([128, 32, L])
            for h0 in (0, 43):
                sl = slice(h0, h0 + 43)
                t1, t2 = tA[:, :, 0:43], tB[:, :, 0:43]
                qr_b = qr[:].unsqueeze(2).to_broadcast([128, 32, 43])
                qi_b = qi[:].unsqueeze(2).to_broadcast([128, 32, 43])
                o(fw.dve, lambda: V.tensor_tensor(out=t1, in0=Es_r[:, :, sl], in1=qi_b, op=ALU.mult))
                o(fw.dve, lambda: V.tensor_tensor(out=t2, in0=Es_i[:, :, sl], in1=qi_b, op=ALU.mult))
                o(fw.dve, lambda: V.tensor_tensor(out=Es_r[:, :, sl], in0=Es_r[:, :, sl], in1=qr_b, op=ALU.mult))
                o(fw.dve, lambda: V.tensor_tensor(out=Es_i[:, :, sl], in0=Es_i[:, :, sl], in1=qr_b, op=ALU.mult))
                o(fw.dve, lambda: V.tensor_tensor(out=Es_r[:, :, sl], in0=Es_r[:, :, sl], in1=t2, op=ALU.subtract))
                o(fw.dve, lambda: V.tensor_tensor(out=Es_i[:, :, sl], in0=Es_i[:, :, sl], in1=t1, op=ALU.add))
            for k in range(32):
                for ri, (src, dst) in enumerate(((Es_r, En_r), (Es_i, En_i))):
                    pb = ps[(2 * k + ri) % 8]
                    o(fw.pe, lambda: P.transpose(pb[:L, 0:128], src[:, k, :], identf[:, :]))
                    o(fw.act, lambda: A.copy(out=dst[:L, k, :], in_=pb[:L, 0:128]))
            for ct in range(8):
                pb = ps[ct % 8]
                for ri in range(2):
                    o(fw.pe, lambda: P.transpose(pb[:, ri * 64:(ri + 1) * 64],
                                                 Bp[ri][:, ct * 8:(ct + 1) * 8, :].rearrange("p g h -> p (g h)"),
                                                 identf[:64, :64]))
                src = pb[:, 0:128].rearrange("c (ri p) -> c ri p", ri=2).unsqueeze(2).to_broadcast([128, 2, 8, 64])
                msk = mask8[:].unsqueeze(1).unsqueeze(3).to_broadcast([128, 2, 8, 64])
                o(fw.dve, lambda: V.tensor_tensor(out=Bblk[:, ct, :].rearrange("c (ri g p) -> c ri g p", ri=2, g=8),
                                                  in0=src, in1=msk, op=ALU.mult))
            for ct in range(8):
                for ri in range(2):
                    pb = ps[(2 * ct + ri) % 8]
                    o(fw.dve, lambda: V.tensor_copy(out=Cn2[:], in_=Cn[ri][:, ct, :].unsqueeze(1).to_broadcast([128, 2, 64])))
                    o(fw.pe, lambda: P.transpose(pb[:, 0:128], Cn2[:].rearrange("c a p -> c (a p)"), identf[:, :]))
                    for kl in range(4):
                        k = ct * 4 + kl
                        o(fw.dve, lambda: V.scalar_tensor_tensor(out=Cblk[:, k, ri, :], in0=pb[:, 0:128],
                                                                 scalar=(1.0 if ri == 0 else -1.0), in1=maskC[:, kl, :],
                                                                 op0=ALU.mult, op1=ALU.mult))
            o(fw.dve, lambda: V.memset(car_r[:], 0.0))
            fw.op(fw.dve, lambda: V.memset(car_i[:], 0.0), reads=S, writes=S + [t_tab] + t_car)
            fw.barrier()
        fw.stack = st

        ub = [fw.sb("s5ub%d" % i, [128, 16, NB], BF16) for i in range(2)]
        t_ub = [T("s5ub%d" % i) for i in range(2)]
        tmp = [fw.sb("s5t%d" % i, [128, 512], F32) for i in range(4)]
        t_tmp = T("s5tmp")
        zin = [fw.sb("zin%d" % i, [128, 1024], BF16) for i in range(2)]
        t_zin = [T("zin%d" % i) for i in range(2)]
        zc = fw.sb("zc", [128, 2, 4, L], F32); t_zc = T("zc")
        rt = [fw.sb("s5rt%d" % i, [128, 4, L], F32) for i in range(4)]
        t_rt = T("s5rt")
        xb = [fw.sb("s5xb%d" % i, [128, 2, 4, L], BF16) for i in range(2)]
        t_xb = [T("s5xb%d" % i) for i in range(2)]
        yv = fw.sb("s5yv", [128, L], F32); t_yv = T("s5yv")
        e1 = fw.sb("s5e1", [128, L], F32); e2 = fw.sb("s5e2", [128, L], F32); t_e = T("s5e")
        g2 = [fw.sb("s5g2%d" % i, [128, 8, L], BF16) for i in range(2)]
        t_g2 = [T("s5g2%d" % i) for i in range(2)]
        ob = [fw.sb("s5ob%d" % i, [128, 8, NB], BF16) for i in range(2)]
        t_ob = [T("s5ob%d" % i) for i in range(2)]
        usrc = p0T[0:2048, :].rearrange("(j c) t -> c j t", c=128)
        ydst = y0T[0:1024, :].rearrange("(j c) t -> c j t", c=128)
        ci = 0
        for seq in range(NSEQ):
            if seq > 0:
                fw.op(fw.dve, lambda: V.memset(car_r[:], 0.0), writes=t_car)
                fw.op(fw.dve, lambda: V.memset(car_i[:], 0.0), writes=t_car)
            for tb in range(NTB):
                bs = (seq * NTB + tb) % 2
                tok0 = seq * TSEQ + tb * NB
                fw.dma(fw.sp, ub[bs][:], usrc[:, :, tok0:tok0 + NB], reads=[t_p0], writes=[t_ub[bs]])
                for j in range(3):
                    cs_ = slice(j * L, (j + 1) * L)
                    gs = ci % 2
                    ci += 1
                    for ct in range(8):
                        zs = ct % 2
                        pA, pB, pZr, pZi, pY = ps[0], ps[1], ps[2 + 2 * zs], ps[3 + 2 * zs], ps[6 + zs]
                        tA_, tB_, tZr, tZi, tY = t_ps[0], t_ps[1], t_ps[2 + 2 * zs], t_ps[3 + 2 * zs], t_ps[6 + zs]
                        u_ct = ub[bs][:, ct, cs_]
                        fw.op(fw.pe, lambda: P.matmul(pA[:L, :], lhsT=u_ct, rhs=Bblk[:, ct, 0:512], start=True, stop=True),
                              reads=[t_ub[bs], t_tab], writes=[tA_])
                        fw.op(fw.pe, lambda: P.matmul(pB[:L, :], lhsT=u_ct, rhs=Bblk[:, ct, 512:1024], start=True, stop=True),
                              reads=[t_ub[bs], t_tab], writes=[tB_])
                        enr = En_r[:L, ct * 4:(ct + 1) * 4, :].rearrange("s k q -> s (k q)")
                        eni = En_i[:L, ct * 4:(ct + 1) * 4, :].rearrange("s k q -> s (k q)")
                        cmul_ops(fw, nc, zin[zs][:L, 0:512], zin[zs][:L, 512:1024], pA[:L, :], pB[:L, :], enr, eni,
                                 (tmp[0][:L, :], tmp[1][:L, :]), [t_zin[zs]], [tA_, tB_, t_tab], [t_tmp])
                        for cb in range(8):
                            pz = pZr if cb < 4 else pZi
                            tz = tZr if cb < 4 else tZi
                            fw.op(fw.pe, lambda: P.matmul(pz[:, (cb % 4) * L:(cb % 4 + 1) * L],
                                                          lhsT=zin[zs][:L, cb * 128:(cb + 1) * 128], rhs=tri[:L, :],
                                                          start=True, stop=True),
                                  reads=[t_zin[zs], t_tab], writes=[tz], signal=(cb % 4 == 3))
                        k4 = slice(ct * 4, ct * 4 + 4)
                        fw.op(fw.dve, lambda: V.tensor_tensor(out=zc[:, 0, :, :],
                                                              in0=pZr[:, 0:4 * L].rearrange("q (k t) -> q k t", k=4),
                                                              in1=car_r[:, k4].unsqueeze(2).to_broadcast([128, 4, L]),
                                                              op=ALU.add),
                              reads=[tZr, t_car[ct]], writes=[t_zc])
                        fw.op(fw.dve, lambda: V.tensor_tensor(out=zc[:, 1, :, :],
                                                              in0=pZi[:, 0:4 * L].rearrange("q (k t) -> q k t", k=4),
                                                              in1=car_i[:, k4].unsqueeze(2).to_broadcast([128, 4, L]),
                                                              op=ALU.add),
                              reads=[tZi, t_car[ct]], writes=[t_zc])
                        xs_ = ct % 2
                        epr, epi = Ep_r[:, k4, :], Ep_i[:, k4, :]
                        RT = [t_rt]
                        fw.op(fw.dve, lambda: V.tensor_tensor(out=rt[0][:], in0=zc[:, 0], in1=epr, op=ALU.mult), reads=[t_zc, t_tab], writes=RT)
                        fw.op(fw.dve, lambda: V.tensor_tensor(out=rt[1][:], in0=zc[:, 1], in1=epi, op=ALU.mult), reads=[t_zc, t_tab], writes=RT)
                        fw.op(fw.dve, lambda: V.tensor_tensor(out=rt[2][:], in0=zc[:, 0], in1=epi, op=ALU.mult), reads=[t_zc, t_tab], writes=RT)
                        fw.op(fw.dve, lambda: V.tensor_tensor(out=rt[3][:], in0=zc[:, 1], in1=epr, op=ALU.mult), reads=[t_zc, t_tab], writes=RT)
                        fw.op(fw.dve, lambda: V.tensor_tensor(out=xb[xs_][:, 0], in0=rt[0][:], in1=rt[1][:], op=ALU.subtract), reads=RT, writes=[t_xb[xs_]])
                        fw.op(fw.dve, lambda: V.tensor_tensor(out=xb[xs_][:, 1], in0=rt[2][:], in1=rt[3][:], op=ALU.add), reads=RT, writes=[t_xb[xs_]])
                        fw.op(fw.dve, lambda: V.tensor_tensor(out=car_r[:, k4], in0=rt[0][:, :, L - 1], in1=rt[1][:, :, L - 1], op=ALU.subtract), reads=RT, writes=[t_car[ct]])
                        fw.op(fw.dve, lambda: V.tensor_tensor(out=car_i[:, k4], in0=rt[2][:, :, L - 1], in1=rt[3][:, :, L - 1], op=ALU.add), reads=RT, writes=[t_car[ct]])
                        for kl in range(4):
                            for ri in range(2):
                                fw.op(fw.pe, lambda: P.matmul(pY[:, 0:L], lhsT=Cblk[:, ct * 4 + kl, ri, :], rhs=xb[xs_][:, ri, kl, :],
                                                              start=(kl == 0 and ri == 0), stop=(kl == 3 and ri == 1)),
                                      reads=[t_xb[xs_], t_tab], writes=[tY], signal=(kl == 3 and ri == 1))
                        fw.op(fw.dve, lambda: V.scalar_tensor_tensor(out=yv[:], in0=u_ct, scalar=dvec[:, ct:ct + 1], in1=pY[:, 0:L],
                                                                     op0=ALU.mult, op1=ALU.add),
                              reads=[t_ub[bs], tY, t_tab], writes=[t_yv])
                        fw.op(fw.act, lambda: A.activation(out=e1[:], in_=yv[:], func=AF.Square), reads=[t_yv], writes=[t_e])
                        fw.op(fw.dve, lambda: V.tensor_scalar(out=e1[:], in0=e1[:], scalar1=0.044715, scalar2=1.0, op0=ALU.mult, op1=ALU.add),
                              reads=[t_e], writes=[t_e])
                        fw.op(fw.dve, lambda: V.tensor_tensor(out=e1[:], in0=e1[:], in1=yv[:], op=ALU.mult), reads=[t_e, t_yv], writes=[t_e])
                        fw.op(fw.act, lambda: A.activation(out=e2[:], in_=e1[:], func=AF.Tanh, scale=0.7978845608), reads=[t_e], writes=[t_e])
                        fw.op(fw.dve, lambda: V.scalar_tensor_tensor(out=g2[gs][:, ct, :], in0=e2[:], scalar=1.0, in1=yv[:],
                                                                     op0=ALU.add, op1=ALU.mult),
                              reads=[t_e, t_yv], writes=[t_g2[gs]])
                    for co in range(8):
                        pg = ps[6 + co % 2]
                        tg = t_ps[6 + co % 2]
                        for ci_ in range(8):
                            fw.op(fw.pe, lambda: P.matmul(pg[:, 0:L], lhsT=glu[:, ci_, co * 128:(co + 1) * 128], rhs=g2[gs][:, ci_, :],
                                                          start=(ci_ == 0), stop=(ci_ == 7)),
                                  reads=[t_g2[gs], t_tab], writes=[tg], signal=(ci_ == 7))
                        fw.op(fw.act, lambda: A.activation(out=e1[:], in_=pg[:, 0:L], func=AF.Tanh, scale=0.25, bias=hb[:, co:co + 1]),
                              reads=[tg, t_tab], writes=[t_e])
                        fw.op(fw.act, lambda: A.activation(out=e2[:], in_=ub[bs][:, 8 + co, cs_], func=AF.Tanh, scale=0.5),
                              reads=[t_ub[bs]], writes=[t_e])
                        fw.op(fw.dve, lambda: V.scalar_tensor_tensor(out=e1[:], in0=e1[:], scalar=1.0, in1=g2[gs][:, co, :],
                                                                     op0=ALU.add, op1=ALU.mult), reads=[t_e, t_g2[gs]], writes=[t_e])
                        fw.op(fw.dve, lambda: V.scalar_tensor_tensor(out=e2[:], in0=e2[:], scalar=1.0, in1=ub[bs][:, 8 + co, cs_],
                                                                     op0=ALU.add, op1=ALU.mult), reads=[t_e, t_ub[bs]], writes=[t_e])
                        fw.op(fw.dve, lambda: V.scalar_tensor_tensor(out=ob[bs][:, co, cs_], in0=e1[:], scalar=0.125, in1=e2[:],
                                                                     op0=ALU.mult, op1=ALU.mult), reads=[t_e], writes=[t_ob[bs]])
                fw.dma(fw.act, ydst[:, :, tok0:tok0 + NB], ob[bs][:], reads=[t_ob[bs]], writes=[t_y0], owner=t_ob[bs])
        fw.barrier()
        fw.stack = old


NH = 8
DH = 384
DBG = {}


class StopEmit(Exception):
    pass


def stage(n):
    if DBG.get("stage", 999) <= n:
        raise StopEmit()
HEAD_EPS = 1e-5


def phase_mlstm(fw, p0T, t_p0, prm, y0T, t_y0):
    nc = fw.nc
    V, A, P = nc.vector, nc.scalar, nc.tensor
    L = TS
    NT = 24
    with contextlib.ExitStack() as st:
        old = fw.stack
        fw.stack = st
        Wq = fw.sb("Wq", [128, NT, 128], BF16)
        Wk = fw.sb("Wk", [128, NT, 128], BF16)
        Wv = fw.sb("Wv", [128, NT, 128], BF16)
        Gqk = fw.sb("Gqk", [128, NT, 16], BF16)
        Gv = fw.sb("Gv", [128, NT, 16], BF16)
        cw = fw.sb("cw", [128, 4, NT], F32)
        cb = fw.sb("cbias", [128, NT], F32)
        nw = fw.sb("nw", [128, NT], F32)
        sk = fw.sb("skp", [128, NT], F32)
        bg = fw.sb("bg", [1, 16], BF16)
        ones_b = fw.sb("ones_b", [128, 128], BF16)
        ones_f = fw.sb("ones_f", [128, 128], F32)
        identb = fw.sb("identb", [128, 128], BF16)
        identf = fw.sb("identf", [128, 128], F32)
        trif = fw.sb("trif", [128, L], F32)
        Cst = fw.sb("Cst", [128, NH, 3, DH + 1], F32)
        Cbf = fw.sb("Cbf", [128, NH, 3, DH + 1], BF16)
        m_b = fw.sb("m_b", [128, NH], F32)
        m_c = fw.sb("m_c", [NH, 1], F32)
        t_tab = T("mltab")
        t_C = [T("C%d" % h) for h in range(NH)]
        t_Cb = [T("Cb%d" % h) for h in range(NH)]
        t_m = T("m")
        ps = [fw.ps("mlps%d" % i, [128, 512], F32) for i in range(7)]
        t_ps = [T("mlps%d" % i) for i in range(7)]
        pTb = fw.ps("mlpT", [128, 1024], BF16)
        t_pTb = [T("mlpT%d" % i) for i in range(3)]
        t_psk = T("mlpsk")

        with contextlib.ExitStack() as st2:
            fw.stack = st2
            S = [T("mlsetup")]
            o = lambda eng, fn: fw.op(eng, fn, reads=S, writes=S)
            bd4 = fw.sb("bd4", [128, 128], F32)
            wexp = [fw.sb("wexp%d" % i, [128, NT, 4], F32) for i in range(3)]
            Wraw = fw.sb("Wraw", [128, 128], BF16)
            wgf = fw.sb("wgf", [128, 3, NT, 16], F32)
            wgb = fw.sb("wgb", [128, 3, NT, 16], BF16)
            WT = [fw.sb("WT%d" % i, [128, 128], BF16) for i in range(3)]
            bgf = fw.sb("bgf", [1, 16], F32)
            skf = fw.sb("skf", [128, NT], F32)
            fw.dma(fw.sp, bd4[:], fw.consts["bd4"], writes=S)
            fw.dma(fw.sp, identf[:], fw.consts["ident"], writes=S)
            fw.dma(fw.sp, trif[:L, :], fw.consts["tri"], writes=S)
            for i, nm in enumerate(("ml_wq", "ml_wk", "ml_wv")):
                fw.dma(fw.sp, wexp[i][:], prm[nm].rearrange("n i o -> (n i) o").rearrange("(t p) o -> p t o", p=128), writes=S)
            fw.dma(fw.sp, wgf[:], prm["ml_w_gate"].rearrange("(a t p) g -> p a t g", a=3, p=128), writes=S)
            for k in range(4):
                fw.dma(fw.sp, cw[:, k, :], prm["ml_conv_w"][k, :].rearrange("(t p) -> p t", p=128), writes=S, allow_slow_non_contiguous=True)
            fw.dma(fw.sp, cb[:], prm["ml_conv_b"].rearrange("(t p) -> p t", p=128), writes=S, allow_slow_non_contiguous=True)
            fw.dma(fw.sp, nw[:], prm["ml_norm"].rearrange("(t p) -> p t", p=128), writes=S, allow_slow_non_contiguous=True)
            fw.dma(fw.sp, skf[:], prm["ml_skip"].rearrange("(t p) -> p t", p=128), writes=S, allow_slow_non_contiguous=True)
            fw.dma(fw.sp, bgf[:], prm["ml_b_gate"].rearrange("(a g) -> a g", a=1), writes=S)
            o(fw.dve, lambda: V.tensor_copy(out=bg[:], in_=bgf[:]))
            o(fw.dve, lambda: V.tensor_scalar(out=sk[:], in0=skf[:], scalar1=0.5, scalar2=None, op0=ALU.mult))
            o(fw.dve, lambda: V.tensor_copy(out=wgb[:], in_=wgf[:]))
            o(fw.dve, lambda: V.tensor_copy(out=identb[:], in_=identf[:]))
            o(fw.dve, lambda: V.memset(ones_b[:], 1.0))
            o(fw.dve, lambda: V.memset(ones_f[:], 1.0))
            bdv = bd4[:].rearrange("p (n o) -> p n o", o=4)
            scl = (0.5 * DH ** -0.5, 0.5, 1.0)
            for t in range(NT):
                for i, Wd in enumerate((Wq, Wk, Wv)):
                    o(fw.dve, lambda: V.scalar_tensor_tensor(out=Wd[:, t, :].rearrange("p (n o) -> p n o", o=4),
                                                             in0=wexp[i][:, t, :].unsqueeze(1).to_broadcast([128, 32, 4]),
                                                             scalar=scl[i], in1=bdv, op0=ALU.mult, op1=ALU.mult))
                    o(fw.dve, lambda: V.tensor_tensor(out=Wraw[:].rearrange("p (n o) -> p n o", o=4),
                                                      in0=wexp[i][:, t, :].unsqueeze(1).to_broadcast([128, 32, 4]),
                                                      in1=bdv, op=ALU.mult))
                    o(fw.pe, lambda: P.transpose(pTb[:, 0:128], Wraw[:], identb[:]))
                    o(fw.act, lambda: A.copy(out=WT[i][:], in_=pTb[:, 0:128]))
                pq = ps[t % 4]
                o(fw.pe, lambda: P.matmul(pq[:, 0:16], lhsT=WT[0][:], rhs=wgb[:, 0, t, :], start=True, stop=False))
                o(fw.pe, lambda: P.matmul(pq[:, 0:16], lhsT=WT[1][:], rhs=wgb[:, 1, t, :], start=False, stop=True))
                o(fw.pe, lambda: P.matmul(pq[:, 16:32], lhsT=WT[2][:], rhs=wgb[:, 2, t, :], start=True, stop=True))
                o(fw.act, lambda: A.mul(out=Gqk[:, t, :], in_=pq[:, 0:16], mul=0.5))
                o(fw.act, lambda: A.copy(out=Gv[:, t, :], in_=pq[:, 16:32]))
            fw.op(fw.dve, lambda: V.memset(m_b[:], 0.0), reads=S, writes=S + [t_tab, t_m])
            fw.barrier()
        fw.stack = st

        xblk = [fw.sb("xblk%d" % i, [128, NT, NB + 3], BF16) for i in range(2)]
        zblk = [fw.sb("zblk%d" % i, [128, NT, NB], BF16) for i in range(2)]
        t_xblk = [T("xblk%d" % i) for i in range(2)]
        t_zblk = [T("zblk%d" % i) for i in range(2)]
        accw = fw.sb("caccw", [128, 12, L], F32); t_acc = T("cacc")
        tmpw = fw.sb("ctmpw", [128, 12, L], F32); t_th = T("cth")
        xc2 = fw.sb("xc2", [128, NT, L], BF16); t_xc2 = T("xc2")
        qT = fw.sb("qT", [128, NT, L], BF16); t_qT = T("qT")
        kT = fw.sb("kT", [128, NT, L], BF16); t_kT = T("kT")
        ktok = fw.sb("ktok", [128, NH, DH], BF16); t_ktok = [T("ktok%d" % h) for h in range(NH)]
        vext = fw.sb("vext", [128, NH, DH + 1], BF16); t_vext = [T("vext%d" % h) for h in range(NH)]
        gts = fw.sb("gts", [128, 64], F32); t_g = T("gts")
        g8 = fw.sb("g8", [8, 256], F32); t_g8 = T("g8")
        Mcb = fw.sb("Mcb", [128, 8], F32); dlt = fw.sb("dlt", [128, 8], F32); t_Mc = T("Mcb")
        Sp = [fw.sb("Sp%d" % i, [128, L], BF16) for i in range(2)]; t_Sp = [T("Sp%d" % i) for i in range(2)]
        hh = fw.sb("hh", [128, DH], F32); t_hh = T("hh")
        hst = fw.sb("hst", [128, 16], F32); t_hst = T("hst")
        hn = [fw.sb("hn%d" % i, [128, DH], BF16) for i in range(2)]; t_hn = [T("hn%d" % i) for i in range(2)]
        hjunk = fw.sb("hjunk", [128, DH], BF16); t_hjunk = T("hjunk")
        e1 = fw.sb("mle1", [128, 3, L], F32); e2 = fw.sb("mle2", [128, 3, L], F32); e3 = fw.sb("mle3", [128, 3, L], F32)
        t_e = T("mle")
        ob = [fw.sb("mlob%d" % i, [128, NT, NB], BF16) for i in range(2)]; t_ob = [T("mlob%d" % i) for i in range(2)]
        fw.op(fw.dve, lambda: V.memset(vext[:, :, DH:DH + 1], 1.0), writes=t_vext)
        xsrc = p0T[2048:5120, :].rearrange("(j c) t -> c j t", c=128)
        zsrc = p0T[5120:8192, :].rearrange("(j c) t -> c j t", c=128)
        ydst = y0T[1024:4096, :].rearrange("(j c) t -> c j t", c=128)
        try:
            for seq in range(NSEQ):
                if DBG.get("stage", 999) < 999 and seq > 0:
                    break
                fw.op(fw.dve, lambda: V.memset(Cst[:], 0.0), writes=t_C)
                fw.op(fw.dve, lambda: V.memset(m_b[:], 0.0), writes=[t_m])
                fw.op(fw.dve, lambda: V.memset(m_c[:], 0.0), writes=[t_m])
                for tb in range(NTB):
                    if seq * NTB + tb >= DBG.get("ml_max_blocks", 99):
                        break
                    bs = (seq * NTB + tb) % 2
                    tok0 = seq * TSEQ + tb * NB
                    if tb == 0:
                        fw.op(fw.dve, lambda: V.memset(xblk[bs][:, :, 0:3], 0.0), writes=[t_xblk[bs]])
                        fw.dma(fw.sp, xblk[bs][:, :, 3:NB + 3], xsrc[:, :, tok0:tok0 + NB], reads=[t_p0], writes=[t_xblk[bs]])
                    else:
                        fw.dma(fw.sp, xblk[bs][:, :, :], xsrc[:, :, tok0 - 3:tok0 + NB], reads=[t_p0], writes=[t_xblk[bs]])
                    fw.dma(fw.sp, zblk[bs][:], zsrc[:, :, tok0:tok0 + NB], reads=[t_p0], writes=[t_zblk[bs]])
                    for j in range(3):
                        c0 = j * L
                        XB, ZB = xblk[bs], zblk[bs]
                        tXB, tZB = t_xblk[bs], t_zblk[bs]
                        for hf in range(2):
                            ts_ = slice(hf * 12, hf * 12 + 12)
                            bw = lambda k: cw[:, k, ts_].unsqueeze(2).to_broadcast([128, 12, L])
                            fw.op(fw.dve, lambda: V.tensor_tensor(out=accw[:], in0=XB[:, ts_, c0:c0 + L], in1=bw(0), op=ALU.mult),
                                  reads=[tXB, t_tab], writes=[t_acc])
                            for k in range(1, 4):
                                fw.op(fw.dve, lambda: V.tensor_tensor(out=tmpw[:], in0=XB[:, ts_, c0 + k:c0 + k + L], in1=bw(k), op=ALU.mult),
                                      reads=[tXB, t_tab], writes=[t_th])
                                fw.op(fw.dve, lambda: V.tensor_tensor(out=accw[:], in0=accw[:], in1=tmpw[:], op=ALU.add),
                                      reads=[t_th, t_acc], writes=[t_acc])
                            fw.op(fw.dve, lambda: V.tensor_tensor(out=accw[:], in0=accw[:], in1=cb[:, ts_].unsqueeze(2).to_broadcast([128, 12, L]), op=ALU.add),
                                  reads=[t_acc, t_tab], writes=[t_acc])
                            fw.op(fw.act, lambda: A.activation(out=tmpw[:], in_=accw[:], func=AF.Tanh, scale=0.5), reads=[t_acc], writes=[t_th])
                            fw.op(fw.dve, lambda: V.scalar_tensor_tensor(out=xc2[:, ts_, :], in0=tmpw[:], scalar=1.0, in1=accw[:], op0=ALU.add, op1=ALU.mult),
                                  reads=[t_th, t_acc], writes=[t_xc2])
                        try_stage = stage(1)
                        for t4 in range(NT // 4):
                            for i4 in range(4):
                                t = 4 * t4 + i4
                                fw.op(fw.pe, lambda: P.matmul(ps[0][:, i4 * L:(i4 + 1) * L], lhsT=Wq[:, t, :], rhs=xc2[:, t, :], start=True, stop=True),
                                      reads=[t_xc2, t_tab], writes=[t_ps[0]], signal=(i4 == 3))
                            fw.op(fw.act, lambda: A.copy(out=qT[:, 4 * t4:4 * t4 + 4, :], in_=ps[0][:, 0:4 * L].rearrange("p (a t) -> p a t", a=4)),
                                  reads=[t_ps[0]], writes=[t_qT])
                            for i4 in range(4):
                                t = 4 * t4 + i4
                                fw.op(fw.pe, lambda: P.matmul(ps[1][:, i4 * L:(i4 + 1) * L], lhsT=Wk[:, t, :], rhs=xc2[:, t, :], start=True, stop=True),
                                      reads=[t_xc2, t_tab], writes=[t_ps[1]], signal=(i4 == 3))
                            fw.op(fw.dve, lambda: V.tensor_copy(out=kT[:, 4 * t4:4 * t4 + 4, :], in_=ps[1][:, 0:4 * L].rearrange("p (a t) -> p a t", a=4)),
                                  reads=[t_ps[1]], writes=[t_kT])
                        try_stage = stage(2)
                        pg = ps[3]
                        tg = t_ps[3]
                        for t in range(NT):
                            fw.op(fw.pe, lambda: P.matmul(pg[:L, 0:16], lhsT=xc2[:, t, :], rhs=Gqk[:, t, :], start=(t == 0), stop=False),
                                  reads=[t_xc2, t_tab], writes=[tg], signal=False)
                            fw.op(fw.pe, lambda: P.matmul(pg[:L, 0:16], lhsT=XB[:, t, c0 + 3:c0 + 3 + L], rhs=Gv[:, t, :], start=False, stop=False),
                                  reads=[tXB, t_tab], writes=[tg], signal=False)
                        fw.op(fw.pe, lambda: P.matmul(pg[:L, 0:16], lhsT=ones_b[0:1, :L], rhs=bg[0:1, :], start=False, stop=True),
                              reads=[t_tab], writes=[tg])
                        try_stage = stage(3)
                        G = [t_g]
                        fw.op(fw.act, lambda: A.activation(out=gts[:L, 8:16], in_=pg[:L, 8:16], func=AF.Exp, scale=-1.0), reads=[tg], writes=G)
                        fw.op(fw.act, lambda: A.activation(out=gts[:L, 8:16], in_=gts[:L, 8:16], func=AF.Ln, bias=1.0), reads=G, writes=G)
                        fw.op(fw.dve, lambda: V.tensor_scalar(out=gts[:L, 8:16], in0=gts[:L, 8:16], scalar1=-1.0, scalar2=None, op0=ALU.mult), reads=G, writes=G)
                        fw.op(fw.dve, lambda: V.tensor_copy(out=gts[:L, 0:8], in_=pg[:L, 0:8]), reads=[tg], writes=G)
                        try_stage = stage(4)
                        fw.op(fw.pe, lambda: P.matmul(pg[:L, 16:24], lhsT=trif[:L, :L], rhs=gts[:L, 8:16], start=True, stop=True), reads=G + [t_tab], writes=[tg])
                        fw.op(fw.pe, lambda: P.matmul(pg[:, 24:32], lhsT=ones_f[:L, :], rhs=gts[:L, 8:16], start=True, stop=True), reads=G + [t_tab], writes=[tg])
                        fw.op(fw.pe, lambda: P.matmul(pg[0:8, 32:33], lhsT=gts[:L, 8:16], rhs=ones_f[:L, 0:1], start=True, stop=True), reads=G + [t_tab], writes=[tg])
                        fw.op(fw.dve, lambda: V.tensor_copy(out=gts[:L, 40:48], in_=pg[:L, 16:24]), reads=[tg], writes=G)
                        fw.op(fw.dve, lambda: V.tensor_tensor(out=gts[:L, 16:24], in0=gts[:L, 0:8], in1=gts[:L, 40:48], op=ALU.subtract), reads=G, writes=G)
                        try_stage = stage(5)
                        fw.op(fw.pe, lambda: P.transpose(pg[0:8, 64:64 + L], gts[:L, 16:24], identf[:L, :L]), reads=G + [t_tab], writes=[tg])
                        G8 = [t_g8]
                        fw.op(fw.dve, lambda: V.reduce_max(out=g8[:, 0:1], in_=pg[0:8, 64:64 + L], axis=AX.X), reads=[tg], writes=G8)
                        fw.op(fw.dve, lambda: V.tensor_tensor(out=g8[:, 1:2], in0=g8[:, 0:1], in1=m_c[:, 0:1], op=ALU.max), reads=G8 + [t_m], writes=G8)
                        fw.op(fw.dve, lambda: V.tensor_scalar(out=g8[:, 8:16], in0=identf[0:8, 0:8], scalar1=g8[:, 1:2], scalar2=None, op0=ALU.mult), reads=G8 + [t_tab], writes=G8)
                        fw.op(fw.pe, lambda: P.matmul(pg[:, 40:48], lhsT=ones_f[0:8, :], rhs=g8[:, 8:16], start=True, stop=True), reads=G8 + [t_tab], writes=[tg])
                        try_stage = stage(6)
                        MC = [t_Mc]
                        fw.op(fw.dve, lambda: V.tensor_copy(out=Mcb[:], in_=pg[:, 40:48]), reads=[tg], writes=MC)
                        fw.op(fw.dve, lambda: V.tensor_tensor(out=dlt[:], in0=m_b[:], in1=Mcb[:], op=ALU.subtract), reads=MC + [t_m], writes=MC)
                        fw.op(fw.act, lambda: A.activation(out=dlt[:], in_=dlt[:], func=AF.Exp), reads=MC, writes=MC)
                        fw.op(fw.dve, lambda: V.tensor_tensor(out=m_b[:], in0=pg[:, 24:32], in1=Mcb[:], op=ALU.add), reads=MC + [tg], writes=[t_m])
                        fw.op(fw.dve, lambda: V.tensor_tensor(out=m_c[:, 0:1], in0=pg[0:8, 32:33], in1=g8[:, 1:2], op=ALU.add), reads=G8 + [tg], writes=[t_m])
                        fw.op(fw.dve, lambda: V.tensor_tensor(out=gts[:L, 24:32], in0=gts[:L, 16:24], in1=Mcb[:L, :], op=ALU.subtract), reads=G + MC, writes=G)
                        fw.op(fw.act, lambda: A.activation(out=gts[:L, 24:32], in_=gts[:L, 24:32], func=AF.Exp), reads=G, writes=G)
                        fw.op(fw.dve, lambda: V.tensor_tensor(out=gts[:L, 32:40], in0=gts[:L, 40:48], in1=Mcb[:L, :], op=ALU.add), reads=G + MC, writes=G)
                        fw.op(fw.act, lambda: A.activation(out=gts[:L, 32:40], in_=gts[:L, 32:40], func=AF.Exp, scale=-1.0), reads=G, writes=G)
                        try_stage = stage(7)
                        for h in range(NH):
                            pkv = ps[1]
                            tkv = t_ps[1]
                            for jj in range(3):
                                t = 3 * h + jj
                                fw.op(fw.pe, lambda: P.matmul(pkv[:L, jj * 128:(jj + 1) * 128], lhsT=xc2[:, t, :], rhs=Wk[:, t, :], start=True, stop=True),
                                      reads=[t_xc2, t_tab], writes=[tkv], signal=(jj == 2))
                            fw.op(fw.act, lambda: A.activation(out=ktok[:L, h, :], in_=pkv[:L, 0:DH], func=AF.Copy, scale=gts[:L, 24 + h:25 + h]),
                                  reads=[tkv] + G, writes=[t_ktok[h]])
                            pv = ps[2]
                            tv = t_ps[2]
                            for jj in range(3):
                                t = 3 * h + jj
                                fw.op(fw.pe, lambda: P.matmul(pv[:L, jj * 128:(jj + 1) * 128], lhsT=XB[:, t, c0 + 3:c0 + 3 + L], rhs=Wv[:, t, :], start=True, stop=True),
                                      reads=[tXB, t_tab], writes=[tv], signal=(jj == 2))
                            fw.op(fw.dve, lambda: V.tensor_copy(out=vext[:L, h, 0:DH], in_=pv[:L, 0:DH]), reads=[tv], writes=[t_vext[h]])
                            try_stage = stage(8)
                            pS = ps[3]
                            tS = t_ps[3]
                            for jj in range(3):
                                t = 3 * h + jj
                                fw.op(fw.pe, lambda: P.matmul(pS[:L, 0:L], lhsT=kT[:, t, :], rhs=qT[:, t, :], start=(jj == 0), stop=(jj == 2)),
                                      reads=[t_kT, t_qT], writes=[tS], signal=(jj == 2))
                            sp = h % 2
                            fw.op(fw.dve, lambda: V.scalar_tensor_tensor(out=Sp[sp][:L, :], in0=pS[:L, 0:L], scalar=gts[:L, 24 + h:25 + h],
                                                                         in1=trif[:L, :], op0=ALU.mult, op1=ALU.mult),
                                  reads=[tS, t_tab] + G, writes=[t_Sp[sp]])
                            try_stage = stage(9)
                            for jj in range(3):
                                fw.op(fw.act, lambda: A.activation(out=Cbf[:, h, jj, :], in_=Cst[:, h, jj, :], func=AF.Copy, scale=dlt[:, h:h + 1]),
                                      reads=[t_C[h]] + MC, writes=[t_Cb[h]])
                            pX = ps[4]
                            tX = t_ps[4]
                            fw.op(fw.pe, lambda: P.matmul(pX[:L, 0:DH + 1], lhsT=Sp[sp][:L, :], rhs=vext[:L, h, :], start=True, stop=False),
                                  reads=[t_Sp[sp], t_vext[h]], writes=[tX], signal=False)
                            for jj in range(3):
                                fw.op(fw.pe, lambda: P.matmul(pX[:L, 0:DH + 1], lhsT=qT[:, 3 * h + jj, :], rhs=Cbf[:, h, jj, :], start=False, stop=(jj == 2)),
                                      reads=[t_qT, t_Cb[h]], writes=[tX], signal=(jj == 2))
                            try_stage = stage(10)
                            for jj in range(3):
                                pc = ps[5 + jj % 2]
                                tc = t_ps[5 + jj % 2]
                                fw.op(fw.pe, lambda: P.matmul(pc[:, 0:DH + 1], lhsT=ktok[:L, h, jj * 128:(jj + 1) * 128], rhs=vext[:L, h, :], start=True, stop=True),
                                      reads=[t_ktok[h], t_vext[h]], writes=[tc])
                                fw.op(fw.dve, lambda: V.scalar_tensor_tensor(out=Cst[:, h, jj, :], in0=Cst[:, h, jj, :], scalar=dlt[:, h:h + 1],
                                                                             in1=pc[:, 0:DH + 1], op0=ALU.mult, op1=ALU.add),
                                      reads=[tc, t_C[h], t_Cb[h]] + MC, writes=[t_C[h]])
                            try_stage = stage(11)
                            H = [t_hst]
                            fw.op(fw.act, lambda: A.activation(out=hst[:L, 0:1], in_=pX[:L, DH:DH + 1], func=AF.Abs), reads=[tX], writes=H)
                            fw.op(fw.dve, lambda: V.tensor_tensor(out=hst[:L, 0:1], in0=hst[:L, 0:1], in1=gts[:L, 32 + h:33 + h], op=ALU.max),
                                  reads=H + G, writes=H)
                            fw.op(fw.dve, lambda: V.reciprocal(out=hst[:L, 1:2], in_=hst[:L, 0:1]), reads=H, writes=H)
                            fw.op(fw.dve, lambda: V.tensor_scalar(out=hh[:L, :], in0=pX[:L, 0:DH], scalar1=hst[:L, 1:2], scalar2=None, op0=ALU.mult),
                                  reads=[tX] + H, writes=[t_hh])
                            try_stage = stage(12)
                            fw.op(fw.dve, lambda: V.reduce_sum(out=hst[:L, 2:3], in_=hh[:L, :], axis=AX.X), reads=[t_hh], writes=H)
                            fw.op(fw.act, lambda: A.activation(out=hjunk[:L, :], in_=hh[:L, :], func=AF.Square,
                                                              accum_out=hst[:L, 3:4]), reads=[t_hh], writes=H + [t_hjunk])
                            fw.op(fw.dve, lambda: V.tensor_scalar(out=hst[:L, 4:5], in0=hst[:L, 2:3], scalar1=1.0 / DH, scalar2=None, op0=ALU.mult), reads=H, writes=H)
                            fw.op(fw.dve, lambda: V.tensor_tensor(out=hst[:L, 5:6], in0=hst[:L, 4:5], in1=hst[:L, 4:5], op=ALU.mult), reads=H, writes=H)
                            fw.op(fw.dve, lambda: V.scalar_tensor_tensor(out=hst[:L, 6:7], in0=hst[:L, 3:4], scalar=1.0 / DH, in1=hst[:L, 5:6],
                                                                         op0=ALU.mult, op1=ALU.subtract), reads=H, writes=H)
                            fw.op(fw.act, lambda: A.activation(out=hst[:L, 7:8], in_=hst[:L, 6:7], func=AF.Ln, bias=HEAD_EPS), reads=H, writes=H)
                            fw.op(fw.act, lambda: A.activation(out=hst[:L, 8:9], in_=hst[:L, 7:8], func=AF.Exp, scale=-0.5), reads=H, writes=H)
                            fw.op(fw.dve, lambda: V.tensor_scalar(out=hn[h % 2][:L, :], in0=hh[:L, :], scalar1=hst[:L, 4:5], scalar2=hst[:L, 8:9],
                                                                  op0=ALU.subtract, op1=ALU.mult), reads=[t_hh] + H, writes=[t_hn[h % 2]])
                            try_stage = stage(13)
                            t3 = slice(3 * h, 3 * h + 3)
                            for jj in range(3):
                                fw.op(fw.pe, lambda: P.transpose(pTb[:, jj * L:(jj + 1) * L], hn[h % 2][:L, jj * 128:(jj + 1) * 128], identb[:L, :L]),
                                      reads=[t_hn[h % 2], t_tab], writes=[t_pTb[0]], signal=(jj == 2))
                            E = [t_e]
                            pT3 = pTb[:, 0:3 * L].rearrange("p (a t) -> p a t", a=3)
                            fw.op(fw.dve, lambda: V.tensor_tensor(out=e1[:], in0=xc2[:, t3, :], in1=sk[:, t3].unsqueeze(2).to_broadcast([128, 3, L]), op=ALU.mult),
                                  reads=[t_xc2, t_tab], writes=E)
                            fw.op(fw.dve, lambda: V.tensor_tensor(out=e3[:], in0=pT3, in1=nw[:, t3].unsqueeze(2).to_broadcast([128, 3, L]), op=ALU.mult),
                                  reads=[t_pTb[0], t_tab], writes=E)
                            fw.op(fw.dve, lambda: V.tensor_tensor(out=e1[:], in0=e1[:], in1=e3[:], op=ALU.add), reads=E, writes=E)
                            fw.op(fw.act, lambda: A.activation(out=e2[:], in_=ZB[:, t3, c0:c0 + L], func=AF.Tanh, scale=0.5), reads=[tZB], writes=E)
                            fw.op(fw.dve, lambda: V.scalar_tensor_tensor(out=e2[:], in0=e2[:], scalar=1.0, in1=ZB[:, t3, c0:c0 + L],
                                                                         op0=ALU.add, op1=ALU.mult), reads=E + [tZB], writes=E)
                            fw.op(fw.dve, lambda: V.scalar_tensor_tensor(out=ob[bs][:, t3, c0:c0 + L], in0=e1[:], scalar=0.5, in1=e2[:],
                                                                         op0=ALU.mult, op1=ALU.mult), reads=E, writes=[t_ob[bs]])
                    fw.dma(fw.act, ydst[:, :, tok0:tok0 + NB], ob[bs][:], reads=[t_ob[bs]], writes=[t_y0], owner=t_ob[bs])
        except StopEmit:
            pass
        fw.barrier()
        fw.stack = old


def phase_ssd(fw, p1T, t_p1, dtT_d, prm, y1T, t_y1):
    nc = fw.nc
    V, A, P = nc.vector, nc.scalar, nc.tensor
    L = TS
    NX = 48
    HP = 64
    with contextlib.ExitStack() as st:
        old = fw.stack
        fw.stack = st
        cw = fw.sb("scw", [128, 4, NX], F32)
        cb = fw.sb("scb", [128, NX], F32)
        dtb = fw.sb("dtb", [64, 1], F32)
        aneg = fw.sb("aneg", [64, 1], F32)
        D_b = fw.sb("D_b", [128, 64], F32)
        gn = fw.sb("gn", [128, 32], F32)
        Sel = fw.sb("Sel", [64, 64, L], F32)
        maskb = fw.sb("maskb", [128, L], F32)
        trif = fw.sb("trif", [128, L], F32)
        identf = fw.sb("identf", [128, 128], F32)
        identb = fw.sb("identb", [128, 128], BF16)
        ones_f = fw.sb("ones_f", [128, 128], F32)
        stT = fw.sb("stT", [128, 64, HP], F32)
        stb = fw.sb("stb", [128, 64, HP], BF16)
        t_tab = T("ssdtab")
        t_st = [T("st%d" % g) for g in range(8)]
        t_stb = [T("stb%d" % g) for g in range(8)]
        pf = [fw.ps("sdps%d" % i, [128, 512], F32) for i in range(6)]
        t_pf = [T("sdps%d" % i) for i in range(6)]
        pb = [fw.ps("sdpb%d" % i, [128, 1024], BF16) for i in range(2)]
        t_pb = [T("sdpb%d" % i) for i in range(2)]
        S = [T("ssdsetup")]
        o = lambda eng, fn: fw.op(eng, fn, reads=S, writes=S)
        with contextlib.ExitStack() as st2:
            fw.stack = st2
            tmpf = fw.sb("stmpf", [128, NX], F32)
            alog = fw.sb("alog", [64, 1], F32)
            fw.dma(fw.sp, identf[:], fw.consts["ident"], writes=S)
            fw.dma(fw.sp, trif[:L, :], fw.consts["tri"], writes=S)
            for k in range(4):
                fw.dma(fw.sp, cw[:, k, :], prm["ssd_conv_w"][k, :].rearrange("(t p) -> p t", p=128), writes=S, allow_slow_non_contiguous=True)
            fw.dma(fw.sp, cb[:], prm["ssd_conv_b"].rearrange("(t p) -> p t", p=128), writes=S, allow_slow_non_contiguous=True)
            fw.dma(fw.sp, gn[:], prm["ssd_gnorm"].rearrange("(t p) -> p t", p=128), writes=S, allow_slow_non_contiguous=True)
            fw.dma(fw.sp, dtb[:], prm["ssd_dt_bias"].rearrange("(h a) -> h a", a=1), writes=S)
            fw.dma(fw.sp, alog[:], prm["ssd_a_log"].rearrange("(h a) -> h a", a=1), writes=S)
            fw.dma(fw.sp, D_b[:], prm["ssd_d"].partition_broadcast(128), writes=S)
            o(fw.dve, lambda: V.tensor_scalar(out=cw[:], in0=cw[:], scalar1=0.5, scalar2=None, op0=ALU.mult))
            o(fw.dve, lambda: V.tensor_scalar(out=cb[:], in0=cb[:], scalar1=0.5, scalar2=None, op0=ALU.mult))
            o(fw.act, lambda: A.activation(out=aneg[:], in_=alog[:], func=AF.Exp))
            o(fw.dve, lambda: V.tensor_scalar(out=aneg[:], in0=aneg[:], scalar1=-1.0, scalar2=None, op0=ALU.mult))
            o(fw.dve, lambda: V.tensor_copy(out=identb[:], in_=identf[:]))
            o(fw.dve, lambda: V.memset(ones_f[:], 1.0))
            o(fw.dve, lambda: V.tensor_copy(out=Sel[:], in_=identf[0:64, 0:64].unsqueeze(2).to_broadcast([64, 64, L])))
            o(fw.dve, lambda: V.tensor_scalar(out=maskb[:L, :], in0=trif[:L, :], scalar1=-1.0, scalar2=30000.0, op0=ALU.add, op1=ALU.mult))
            fw.op(fw.dve, lambda: V.memset(stT[:], 0.0), reads=S, writes=S + [t_tab] + t_st)
            fw.barrier()
        fw.stack = st

        xblk = fw.sb("sxblk", [128, NX, NB + 3], BF16); t_xblk = T("sxblk")
        zblk = fw.sb("szblk", [128, 32, NB], BF16); t_zblk = T("szblk")
        dblk = fw.sb("sdblk", [64, NB], F32); t_dblk = T("sdblk")
        ob = fw.sb("sob", [128, 32, NB], BF16); t_ob = T("sob")
        accw = fw.sb("saccw", [128, 24, L], F32); t_acc = T("sacc")
        tmpw = fw.sb("stmpw", [128, 24, L], F32); t_th = T("sth")
        maskb4 = fw.sb("maskb4", [128, 4, L], F32)
        xc = fw.sb("sxc", [128, NX, L], BF16); t_xc = T("sxc")
        xtok = fw.sb("xtok", [128, 40 * 128], BF16); t_xtok = T("xtok")
        fT = fw.sb("fT", [64, 4, L], F32); t_fT = T("fT")
        tk = fw.sb("tk", [128, 5, 64], F32); t_tk = T("tk")
        dg = fw.sb("sdg", [64, 64], F32); t_dg = T("sdg")
        elb = fw.sb("elb", [128, 64], F32); lastb = fw.sb("lastb", [128, 64], F32); t_el = T("elb")
        cbT = fw.sb("cbT", [128, L], F32); t_cbT = T("cbT")
        Eh = [fw.sb("Eh%d" % i, [128, 4 * L], F32) for i in range(2)]; t_Eh = [T("Eh%d" % i) for i in range(2)]
        Wh = [fw.sb("Wh%d" % i, [128, 4 * L], BF16) for i in range(2)]; t_Wh = [T("Wh%d" % i) for i in range(2)]
        ys = fw.sb("sys", [128, 512], F32); xd = fw.sb("sxd", [128, 512], F32); t_ys = T("sys")
        xsd = fw.sb("xsd", [128, 512], BF16); t_xsd = T("xsd")
        gz = fw.sb("sgz", [128, 512], F32); t_gz = T("sgz")
        yz = fw.sb("syz", [128, 8, 512], BF16); t_yz = T("syz")
        sq = fw.sb("ssq", [128, 512], BF16); t_sq = T("ssq")
        nst = fw.sb("snst", [128, 32], F32); t_nst = T("snst")
        yn = fw.sb("syn", [128, 512], BF16); t_yn = T("syn")
        xsrc = p1T[4096:10240, :].rearrange("(j c) t -> c j t", c=128)
        zsrc = p1T[0:4096, :].rearrange("(j c) t -> c j t", c=128)
        ydst = y1T.rearrange("(j c) t -> c j t", c=128)
        fw.op(fw.dve, lambda: V.tensor_copy(out=maskb4[:L, :, :], in_=maskb[:L, :].unsqueeze(1).to_broadcast([L, 4, L])), reads=[t_tab], writes=[t_tab])
        try:
            for seq in range(NSEQ):
                if seq > 0:
                    fw.op(fw.dve, lambda: V.memset(stT[:], 0.0), writes=t_st)
                fw.op(fw.dve, lambda: V.memset(stb[:], 0.0), writes=t_stb)
                for tb in range(NTB):
                    if seq * NTB + tb >= DBG.get("ssd_max_blocks", 99):
                        raise StopEmit()
                    tok0 = seq * TSEQ + tb * NB
                    if tb == 0:
                        fw.op(fw.dve, lambda: V.memset(xblk[:, :, 0:3], 0.0), writes=[t_xblk])
                        fw.dma(fw.sp, xblk[:, 0:24, 3:NB + 3], xsrc[:, 0:24, tok0:tok0 + NB], reads=[t_p1], writes=[t_xblk])
                        fw.dma(fw.sp, xblk[:, 24:48, 3:NB + 3], xsrc[:, 24:48, tok0:tok0 + NB], reads=[t_p1], writes=[t_xblk])
                    else:
                        fw.dma(fw.sp, xblk[:, 0:24, :], xsrc[:, 0:24, tok0 - 3:tok0 + NB], reads=[t_p1], writes=[t_xblk])
                        fw.dma(fw.sp, xblk[:, 24:48, :], xsrc[:, 24:48, tok0 - 3:tok0 + NB], reads=[t_p1], writes=[t_xblk])
                    fw.dma(fw.sp, zblk[:], zsrc[:, :, tok0:tok0 + NB], reads=[t_p1], writes=[t_zblk])
                    fw.dma(fw.sp, dblk[:], dtT_d[:, tok0:tok0 + NB], reads=[t_p1], writes=[t_dblk])
                    for j in range(3):
                        c0 = j * L
                        for hf in range(2):
                            ts_ = slice(hf * 24, hf * 24 + 24)
                            bw = lambda k: cw[:, k, ts_].unsqueeze(2).to_broadcast([128, 24, L])
                            fw.op(fw.dve, lambda: V.tensor_tensor(out=accw[:], in0=xblk[:, ts_, c0:c0 + L], in1=bw(0), op=ALU.mult),
                                  reads=[t_xblk, t_tab], writes=[t_acc])
                            for k in range(1, 4):
                                fw.op(fw.dve, lambda: V.tensor_tensor(out=tmpw[:], in0=xblk[:, ts_, c0 + k:c0 + k + L], in1=bw(k), op=ALU.mult),
                                      reads=[t_xblk, t_tab], writes=[t_th])
                                fw.op(fw.dve, lambda: V.tensor_tensor(out=accw[:], in0=accw[:], in1=tmpw[:], op=ALU.add),
                                      reads=[t_th, t_acc], writes=[t_acc])
                            fw.op(fw.dve, lambda: V.tensor_tensor(out=accw[:], in0=accw[:], in1=cb[:, ts_].unsqueeze(2).to_broadcast([128, 24, L]), op=ALU.add),
                                  reads=[t_acc, t_tab], writes=[t_acc])
                            fw.op(fw.act, lambda: A.activation(out=tmpw[:], in_=accw[:], func=AF.Tanh), reads=[t_acc], writes=[t_th])
                            fw.op(fw.dve, lambda: V.scalar_tensor_tensor(out=xc[:, ts_, :], in0=tmpw[:], scalar=1.0, in1=accw[:], op0=ALU.add, op1=ALU.mult),
                                  reads=[t_th, t_acc], writes=[t_xc])
                        stage(21)
                        F_ = [t_fT]
                        fw.op(fw.act, lambda: A.activation(out=fT[:, 0, :], in_=dblk[:, c0:c0 + L], func=AF.Exp, bias=dtb[:, 0:1]), reads=[t_dblk, t_tab], writes=F_)
                        fw.op(fw.act, lambda: A.activation(out=fT[:, 0, :], in_=fT[:, 0, :], func=AF.Ln, bias=1.0), reads=F_, writes=F_)
                        fw.op(fw.dve, lambda: V.tensor_scalar(out=fT[:, 1, :], in0=fT[:, 0, :], scalar1=aneg[:, 0:1], scalar2=None, op0=ALU.mult), reads=F_ + [t_tab], writes=F_)
                        fw.op(fw.dve, lambda: V.tensor_tensor_scan(out=fT[:, 2, :], data0=ones_f[0:64, 0:L], data1=fT[:, 1, :], initial=0.0, op0=ALU.mult, op1=ALU.add),
                              reads=F_ + [t_tab], writes=F_)
                        fw.op(fw.dve, lambda: V.tensor_scalar(out=fT[:, 3, :], in0=fT[:, 2, :], scalar1=-1.0, scalar2=None, op0=ALU.mult), reads=F_, writes=F_)
                        pm = pf[5]; tm = t_pf[5]
                        fw.op(fw.pe, lambda: P.transpose(pm[:L, 0:64], fT[:, 0, :], identf[0:64, 0:64]), reads=F_ + [t_tab], writes=[tm])
                        fw.op(fw.pe, lambda: P.transpose(pm[:L, 64:128], fT[:, 2, :], identf[0:64, 0:64]), reads=F_ + [t_tab], writes=[tm])
                        fw.op(fw.dve, lambda: V.tensor_scalar(out=dg[:], in0=identf[0:64, 0:64], scalar1=fT[:, 2, L - 1:L], scalar2=None, op0=ALU.mult), reads=F_ + [t_tab], writes=[t_dg])
                        fw.op(fw.pe, lambda: P.matmul(pm[:, 128:192], lhsT=ones_f[0:64, :], rhs=dg[:], start=True, stop=True), reads=[t_dg, t_tab], writes=[tm])
                        K_ = [t_tk]
                        fw.op(fw.dve, lambda: V.tensor_copy(out=tk[:L, 0, :], in_=pm[:L, 0:64]), reads=[tm], writes=K_)
                        fw.op(fw.dve, lambda: V.tensor_copy(out=tk[:L, 4, :], in_=pm[:L, 64:128]), reads=[tm], writes=K_)
                        fw.op(fw.dve, lambda: V.tensor_scalar(out=tk[:L, 1, :], in0=pm[:L, 64:128], scalar1=-1.0, scalar2=None, op0=ALU.mult), reads=[tm], writes=K_)
                        fw.op(fw.act, lambda: A.activation(out=tk[:L, 2, :], in_=pm[:L, 64:128], func=AF.Exp), reads=[tm], writes=K_)
                        fw.op(fw.dve, lambda: V.tensor_copy(out=lastb[:], in_=pm[:, 128:192]), reads=[tm] + K_, writes=[t_el])
                        fw.op(fw.act, lambda: A.activation(out=elb[:], in_=lastb[:], func=AF.Exp), reads=[t_el], writes=[t_el])
                        fw.op(fw.dve, lambda: V.tensor_tensor(out=tk[:L, 3, :], in0=lastb[:L, :], in1=tk[:L, 4, :], op=ALU.subtract), reads=[t_el] + K_, writes=K_)
                        fw.op(fw.act, lambda: A.activation(out=tk[:L, 3, :], in_=tk[:L, 3, :], func=AF.Exp), reads=K_, writes=K_)
                        fw.op(fw.dve, lambda: V.tensor_tensor(out=tk[:L, 3, :], in0=tk[:L, 3, :], in1=tk[:L, 0, :], op=ALU.mult), reads=K_, writes=K_)
                        stage(22)
                        for r5 in range(5):
                            pbb = pb[r5 % 2]; tpb = t_pb[r5 % 2]
                            for i8 in range(8):
                                t = r5 * 8 + i8
                                fw.op(fw.pe, lambda: P.transpose(pbb[:L, i8 * 128:(i8 + 1) * 128], xc[:, t, :], identb[:, :]),
                                      reads=[t_xc, t_tab], writes=[tpb], signal=(i8 == 7))
                            if r5 % 2 == 0:
                                fw.op(fw.act, lambda: A.copy(out=xtok[:L, r5 * 1024:(r5 + 1) * 1024], in_=pbb[:L, :]), reads=[tpb], writes=[t_xtok])
                            else:
                                fw.op(fw.dve, lambda: V.tensor_copy(out=xtok[:L, r5 * 1024:(r5 + 1) * 1024], in_=pbb[:L, :]), reads=[tpb], writes=[t_xtok])
                        stage(23)
                        for g in range(8):
                            bmf = xc[:, 32 + g, :]
                            cmf = xc[:, 40 + g, :]
                            fw.op(fw.pe, lambda: P.matmul(pf[0][:L, 0:L], lhsT=bmf, rhs=cmf, start=True, stop=True), reads=[t_xc], writes=[t_pf[0]])
                            fw.op(fw.act, lambda: A.copy(out=cbT[:L, :], in_=pf[0][:L, 0:L]), reads=[t_pf[0]], writes=[t_cbT])
                            fw.op(fw.pe, lambda: P.matmul(pf[3][:L, :], lhsT=cmf, rhs=stb[:, 8 * g:8 * g + 8, :].rearrange("n h p -> n (h p)"), start=True, stop=True),
                                  reads=[t_xc, t_stb[g]], writes=[t_pf[3]])
                            for q4 in range(2):
                                h0_ = 8 * g + 4 * q4
                                e = q4 % 2
                                fw.op(fw.pe, lambda: P.matmul(pf[1][:L, 0:4 * L], lhsT=fT[:, 3, :], rhs=Sel[:, h0_:h0_ + 4, :].rearrange("k h t -> k (h t)"),
                                                              start=True, stop=False), reads=F_ + [t_tab], writes=[t_pf[1]], signal=False)
                                fw.op(fw.pe, lambda: P.matmul(pf[1][:L, 0:4 * L], lhsT=identf[:L, :L], rhs=maskb4[:L, :, :].rearrange("s h t -> s (h t)"),
                                                              start=False, stop=False), reads=[t_tab], writes=[t_pf[1]], signal=False)
                                for i4 in range(4):
                                    fw.op(fw.pe, lambda: P.matmul(pf[1][:L, i4 * L:(i4 + 1) * L], lhsT=Sel[:, h0_ + i4, :], rhs=fT[:, 2, :], start=False, stop=(i4 == 3)),
                                          reads=F_ + [t_tab], writes=[t_pf[1]], signal=(i4 == 3))
                                fw.op(fw.act, lambda: A.activation(out=Eh[e][:L, :], in_=pf[1][:L, 0:4 * L], func=AF.Exp), reads=[t_pf[1]], writes=[t_Eh[e]])
                                Ev = Eh[e][:L, :].rearrange("s (h t) -> s h t", h=4)
                                fw.op(fw.dve, lambda: V.tensor_tensor(out=Ev, in0=Ev, in1=tk[:L, 0, h0_:h0_ + 4].unsqueeze(2).to_broadcast([L, 4, L]), op=ALU.mult),
                                      reads=[t_Eh[e]] + K_, writes=[t_Eh[e]])
                                fw.op(fw.dve, lambda: V.tensor_tensor(out=Wh[e][:L, :].rearrange("s (h t) -> s h t", h=4), in0=Ev,
                                                                      in1=cbT[:L, :].unsqueeze(1).to_broadcast([L, 4, L]), op=ALU.mult),
                                      reads=[t_Eh[e], t_cbT], writes=[t_Wh[e]])
                                for i4 in range(4):
                                    hh_ = h0_ + i4
                                    hl = 4 * q4 + i4
                                    fw.op(fw.pe, lambda: P.matmul(pf[2][:L, hl * HP:(hl + 1) * HP], lhsT=Wh[e][:L, i4 * L:(i4 + 1) * L], rhs=xtok[:L, hh_ * HP:(hh_ + 1) * HP], start=True, stop=True),
                                          reads=[t_Wh[e], t_xtok], writes=[t_pf[2]], signal=(hl == 7))
                            g8 = slice(8 * g, 8 * g + 8)
                            xg = xtok[:L, g * 512:(g + 1) * 512].rearrange("s (h p) -> s h p", p=HP)
                            Y = [t_ys]
                            fw.op(fw.dve, lambda: V.tensor_tensor(out=ys[:L, :].rearrange("s (h p) -> s h p", p=HP), in0=pf[3][:L, :].rearrange("s (h p) -> s h p", p=HP),
                                                                  in1=tk[:L, 2, g8].unsqueeze(2).to_broadcast([L, 8, HP]), op=ALU.mult), reads=[t_pf[3]] + K_, writes=Y)
                            fw.op(fw.dve, lambda: V.tensor_tensor(out=xd[:L, :].rearrange("s (h p) -> s h p", p=HP), in0=xg,
                                                                  in1=D_b[:L, g8].unsqueeze(2).to_broadcast([L, 8, HP]), op=ALU.mult), reads=[t_xtok, t_tab], writes=Y)
                            fw.op(fw.dve, lambda: V.tensor_tensor(out=ys[:L, :], in0=ys[:L, :], in1=xd[:L, :], op=ALU.add), reads=Y, writes=Y)
                            fw.op(fw.dve, lambda: V.tensor_tensor(out=ys[:L, :], in0=pf[2][:L, :], in1=ys[:L, :], op=ALU.add), reads=Y + [t_pf[2]], writes=Y)
                            fw.op(fw.dve, lambda: V.tensor_tensor(out=xsd[:L, :].rearrange("s (h p) -> s h p", p=HP), in0=xg,
                                                                  in1=tk[:L, 3, g8].unsqueeze(2).to_broadcast([L, 8, HP]), op=ALU.mult), reads=[t_xtok] + K_, writes=[t_xsd])
                            fw.op(fw.pe, lambda: P.matmul(pf[4][:, :], lhsT=xtok[:L, 4096 + g * 128:4096 + (g + 1) * 128], rhs=xsd[:L, :], start=True, stop=True),
                                  reads=[t_xtok, t_xsd], writes=[t_pf[4]])
                            fw.op(fw.dve, lambda: V.tensor_tensor(out=stT[:, g8, :], in0=stT[:, g8, :], in1=elb[:, g8].unsqueeze(2).to_broadcast([128, 8, HP]), op=ALU.mult),
                                  reads=[t_st[g], t_el], writes=[t_st[g]])
                            fw.op(fw.dve, lambda: V.tensor_tensor(out=stT[:, g8, :].rearrange("n h p -> n (h p)"), in0=stT[:, g8, :].rearrange("n h p -> n (h p)"), in1=pf[4][:, :], op=ALU.add),
                                  reads=[t_st[g], t_pf[4]], writes=[t_st[g]])
                            fw.op(fw.act, lambda: A.copy(out=stb[:, g8, :], in_=stT[:, g8, :]), reads=[t_st[g], t_pf[3]], writes=[t_stb[g]])
                            pz = pb[g % 2]; tpz = t_pb[g % 2]
                            for i4 in range(4):
                                fw.op(fw.pe, lambda: P.transpose(pz[:L, i4 * 128:(i4 + 1) * 128], zblk[:, 4 * g + i4, c0:c0 + L], identb[:, :]),
                                      reads=[t_zblk, t_tab], writes=[tpz], signal=(i4 == 3))
                            fw.op(fw.act, lambda: A.activation(out=gz[:L, :], in_=pz[:L, 0:512], func=AF.Tanh, scale=0.5), reads=[tpz], writes=[t_gz])
                            fw.op(fw.dve, lambda: V.scalar_tensor_tensor(out=gz[:L, :], in0=gz[:L, :], scalar=1.0, in1=pz[:L, 0:512], op0=ALU.add, op1=ALU.mult),
                                  reads=[t_gz, tpz], writes=[t_gz])
                            fw.op(fw.dve, lambda: V.scalar_tensor_tensor(out=yz[:L, g, :], in0=ys[:L, :], scalar=0.5, in1=gz[:L, :], op0=ALU.mult, op1=ALU.mult),
                                  reads=Y + [t_gz], writes=[t_yz])
                            fw.op(fw.act, lambda: A.activation(out=sq[:L, :], in_=yz[:L, g, :], func=AF.Square, accum_out=nst[:L, g:g + 1]),
                                  reads=[t_yz], writes=[t_sq, t_nst])
                        stage(24)
                        N_ = [t_nst]
                        fw.op(fw.act, lambda: A.activation(out=nst[:L, 8:16], in_=nst[:L, 0:8], func=AF.Ln, scale=1.0 / 512, bias=EPS), reads=N_, writes=N_)
                        fw.op(fw.act, lambda: A.activation(out=nst[:L, 16:24], in_=nst[:L, 8:16], func=AF.Exp, scale=-0.5), reads=N_, writes=N_)
                        for g in range(8):
                            fw.op(fw.dve, lambda: V.tensor_scalar(out=yn[:L, :], in0=yz[:L, g, :], scalar1=nst[:L, 16 + g:17 + g], scalar2=None, op0=ALU.mult),
                                  reads=[t_yz] + N_, writes=[t_yn])
                            py = pb[g % 2]; tpy = t_pb[g % 2]
                            for i4 in range(4):
                                fw.op(fw.pe, lambda: P.transpose(py[:, i4 * L:(i4 + 1) * L], yn[:L, i4 * 128:(i4 + 1) * 128], identb[:L, :L]),
                                      reads=[t_yn, t_tab], writes=[tpy], signal=(i4 == 3))
                            for i4 in range(4):
                                t = 4 * g + i4
                                fw.op(fw.act, lambda: A.activation(out=ob[:, t, c0:c0 + L], in_=py[:, i4 * L:(i4 + 1) * L], func=AF.Copy, scale=gn[:, t:t + 1]),
                                      reads=[tpy, t_tab], writes=[t_ob])
                    fw.dma(fw.act, ydst[:, :, tok0:tok0 + NB], ob[:], reads=[t_ob], writes=[t_y1], owner=t_ob)
        except StopEmit:
            pass
        fw.barrier()
        fw.stack = old


PARAM_SHAPES = {
    "ab_norm": [D], "ab_w_in": [D, 8192],
    "s5_lambda_re": [64, 64], "s5_lambda_im": [64, 64], "s5_log_dt": [64], "s5_b_re": [64, 64, 16], "s5_b_im": [64, 64, 16],
    "s5_c_re": [64, 16, 64], "s5_c_im": [64, 16, 64], "s5_d": [1024], "s5_glu_w": [1024, 1024], "s5_glu_b": [1024],
    "ml_conv_w": [4, 3072], "ml_conv_b": [3072], "ml_wq": [768, 4, 4], "ml_wk": [768, 4, 4], "ml_wv": [768, 4, 4],
    "ml_w_gate": [9216, 16], "ml_b_gate": [16], "ml_norm": [3072], "ml_skip": [3072], "ab_w_out": [4096, D],
    "ssd_norm": [D], "ssd_w_in": [D, 10304], "ssd_conv_w": [4, 6144], "ssd_conv_b": [6144], "ssd_dt_bias": [64],
    "ssd_a_log": [64], "ssd_d": [64], "ssd_gnorm": [4096], "ssd_w_out": [4096, D], "final_norm": [D],
}


def build_program():
    nc = bass.Bass("TRN2", target_bir_lowering=False)
    dI = lambda n, s, dt=F32: nc.dram_tensor(n, list(s), dt, kind="ExternalInput").ap()
    dS = lambda n, s, dt: nc.dram_tensor(n, list(s), dt, kind="Internal").ap()
    x = dI("x", [NSEQ, SEQ, D])
    meta = dI("meta_tokens", [NMETA, D])
    prm = {k: dI(k, v) for k, v in PARAM_SHAPES.items()}
    cst = {k: dI("c_" + k, v) for k, v in CONST_SHAPES.items()}
    out = nc.dram_tensor("out", [NSEQ, SEQ, D], F32, kind="ExternalOutput").ap()
    wb0 = dS("wb0", [D, 8192], BF16); glub = dS("glub", [1024, 1024], BF16); wo0 = dS("wo0", [4096, D], BF16)
    wb1 = dS("wb1", [D, 10304], BF16); wo1 = dS("wo1", [4096, D], BF16)
    p0T = dS("p0T", [8192, TTOT], BF16); y0T = dS("y0T", [4096, TTOT], BF16)
    h1 = dS("h1", [TTOT, D], F32)
    p1T = dS("p1T", [10304, TTOT], BF16); dtT = dS("dtT", [64, TTOT], F32); y1T = dS("y1T", [4096, TTOT], BF16)
    with contextlib.ExitStack() as st:
        fw = FW(nc, st)
        fw.consts = cst
        fw.ident_dram = cst["ident"]
        t_wb0, t_glub, t_wo0, t_wb1, t_wo1 = T("wb0"), T("glub"), T("wo0"), T("wb1"), T("wo1")
        t_p0, t_y0, t_h1, t_p1, t_y1, t_out = T("p0T"), T("y0T"), T("h1"), T("p1T"), T("y1T"), T("out")
        cast_weights(fw, wb0, prm["ab_w_in"], D, t_wb0)
        cast_weights(fw, glub, prm["s5_glu_w"], 1024, t_glub)
        cast_weights(fw, wo0, prm["ab_w_out"], 4096, t_wo0)
        cast_weights(fw, wb1, prm["ssd_w_in"], D, t_wb1)
        cast_weights(fw, wo1, prm["ssd_w_out"], 4096, t_wo1)

        def load_h0(seq, tt, dst, tr):
            t0 = tt * TS
            if tt == 0:
                fw.dma(fw.sp, dst[0:NMETA, :], meta, writes=[tr])
                fw.dma(fw.sp, dst[NMETA:TS, :], x[seq, 0:TS - NMETA, :], writes=[tr])
            else:
                fw.dma(fw.sp, dst[0:TS, :], x[seq, t0 - NMETA:t0 - NMETA + TS, :], writes=[tr])

        def load_h1(seq, tt, dst, tr):
            r0 = seq * TSEQ + tt * TS
            fw.dma(fw.sp, dst[0:TS, :], h1[r0:r0 + TS, :], reads=[t_h1], writes=[tr])

        def store_h1(seq, tt, src, tr):
            r0 = seq * TSEQ + tt * TS
            fw.dma(fw.act, h1[r0:r0 + TS, :], src[0:TS, :], reads=[tr], writes=[t_h1], owner=tr)

        def store_out(seq, tt, src, tr):
            t0 = tt * TS
            if tt == 0:
                fw.dma(fw.act, out[seq, 0:TS - NMETA, :], src[NMETA:TS, :], reads=[tr], writes=[t_out], owner=tr)
            else:
                fw.dma(fw.act, out[seq, t0 - NMETA:t0 - NMETA + TS, :], src[0:TS, :], reads=[tr], writes=[t_out], owner=tr)

        phases = [
            lambda: phase_in_proj(fw, load_h0, prm["ab_norm"], wb0, t_wb0, 8192, p0T, t_p0),
            lambda: phase_s5(fw, p0T, t_p0, prm, glub, t_glub, y0T, t_y0),
            lambda: phase_mlstm(fw, p0T, t_p0, prm, y0T, t_y0),
            lambda: phase_out_proj(fw, y0T, t_y0, wo0, t_wo0, load_h0, store_h1),
            lambda: phase_in_proj(fw, load_h1, prm["ssd_norm"], wb1, t_wb1, 10304, p1T, t_p1, dt_out=(dtT, 10240)),
            lambda: phase_ssd(fw, p1T, t_p1, dtT, prm, y1T, t_y1),
            lambda: phase_out_proj(fw, y1T, t_y1, wo1, t_wo1, load_h1, store_out, final_g=prm["final_norm"]),
        ]
        for i, ph in enumerate(phases):
            if i in DBG.get("skip_phases", ()):
                continue
            ph()
        fw.barrier()
        fw.finish(fw.sp, [t_out])
    return nc


_NC_CACHE = {}


def kernel(**inputs):
    n_cores = 8
    if "nc" not in _NC_CACHE:
        _NC_CACHE["nc"] = build_program()
    nc = _NC_CACHE["nc"]
    consts = host_consts()
    shared = {}
    for k in PARAM_SHAPES:
        a = np.asarray(inputs[k], dtype=np.float32)
        if k != "final_norm":
            a = a[0]
        shared[k] = np.ascontiguousarray(a)
    shared["meta_tokens"] = np.ascontiguousarray(np.asarray(inputs["meta_tokens"], dtype=np.float32))
    for k, v in consts.items():
        shared["c_" + k] = v
    xin = np.asarray(inputs["x"], dtype=np.float32)
    in_maps = []
    for c in range(n_cores):
        m = dict(shared)
        m["x"] = np.ascontiguousarray(xin[c * NSEQ:(c + 1) * NSEQ])
        in_maps.append(m)
    res = run_bass_kernel_spmd(nc, in_maps, core_ids=list(range(n_cores)))
    return np.concatenate([np.asarray(r["out"], dtype=np.float32) for r in res.results], axis=0)
```

```python
import contextlib
import numpy as np
import concourse.bass as bass
import concourse.mybir as mybir
from concourse.bass_utils import run_bass_kernel_spmd

F32 = mybir.dt.float32
BF16 = mybir.dt.bfloat16
AF = mybir.ActivationFunctionType
ALU = mybir.AluOpType
AX = mybir.AxisListType


class Eng:
    def __init__(self, name, h, sem, is_pe=False):
        self.name, self.h, self.sem, self.is_pe = name, h, sem, is_pe
        self.count = 0
        self.pending = False
        self.known = {}


class T:
    __slots__ = ("name", "w", "r", "dsem", "dcount")

    def __init__(self, name):
        self.name = name
        self.w = None
        self.r = {}
        self.dsem = None
        self.dcount = 0


class FW:
    def __init__(self, nc, stack):
        self.nc, self.stack = nc, stack
        self.root = stack
        self.nsem = 0
        self.free_dsems = []
        self.consts = {}
        self.pe = Eng("pe", nc.tensor, self.sem("pe"), is_pe=True)
        self.dve = Eng("dve", nc.vector, self.sem("dve"))
        self.act = Eng("act", nc.scalar, self.sem("act"))
        self.pool = Eng("pool", nc.gpsimd, self.sem("pool"))
        self.sp = Eng("sp", nc.sync, self.sem("sp"))
        self.engs = [self.pe, self.dve, self.act, self.pool, self.sp]
        self.ninstr = 0
        self.dma_owners = []
        self.ident_dram = None

    def sem(self, name):
        self.nsem += 1
        self.uid = getattr(self, "uid", 0) + 1
        return self.root.enter_context(self.nc.semaphore("%s_u%d" % (name, self.uid)))

    def sb(self, name, shape, dt):
        self.uid = getattr(self, "uid", 0) + 1
        return self.stack.enter_context(self.nc.sbuf_tensor("%s_u%d" % (name, self.uid), list(shape), dt))

    def ps(self, name, shape, dt):
        self.uid = getattr(self, "uid", 0) + 1
        return self.stack.enter_context(self.nc.psum_tensor("%s_u%d" % (name, self.uid), list(shape), dt))

    def _wait(self, eng, tok):
        if tok is None:
            return
        sem, val, src = tok
        if src is eng and eng.is_pe:
            return
        k = id(sem)
        if eng.known.get(k, 0) >= val:
            return
        eng.h.wait_ge(sem, val)
        eng.known[k] = val
        self.ninstr += 1

    def _deps(self, eng, reads, writes):
        for t in reads:
            self._wait(eng, t.w)
        for t in writes:
            self._wait(eng, t.w)
            for tok in t.r.values():
                self._wait(eng, tok)

    def _record(self, tok, reads, writes):
        for t in reads:
            t.r[id(tok[0])] = tok
        for t in writes:
            t.w = tok
            t.r = {}

    def op(self, eng, fn, reads=(), writes=(), signal=True):
        self._deps(eng, reads, writes)
        ins = fn()
        self.ninstr += 1
        if signal:
            eng.count += 1
            ins.then_inc(eng.sem, 1)
            tok = (eng.sem, eng.count, eng)
        else:
            assert eng.is_pe
            tok = (eng.sem, eng.count + 1, eng)
        self._record(tok, reads, writes)
        return ins

    def dma(self, q, out, in_, reads=(), writes=(), owner=None, **kw):
        self._deps(q, reads, writes)
        if owner is None:
            owner = writes[0] if writes else reads[0]
        if owner.dsem is None:
            if self.free_dsems:
                owner.dsem, owner.dcount = self.free_dsems.pop()
            else:
                owner.dsem = self.sem("d_" + owner.name)
            self.dma_owners.append(owner)
        ins = q.h.dma_start(out=out, in_=in_, **kw)
        owner.dcount += 16
        ins.then_inc(owner.dsem, 16)
        self.ninstr += 1
        tok = (owner.dsem, owner.dcount, None)
        self._record(tok, reads, writes)
        return ins

    def finish(self, eng, trackers):
        for t in trackers:
            self._wait(eng, t.w)
            for tok in t.r.values():
                self._wait(eng, tok)

    def barrier(self):
        for e in self.engs:
            for e2 in self.engs:
                if e2 is e or e2.count == 0:
                    continue
                self._wait(e, (e2.sem, e2.count, e2))
            for t in self.dma_owners:
                if t.dcount:
                    self._wait(e, (t.dsem, t.dcount, None))
        for t in self.dma_owners:
            self.free_dsems.append((t.dsem, t.dcount))
            t.dsem = None
            t.dcount = 0
        self.dma_owners = []
        for e in self.engs:
            if e.count > 0:
                e.sem = self.sem(e.name)
                e.count = 0


D = 2048
KC = D // 128
NSEQ = 2
NMETA = 16
SEQ = 2048
TSEQ = SEQ + NMETA
TS = 86
NCH = TSEQ // TS
NB = 3 * TS
NTB = TSEQ // NB
TTOT = NSEQ * TSEQ
EPS = 1e-6


def cast_weights(fw, dst, src, rows, tr):
    nc = fw.nc
    for r0 in range(0, rows, 256):
        r1 = min(rows, r0 + 256)
        fw.dma(fw.pool, dst[r0:r1, :], src[r0:r1, :], writes=[tr], owner=tr)


def phase_in_proj(fw, load_tok, g_vec, w_bf, t_w, F, outT, t_out, dt_out=None):
    nc = fw.nc
    with contextlib.ExitStack() as st:
        old = fw.stack
        fw.stack = st
        xnT = fw.sb("xnT", [128, KC, TSEQ], BF16)
        t_xnT = [T("xnT%d" % i) for i in range(NCH)]
        gbc = fw.sb("gbc", [128, D], F32); t_gbc = T("gbc")
        ident = fw.sb("identb", [128, 128], BF16); t_ident = T("identb")
        identf = fw.sb("identf", [128, 128], F32); t_identf = T("identf")
        xt = [fw.sb("xt%d" % i, [128, D], F32) for i in range(2)]
        t_xt = [T("xt%d" % i) for i in range(2)]
        xs = [fw.sb("xs%d" % i, [128, D], BF16) for i in range(2)]
        t_xs = [T("xs%d" % i) for i in range(2)]
        junk = fw.sb("junk", [128, D], BF16); t_junk = T("junk")
        st_ = [fw.sb("stat%d" % i, [128, 4], F32) for i in range(2)]
        t_st = [T("stat%d" % i) for i in range(2)]
        WS = 512
        wsl = [fw.sb("wsl%d" % i, [128, KC, WS], BF16) for i in range(2)]
        t_wsl = [T("wsl%d" % i) for i in range(2)]
        ob = [fw.sb("ob%d" % i, [128, TSEQ], BF16) for i in range(2)]
        t_ob = [T("ob%d" % i) for i in range(2)]
        obf = fw.sb("obf", [128, TSEQ], F32); t_obf = T("obf")
        pst = [fw.ps("pst%d" % i, [128, 8, TS], BF16) for i in range(2)]
        t_pst = [T("pst%d" % i) for i in range(2)]
        pmm = [fw.ps("pmm%d" % i, [128, 512], F32) for i in range(4)]
        t_pmm = [T("pmm%d" % i) for i in range(4)]

        fw.dma(fw.sp, gbc[:], g_vec.partition_broadcast(128), writes=[t_gbc])
        fw.dma(fw.sp, identf[:], fw.ident_dram, writes=[t_identf])
        fw.op(fw.dve, lambda: nc.vector.tensor_copy(out=ident[:], in_=identf[:]), reads=[t_identf], writes=[t_ident])

        nslab = (F + WS - 1) // WS
        ev = 0
        ftc = 0
        for seq in range(NSEQ):
            for tt in range(NCH):
                s = tt % 2
                load_tok(seq, tt, xt[s], t_xt[s])
                fw.op(fw.act, lambda: nc.scalar.activation(out=junk[:TS, :], in_=xt[s][:TS, :], func=AF.Square,
                                                          accum_out=st_[s][:TS, 0:1]),
                      reads=[t_xt[s]], writes=[t_junk, t_st[s]])
                fw.op(fw.act, lambda: nc.scalar.activation(out=st_[s][:TS, 1:2], in_=st_[s][:TS, 0:1], func=AF.Ln,
                                                          scale=1.0 / D, bias=EPS),
                      reads=[t_st[s]], writes=[t_st[s]])
                fw.op(fw.act, lambda: nc.scalar.activation(out=st_[s][:TS, 2:3], in_=st_[s][:TS, 1:2], func=AF.Exp,
                                                          scale=-0.5),
                      reads=[t_st[s]], writes=[t_st[s]])
                fw.op(fw.dve, lambda: nc.vector.scalar_tensor_tensor(out=xs[s][:TS, :], in0=xt[s][:TS, :],
                                                                     scalar=st_[s][:TS, 2:3], in1=gbc[:TS, :],
                                                                     op0=ALU.mult, op1=ALU.mult),
                      reads=[t_xt[s], t_st[s], t_gbc], writes=[t_xs[s]])
                for half in range(2):
                    p = pst[half]
                    for j in range(8):
                        kc = half * 8 + j
                        fw.op(fw.pe, lambda: nc.tensor.transpose(p[:, j, :], xs[s][:TS, kc * 128:(kc + 1) * 128],
                                                                 ident[:TS, :TS]),
                              reads=[t_xs[s], t_ident], writes=[t_pst[half]], signal=(j == 7))
                    dst = xnT[:, half * 8:(half + 1) * 8, tt * TS:(tt + 1) * TS]
                    if half == 0:
                        fw.op(fw.act, lambda: nc.scalar.copy(out=dst, in_=p[:]), reads=[t_pst[half]], writes=[t_xnT[tt]])
                    else:
                        fw.op(fw.dve, lambda: nc.vector.tensor_copy(out=dst, in_=p[:]), reads=[t_pst[half]],
                              writes=[t_xnT[tt]])
            for sl in range(nslab):
                f0 = sl * WS
                fw_ = min(WS, F - f0)
                ws = sl % 2
                src = w_bf[:, f0:f0 + fw_].rearrange("(kc p) f -> p kc f", p=128)
                h = KC // 2
                fw.dma(fw.sp, wsl[ws][:, 0:h, 0:fw_], src[:, 0:h, :], reads=[t_w], writes=[t_wsl[ws]])
                fw.dma(fw.sp, wsl[ws][:, h:KC, 0:fw_], src[:, h:KC, :], reads=[t_w], writes=[t_wsl[ws]])
                for fi in range(0, fw_, 128):
                    m = min(128, fw_ - fi)
                    o = ftc % 2
                    ftc += 1
                    for tb in range(NTB):
                        pb = ev % 4
                        pm = pmm[pb]
                        for kc in range(KC):
                            fw.op(fw.pe, lambda: nc.tensor.matmul(pm[:m, 0:NB], lhsT=wsl[ws][:, kc, fi:fi + m],
                                                                  rhs=xnT[:, kc, tb * NB:(tb + 1) * NB],
                                                                  start=(kc == 0), stop=(kc == KC - 1)),
                                  reads=[t_wsl[ws]] + t_xnT[tb * 3:(tb + 1) * 3], writes=[t_pmm[pb]],
                                  signal=(kc == KC - 1))
                        dst = ob[o][:m, tb * NB:(tb + 1) * NB]
                        if dt_out is not None and f0 + fi >= dt_out[1]:
                            fw.op(fw.dve, lambda: nc.vector.tensor_copy(out=obf[:m, tb * NB:(tb + 1) * NB],
                                                                        in_=pm[:m, 0:NB]),
                                  reads=[t_pmm[pb]], writes=[t_obf])
                            fw.op(fw.dve, lambda: nc.vector.tensor_copy(out=dst, in_=obf[:m, tb * NB:(tb + 1) * NB]),
                                  reads=[t_obf], writes=[t_ob[o]])
                        elif ev % 2 == 0:
                            fw.op(fw.act, lambda: nc.scalar.copy(out=dst, in_=pm[:m, 0:NB]), reads=[t_pmm[pb]],
                                  writes=[t_ob[o]])
                        else:
                            fw.op(fw.dve, lambda: nc.vector.tensor_copy(out=dst, in_=pm[:m, 0:NB]), reads=[t_pmm[pb]],
                                  writes=[t_ob[o]])
                        ev += 1
                    fw.dma(fw.act, outT[f0 + fi:f0 + fi + m, seq * TSEQ:(seq + 1) * TSEQ], ob[o][:m, :],
                           reads=[t_ob[o]], writes=[t_out], owner=t_ob[o])
                    if dt_out is not None and f0 + fi >= dt_out[1]:
                        fw.dma(fw.act, dt_out[0][0:m, seq * TSEQ:(seq + 1) * TSEQ], obf[:m, :],
                               reads=[t_obf], writes=[t_out], owner=t_obf)
        fw.barrier()
        fw.stack = old


def phase_out_proj(fw, yT, t_y, w_bf, t_w, load_res, store, final_g=None):
    nc = fw.nc
    FI = 4096
    FC = FI // 128
    with contextlib.ExitStack() as st:
        old = fw.stack
        fw.stack = st
        wres = fw.sb("wres", [128, FC, D], BF16); t_wres = T("wres")
        yb = [fw.sb("yb%d" % i, [128, FC, NB], BF16) for i in range(2)]
        t_yb = [T("yb%d" % i) for i in range(2)]
        xt = [fw.sb("rt%d" % i, [128, D], F32) for i in range(2)]
        t_xt = [T("rt%d" % i) for i in range(2)]
        po = [fw.ps("po%d" % i, [128, 512], F32) for i in range(8)]
        t_po = [T("po%d" % i) for i in range(8)]
        if final_g is not None:
            gbc = fw.sb("gbcf", [128, D], F32); t_gbc = T("gbcf")
            junk = fw.sb("junkf", [128, D], BF16); t_junk = T("junkf")
            st_ = [fw.sb("statf%d" % i, [128, 4], F32) for i in range(2)]
            t_st = [T("statf%d" % i) for i in range(2)]
            fw.dma(fw.sp, gbc[:], final_g.partition_broadcast(128), writes=[t_gbc])
        src = w_bf.rearrange("(fc p) d -> p fc d", p=128)
        for q4 in range(4):
            fw.dma(fw.sp, wres[:, q4 * 8:(q4 + 1) * 8, :], src[:, q4 * 8:(q4 + 1) * 8, :], reads=[t_w], writes=[t_wres])
        ysrc = yT.rearrange("(fc p) t -> p fc t", p=128)
        pc = 0
        for seq in range(NSEQ):
            for tb in range(NTB):
                ys = (seq * NTB + tb) % 2
                tok0 = seq * TSEQ + tb * NB
                fw.dma(fw.sp, yb[ys][:, 0:FC // 2, :], ysrc[:, 0:FC // 2, tok0:tok0 + NB], reads=[t_y], writes=[t_yb[ys]])
                fw.dma(fw.sp, yb[ys][:, FC // 2:FC, :], ysrc[:, FC // 2:FC, tok0:tok0 + NB], reads=[t_y], writes=[t_yb[ys]])
                for j in range(3):
                    tt = tb * 3 + j
                    s = tt % 2
                    load_res(seq, tt, xt[s], t_xt[s])
                    for db in range(4):
                        pb = pc % 8
                        pc += 1
                        for fc in range(FC):
                            fw.op(fw.pe, lambda: nc.tensor.matmul(po[pb][:TS, :], lhsT=yb[ys][:, fc, j * TS:(j + 1) * TS],
                                                                  rhs=wres[:, fc, db * 512:(db + 1) * 512],
                                                                  start=(fc == 0), stop=(fc == FC - 1)),
                                  reads=[t_yb[ys], t_wres], writes=[t_po[pb]], signal=(fc == FC - 1))
                        fw.op(fw.dve, lambda: nc.vector.tensor_tensor(out=xt[s][:TS, db * 512:(db + 1) * 512],
                                                                      in0=po[pb][:TS, :],
                                                                      in1=xt[s][:TS, db * 512:(db + 1) * 512], op=ALU.add),
                              reads=[t_po[pb], t_xt[s]], writes=[t_xt[s]])
                    if final_g is not None:
                        fw.op(fw.act, lambda: nc.scalar.activation(out=junk[:TS, :], in_=xt[s][:TS, :], func=AF.Square,
                                                                  accum_out=st_[s][:TS, 0:1]),
                              reads=[t_xt[s]], writes=[t_junk, t_st[s]])
                        fw.op(fw.act, lambda: nc.scalar.activation(out=st_[s][:TS, 1:2], in_=st_[s][:TS, 0:1], func=AF.Ln,
                                                                  scale=1.0 / D, bias=EPS),
                              reads=[t_st[s]], writes=[t_st[s]])
                        fw.op(fw.act, lambda: nc.scalar.activation(out=st_[s][:TS, 2:3], in_=st_[s][:TS, 1:2], func=AF.Exp,
                                                                  scale=-0.5),
                              reads=[t_st[s]], writes=[t_st[s]])
                        fw.op(fw.dve, lambda: nc.vector.scalar_tensor_tensor(out=xt[s][:TS, :], in0=xt[s][:TS, :],
                                                                             scalar=st_[s][:TS, 2:3], in1=gbc[:TS, :],
                                                                             op0=ALU.mult, op1=ALU.mult),
                              reads=[t_xt[s], t_st[s], t_gbc], writes=[t_xt[s]])
                    store(seq, tt, xt[s], t_xt[s])
        fw.barrier()
        fw.stack = old


def host_consts():
    c = {}
    c["ident"] = np.eye(128, dtype=np.float32)
    s = np.arange(TS)
    c["tri"] = (s[:, None] <= s[None, :]).astype(np.float32)
    p = np.arange(128)
    c["mask8"] = (p[:, None] // 16 == np.arange(8)[None, :]).astype(np.float32)
    mc = np.zeros((128, 4, 128), np.float32)
    for kl in range(4):
        mc[:, kl, :] = ((p[None, :] // 16) == (2 * kl + p[:, None] // 64)).astype(np.float32)
    c["maskC"] = mc
    c["bd4"] = (p[:, None] // 4 == p[None, :] // 4).astype(np.float32)
    return c


CONST_SHAPES = {"ident": [128, 128], "tri": [TS, TS], "mask8": [128, 8], "maskC": [128, 4, 128], "bd4": [128, 128]}


def cmul_ops(fw, nc, out_r, out_i, ar, ai, br, bi, tmp, tr_out, tr_in, tr_tmp, neg_imag_b=False):
    t1, t2 = tmp
    V = nc.vector
    fw.op(fw.dve, lambda: V.tensor_tensor(out=t1, in0=ar, in1=br, op=ALU.mult), reads=tr_in, writes=tr_tmp)
    fw.op(fw.dve, lambda: V.tensor_tensor(out=t2, in0=ai, in1=bi, op=ALU.mult), reads=tr_in, writes=tr_tmp)
    fw.op(fw.dve, lambda: V.tensor_tensor(out=out_r, in0=t1, in1=t2, op=ALU.subtract), reads=tr_tmp, writes=tr_out)
    fw.op(fw.dve, lambda: V.tensor_tensor(out=t1, in0=ar, in1=bi, op=ALU.mult), reads=tr_in + tr_out, writes=tr_tmp)
    fw.op(fw.dve, lambda: V.tensor_tensor(out=t2, in0=ai, in1=br, op=ALU.mult), reads=tr_in, writes=tr_tmp)
    fw.op(fw.dve, lambda: V.tensor_tensor(out=out_i, in0=t1, in1=t2, op=ALU.add), reads=tr_tmp, writes=tr_out)


def phase_s5(fw, p0T, t_p0, prm, glu_bf, t_glu, y0T, t_y0):
    nc = fw.nc
    V, A, P = nc.vector, nc.scalar, nc.tensor
    L = TS
    with contextlib.ExitStack() as st:
        old = fw.stack
        fw.stack = st
        En_r = fw.sb("En_r", [128, 32, 128], F32)
        En_i = fw.sb("En_i", [128, 32, 128], F32)
        Ep_r = fw.sb("Ep_r", [128, 32, L], F32)
        Ep_i = fw.sb("Ep_i", [128, 32, L], F32)
        Bblk = fw.sb("Bblk", [128, 8, 1024], BF16)
        Cblk = fw.sb("Cblk", [128, 32, 2, 128], BF16)
        tri = fw.sb("trib", [128, L], BF16)
        glu = fw.sb("gluw", [128, 8, 1024], BF16)
        dvec = fw.sb("s5d", [128, 8], F32)
        hb = fw.sb("s5hb", [128, 8], F32)
        car_r = fw.sb("car_r", [128, 32], F32)
        car_i = fw.sb("car_i", [128, 32], F32)
        t_tab = T("s5tab")
        t_car = [T("car%d" % i) for i in range(8)]
        ps = [fw.ps("s5ps%d" % i, [128, 512], F32) for i in range(8)]
        t_ps = [T("s5ps%d" % i) for i in range(8)]

        with contextlib.ExitStack() as st2:
            fw.stack = st2
            t_s = T("s5setup")
            identf = fw.sb("identf", [128, 128], F32)
            trif = fw.sb("trif", [128, L], F32)
            mask8 = fw.sb("mask8", [128, 8], F32)
            maskC = fw.sb("maskC", [128, 4, 128], F32)
            lr = fw.sb("lr", [128, 32], F32); li = fw.sb("li", [128, 32], F32); ldt = fw.sb("ldt", [128, 32], F32)
            w = [fw.sb("s5w%d" % i, [128, 32], F32) for i in range(12)]
            Es_r = fw.sb("Es_r", [128, 32, L], F32); Es_i = fw.sb("Es_i", [128, 32, L], F32)
            tA = fw.sb("tA", [128, 32, 64], F32); tB = fw.sb("tB", [128, 32, 64], F32)
            Bp = [fw.sb("Bp%d" % i, [64, 64, 16], F32) for i in range(2)]
            Cn = [fw.sb("Cn%d" % i, [128, 8, 64], F32) for i in range(2)]
            glub = fw.sb("glub", [128, 8], F32)
            Cn2 = fw.sb("Cn2", [128, 2, 64], F32)
            S = [t_s]
            fw.dma(fw.sp, identf[:], fw.consts["ident"], writes=S)
            fw.dma(fw.sp, trif[:L, :], fw.consts["tri"], writes=S)
            fw.dma(fw.sp, mask8[:], fw.consts["mask8"], writes=S)
            fw.dma(fw.sp, maskC[:], fw.consts["maskC"], writes=S)
            fw.dma(fw.sp, lr[:], prm["s5_lambda_re"].rearrange("g p -> (g p)").rearrange("(k q) -> q k", q=128), writes=S,
                   allow_slow_non_contiguous=True)
            fw.dma(fw.sp, li[:], prm["s5_lambda_im"].rearrange("g p -> (g p)").rearrange("(k q) -> q k", q=128), writes=S,
                   allow_slow_non_contiguous=True)
            ldv = prm["s5_log_dt"].rearrange("(k gl) -> gl k", gl=2)
            for gl in range(2):
                fw.dma(fw.sp, ldt[gl * 64:(gl + 1) * 64, :], ldv[gl, :].partition_broadcast(64), writes=S,
                       allow_slow_non_contiguous=True)
            fw.dma(fw.sp, Bp[0][:], prm["s5_b_re"].rearrange("g p h -> p g h"), writes=S)
            fw.dma(fw.sp, Bp[1][:], prm["s5_b_im"].rearrange("g p h -> p g h"), writes=S)
            fw.dma(fw.sp, Cn[0][:], prm["s5_c_re"].rearrange("(ct gl) h p -> (gl h) ct p", gl=8), writes=S)
            fw.dma(fw.sp, Cn[1][:], prm["s5_c_im"].rearrange("(ct gl) h p -> (gl h) ct p", gl=8), writes=S)
            fw.dma(fw.sp, dvec[:], prm["s5_d"].rearrange("(ct c) -> c ct", c=128), writes=S, allow_slow_non_contiguous=True)
            fw.dma(fw.sp, glub[:], prm["s5_glu_b"].rearrange("(ct c) -> c ct", c=128), writes=S, allow_slow_non_contiguous=True)
            gsrc = glu_bf.rearrange("(ci p) co -> p ci co", p=128)
            fw.dma(fw.sp, glu[:], gsrc, reads=[t_glu], writes=[t_tab])
            o = lambda eng, fn: fw.op(eng, fn, reads=S, writes=S)
            o(fw.dve, lambda: V.tensor_copy(out=tri[:L, :], in_=trif[:L, :]))
            o(fw.dve, lambda: V.tensor_scalar(out=hb[:], in0=glub[:], scalar1=0.5, scalar2=None, op0=ALU.mult))
            dt_, lrd, lid, magp, magn, cs, sn, bpr, bpi, bnr, bni, tq = w
            o(fw.act, lambda: A.activation(out=dt_[:], in_=ldt[:], func=AF.Exp))
            o(fw.dve, lambda: V.scalar_tensor_tensor(out=lrd[:], in0=lr[:], scalar=1.0 / 16, in1=dt_[:], op0=ALU.mult, op1=ALU.mult))
            o(fw.dve, lambda: V.scalar_tensor_tensor(out=lid[:], in0=li[:], scalar=1.0 / 16, in1=dt_[:], op0=ALU.mult, op1=ALU.mult))
            o(fw.act, lambda: A.activation(out=magp[:], in_=lrd[:], func=AF.Exp))
            o(fw.act, lambda: A.activation(out=magn[:], in_=lrd[:], func=AF.Exp, scale=-1.0))
            o(fw.act, lambda: A.activation(out=sn[:], in_=lid[:], func=AF.Sin))
            o(fw.dve, lambda: V.tensor_scalar(out=tq[:], in0=lid[:], scalar1=float(np.pi / 2), scalar2=None, op0=ALU.add))
            o(fw.act, lambda: A.activation(out=cs[:], in_=tq[:], func=AF.Sin))
            o(fw.dve, lambda: V.tensor_tensor(out=bpr[:], in0=magp[:], in1=cs[:], op=ALU.mult))
            o(fw.dve, lambda: V.tensor_tensor(out=bpi[:], in0=magp[:], in1=sn[:], op=ALU.mult))
            o(fw.dve, lambda: V.tensor_tensor(out=bnr[:], in0=magn[:], in1=cs[:], op=ALU.mult))
            o(fw.dve, lambda: V.scalar_tensor_tensor(out=bni[:], in0=magn[:], scalar=-1.0, in1=sn[:], op0=ALU.mult, op1=ALU.mult))

            def csq(r, i, t1, t2):
                o(fw.dve, lambda: V.tensor_tensor(out=t1, in0=r, in1=r, op=ALU.mult))
                o(fw.dve, lambda: V.tensor_tensor(out=t2, in0=i, in1=i, op=ALU.mult))
                o(fw.dve, lambda: V.scalar_tensor_tensor(out=i, in0=r, scalar=2.0, in1=i, op0=ALU.mult, op1=ALU.mult))
                o(fw.dve, lambda: V.tensor_tensor(out=r, in0=t1, in1=t2, op=ALU.subtract))

            for _ in range(4):
                csq(bpr[:], bpi[:], magp[:], magn[:])
            for _ in range(4):
                csq(bnr[:], bni[:], magp[:], magn[:])
            am1, den, qr, qi = cs, sn, dt_, lrd
            o(fw.dve, lambda: V.tensor_scalar(out=am1[:], in0=bpr[:], scalar1=-1.0, scalar2=None, op0=ALU.add))
            o(fw.dve, lambda: V.tensor_tensor(out=magp[:], in0=lr[:], in1=lr[:], op=ALU.mult))
            o(fw.dve, lambda: V.tensor_tensor(out=magn[:], in0=li[:], in1=li[:], op=ALU.mult))
            o(fw.dve, lambda: V.tensor_tensor(out=den[:], in0=magp[:], in1=magn[:], op=ALU.add))
            o(fw.dve, lambda: V.reciprocal(out=den[:], in_=den[:]))
            o(fw.dve, lambda: V.tensor_tensor(out=magp[:], in0=am1[:], in1=lr[:], op=ALU.mult))
            o(fw.dve, lambda: V.tensor_tensor(out=magn[:], in0=bpi[:], in1=li[:], op=ALU.mult))
            o(fw.dve, lambda: V.tensor_tensor(out=qr[:], in0=magp[:], in1=magn[:], op=ALU.add))
            o(fw.dve, lambda: V.tensor_tensor(out=qr[:], in0=qr[:], in1=den[:], op=ALU.mult))
            o(fw.dve, lambda: V.tensor_tensor(out=magp[:], in0=bpi[:], in1=lr[:], op=ALU.mult))
            o(fw.dve, lambda: V.tensor_tensor(out=magn[:], in0=am1[:], in1=li[:], op=ALU.mult))
            o(fw.dve, lambda: V.tensor_tensor(out=qi[:], in0=magp[:], in1=magn[:], op=ALU.subtract))
            o(fw.dve, lambda: V.tensor_tensor(out=qi[:], in0=qi[:], in1=den[:], op=ALU.mult))

            def build_pow(Er, Ei, br, bi):
                o(fw.dve, lambda: V.tensor_copy(out=Er[:, :, 0], in_=br))
                o(fw.dve, lambda: V.tensor_copy(out=Ei[:, :, 0], in_=bi))
                n = 1
                while n < L:
                    m = min(n, L - n)
                    cb_r = br.unsqueeze(2).to_broadcast([128, 32, m])
                    cb_i = bi.unsqueeze(2).to_broadcast([128, 32, m])
                    cmul_ops(fw, nc, Er[:, :, n:n + m], Ei[:, :, n:n + m], Er[:, :, 0:m], Ei[:, :, 0:m], cb_r, cb_i,
                             (tA[:, :, 0:m], tB[:, :, 0:m]), S, S, S)
                    n += m
                    if n < L:
                        csq(br, bi, magp[:], magn[:])

            build_pow(Ep_r, Ep_i, bpr[:], bpi[:])
            build_pow(Es_r, Es_i, bnr[:], bni[:])
            qb_r = qr[:].unsqueeze(2).to_broadcast([128, 32, L])
            qb_i = qi[:].unsqueeze(2).to_broadcast([128, 32, L])
            for h0 in (0, 43):
                sl = slice(h0, h0 + 43)
                t1, t2 = tA[:, :, 0:43], tB[:, :, 0:43]
                qr_b = qr[:].unsqueeze(2).to_broadcast([128, 32, 43])
                qi_b = qi[:].unsqueeze(2).to_broadcast([128, 32, 43])
                o(fw.dve, lambda: V.tensor_tensor(out=t1, in0=Es_r[:, :, sl], in1=qi_b, op=ALU.mult))
                o(fw.dve, lambda: V.tensor_tensor(out=t2, in0=Es_i[:, :, sl], in1=qi_b, op=ALU.mult))
                o(fw.dve, lambda: V.tensor_tensor(out=Es_r[:, :, sl], in0=Es_r[:, :, sl], in1=qr_b, op=ALU.mult))
                o(fw.dve, lambda: V.tensor_tensor(out=Es_i[:, :, sl], in0=Es_i[:, :, sl], in1=qr_b, op=ALU.mult))
                o(fw.dve, lambda: V.tensor_tensor(out=Es_r[:, :, sl], in0=Es_r[:, :, sl], in1=t2, op=ALU.subtract))
                o(fw.dve, lambda: V.tensor_tensor(out=Es_i[:, :, sl], in0=Es_i[:, :, sl], in1=t1, op=ALU.add))
            for k in range(32):
                for ri, (src, dst) in enumerate(((Es_r, En_r), (Es_i, En_i))):
                    pb = ps[(2 * k + ri) % 8]
                    o(fw.pe, lambda: P.transpose(pb[:L, 0:128], src[:, k, :], identf[:, :]))
                    o(fw.act, lambda: A.copy(out=dst[:L, k, :], in_=pb[:L, 0:128]))
            for ct in range(8):
                pb = ps[ct % 8]
                for ri in range(2):
                    o(fw.pe, lambda: P.transpose(pb[:, ri * 64:(ri + 1) * 64],
                                                 Bp[ri][:, ct * 8:(ct + 1) * 8, :].rearrange("p g h -> p (g h)"),
                                                 identf[:64, :64]))
                src = pb[:, 0:128].rearrange("c (ri p) -> c ri p", ri=2).unsqueeze(2).to_broadcast([128, 2, 8, 64])
                msk = mask8[:].unsqueeze(1).unsqueeze(3).to_broadcast([128, 2, 8, 64])
                o(fw.dve, lambda: V.tensor_tensor(out=Bblk[:, ct, :].rearrange("c (ri g p) -> c ri g p", ri=2, g=8),
                                                  in0=src, in1=msk, op=ALU.mult))
            for ct in range(8):
                for ri in range(2):
                    pb = ps[(2 * ct + ri) % 8]
                    o(fw.dve, lambda: V.tensor_copy(out=Cn2[:], in_=Cn[ri][:, ct, :].unsqueeze(1).to_broadcast([128, 2, 64])))
                    o(fw.pe, lambda: P.transpose(pb[:, 0:128], Cn2[:].rearrange("c a p -> c (a p)"), identf[:, :]))
                    for kl in range(4):
                        k = ct * 4 + kl
                        o(fw.dve, lambda: V.scalar_tensor_tensor(out=Cblk[:, k, ri, :], in0=pb[:, 0:128],
                                                                 scalar=(1.0 if ri == 0 else -1.0), in1=maskC[:, kl, :],
                                                                 op0=ALU.mult, op1=ALU.mult))
            o(fw.dve, lambda: V.memset(car_r[:], 0.0))
            fw.op(fw.dve, lambda: V.memset(car_i[:], 0.0), reads=S, writes=S + [t_tab] + t_car)
            fw.barrier()
        fw.stack = st

        ub = [fw.sb("s5ub%d" % i, [128, 16, NB], BF16) for i in range(2)]
        t_ub = [T("s5ub%d" % i) for i in range(2)]
        tmp = [fw.sb("s5t%d" % i, [128, 512], F32) for i in range(4)]
        t_tmp = [T("s5tmp0"), T("s5tmp1")]
        zin = [fw.sb("zin%d" % i, [128, 1024], BF16) for i in range(2)]
        t_zin = [T("zin%d" % i) for i in range(2)]
        zc2 = [fw.sb("zc%d" % i, [128, 2, 4, L], F32) for i in range(2)]; t_zc2 = [T("zc0"), T("zc1")]
        rt2 = [[fw.sb("s5rt%d_%d" % (q, i), [128, 4, L], F32) for i in range(4)] for q in range(2)]
        t_rt2 = [T("s5rt0"), T("s5rt1")]
        xb = [fw.sb("s5xb%d" % i, [128, 2, 4, L], BF16) for i in range(2)]
        t_xb = [T("s5xb%d" % i) for i in range(2)]
        yv2 = [fw.sb("s5yv%d" % i, [128, L], F32) for i in range(2)]; t_yv2 = [T("s5yv0"), T("s5yv1")]
        e12 = [fw.sb("s5e1_%d" % i, [128, L], F32) for i in range(2)]; e22 = [fw.sb("s5e2_%d" % i, [128, L], F32) for i in range(2)]
        t_e2 = [T("s5e0"), T("s5e1")]
        e1, e2, t_e = e12[0], e22[0], t_e2[0]
        g2 = [fw.sb("s5g2%d" % i, [128, 8, L], BF16) for i in range(2)]
        t_g2 = [T("s5g2%d" % i) for i in range(2)]
        ob = [fw.sb("s5ob%d" % i, [128, 8, NB], BF16) for i in range(2)]
        t_ob = [T("s5ob%d" % i) for i in range(2)]
        usrc = p0T[0:2048, :].rearrange("(j c) t -> c j t", c=128)
        ydst = y0T[0:1024, :].rearrange("(j c) t -> c j t", c=128)
        ci = 0
        for seq in range(NSEQ):
            if seq > 0:
                fw.op(fw.dve, lambda: V.memset(car_r[:], 0.0), writes=t_car)
                fw.op(fw.dve, lambda: V.memset(car_i[:], 0.0), writes=t_car)
            for tb in range(NTB):
                bs = (seq * NTB + tb) % 2
                tok0 = seq * TSEQ + tb * NB
                fw.dma(fw.sp, ub[bs][:], usrc[:, :, tok0:tok0 + NB], reads=[t_p0], writes=[t_ub[bs]])
                for j in range(3):
                    cs_ = slice(j * L, (j + 1) * L)
                    gs = ci % 2
                    ci += 1
                    for ct in range(8):
                        zs = ct % 2
                        zc, t_zc = zc2[zs], t_zc2[zs]
                        rt, t_rt = rt2[zs], t_rt2[zs]
                        yv, t_yv = yv2[zs], t_yv2[zs]
                        e1, e2, t_e = e12[zs], e22[zs], t_e2[zs]
                        pA, pB, pZr, pZi, pY = ps[0], ps[1], ps[2 + 2 * zs], ps[3 + 2 * zs], ps[6 + zs]
                        tA_, tB_, tZr, tZi, tY = t_ps[0], t_ps[1], t_ps[2 + 2 * zs], t_ps[3 + 2 * zs], t_ps[6 + zs]
                        u_ct = ub[bs][:, ct, cs_]
                        fw.op(fw.pe, lambda: P.matmul(pA[:L, :], lhsT=u_ct, rhs=Bblk[:, ct, 0:512], start=True, stop=True),
                              reads=[t_ub[bs], t_tab], writes=[tA_])
                        fw.op(fw.pe, lambda: P.matmul(pB[:L, :], lhsT=u_ct, rhs=Bblk[:, ct, 512:1024], start=True, stop=True),
                              reads=[t_ub[bs], t_tab], writes=[tB_])
                        enr = En_r[:L, ct * 4:(ct + 1) * 4, :].rearrange("s k q -> s (k q)")
                        eni = En_i[:L, ct * 4:(ct + 1) * 4, :].rearrange("s k q -> s (k q)")
                        cmul_ops(fw, nc, zin[zs][:L, 0:512], zin[zs][:L, 512:1024], pA[:L, :], pB[:L, :], enr, eni,
                                 (tmp[2 * zs][:L, :], tmp[2 * zs + 1][:L, :]), [t_zin[zs]], [tA_, tB_, t_tab], [t_tmp[zs]])
                        for cb in range(8):
                            pz = pZr if cb < 4 else pZi
                            tz = tZr if cb < 4 else tZi
                            fw.op(fw.pe, lambda: P.matmul(pz[:, (cb % 4) * L:(cb % 4 + 1) * L],
                                                          lhsT=zin[zs][:L, cb * 128:(cb + 1) * 128], rhs=tri[:L, :],
                                                          start=True, stop=True),
                                  reads=[t_zin[zs], t_tab], writes=[tz], signal=(cb % 4 == 3))
                        k4 = slice(ct * 4, ct * 4 + 4)
                        fw.op(fw.dve, lambda: V.tensor_tensor(out=zc[:, 0, :, :],
                                                              in0=pZr[:, 0:4 * L].rearrange("q (k t) -> q k t", k=4),
                                                              in1=car_r[:, k4].unsqueeze(2).to_broadcast([128, 4, L]),
                                                              op=ALU.add),
                              reads=[tZr, t_car[ct]], writes=[t_zc])
                        fw.op(fw.dve, lambda: V.tensor_tensor(out=zc[:, 1, :, :],
                                                              in0=pZi[:, 0:4 * L].rearrange("q (k t) -> q k t", k=4),
                                                              in1=car_i[:, k4].unsqueeze(2).to_broadcast([128, 4, L]),
                                                              op=ALU.add),
                              reads=[tZi, t_car[ct]], writes=[t_zc])
                        xs_ = ct % 2
                        epr, epi = Ep_r[:, k4, :], Ep_i[:, k4, :]
                        RT = [t_rt]
                        fw.op(fw.dve, lambda: V.tensor_tensor(out=rt[0][:], in0=zc[:, 0], in1=epr, op=ALU.mult), reads=[t_zc, t_tab], writes=RT)
                        fw.op(fw.dve, lambda: V.tensor_tensor(out=rt[1][:], in0=zc[:, 1], in1=epi, op=ALU.mult), reads=[t_zc, t_tab], writes=RT)
                        fw.op(fw.dve, lambda: V.tensor_tensor(out=rt[2][:], in0=zc[:, 0], in1=epi, op=ALU.mult), reads=[t_zc, t_tab], writes=RT)
                        fw.op(fw.dve, lambda: V.tensor_tensor(out=rt[3][:], in0=zc[:, 1], in1=epr, op=ALU.mult), reads=[t_zc, t_tab], writes=RT)
                        fw.op(fw.dve, lambda: V.tensor_tensor(out=xb[xs_][:, 0], in0=rt[0][:], in1=rt[1][:], op=ALU.subtract), reads=RT, writes=[t_xb[xs_]])
                        fw.op(fw.dve, lambda: V.tensor_tensor(out=xb[xs_][:, 1], in0=rt[2][:], in1=rt[3][:], op=ALU.add), reads=RT, writes=[t_xb[xs_]])
                        fw.op(fw.dve, lambda: V.tensor_tensor(out=car_r[:, k4], in0=rt[0][:, :, L - 1], in1=rt[1][:, :, L - 1], op=ALU.subtract), reads=RT, writes=[t_car[ct]])
                        fw.op(fw.dve, lambda: V.tensor_tensor(out=car_i[:, k4], in0=rt[2][:, :, L - 1], in1=rt[3][:, :, L - 1], op=ALU.add), reads=RT, writes=[t_car[ct]])
                        for kl in range(4):
                            for ri in range(2):
                                fw.op(fw.pe, lambda: P.matmul(pY[:, 0:L], lhsT=Cblk[:, ct * 4 + kl, ri, :], rhs=xb[xs_][:, ri, kl, :],
                                                              start=(kl == 0 and ri == 0), stop=(kl == 3 and ri == 1)),
                                      reads=[t_xb[xs_], t_tab], writes=[tY], signal=(kl == 3 and ri == 1))
                        fw.op(fw.dve, lambda: V.scalar_tensor_tensor(out=yv[:], in0=u_ct, scalar=dvec[:, ct:ct + 1], in1=pY[:, 0:L],
                                                                     op0=ALU.mult, op1=ALU.add),
                              reads=[t_ub[bs], tY, t_tab], writes=[t_yv])
                        fw.op(fw.act, lambda: A.activation(out=e1[:], in_=yv[:], func=AF.Square), reads=[t_yv], writes=[t_e])
                        fw.op(fw.dve, lambda: V.tensor_scalar(out=e1[:], in0=e1[:], scalar1=0.044715, scalar2=1.0, op0=ALU.mult, op1=ALU.add),
                              reads=[t_e], writes=[t_e])
                        fw.op(fw.dve, lambda: V.tensor_tensor(out=e1[:], in0=e1[:], in1=yv[:], op=ALU.mult), reads=[t_e, t_yv], writes=[t_e])
                        fw.op(fw.act, lambda: A.activation(out=e2[:], in_=e1[:], func=AF.Tanh, scale=0.7978845608), reads=[t_e], writes=[t_e])
                        fw.op(fw.dve, lambda: V.scalar_tensor_tensor(out=g2[gs][:, ct, :], in0=e2[:], scalar=1.0, in1=yv[:],
                                                                     op0=ALU.add, op1=ALU.mult),
                              reads=[t_e, t_yv], writes=[t_g2[gs]])
                    for co in range(8):
                        e1, e2, t_e = e12[co % 2], e22[co % 2], t_e2[co % 2]
                        pg = ps[6 + co % 2]
                        tg = t_ps[6 + co % 2]
                        for ci_ in range(8):
                            fw.op(fw.pe, lambda: P.matmul(pg[:, 0:L], lhsT=glu[:, ci_, co * 128:(co + 1) * 128], rhs=g2[gs][:, ci_, :],
                                                          start=(ci_ == 0), stop=(ci_ == 7)),
                                  reads=[t_g2[gs], t_tab], writes=[tg], signal=(ci_ == 7))
                        fw.op(fw.act, lambda: A.activation(out=e1[:], in_=pg[:, 0:L], func=AF.Tanh, scale=0.25, bias=hb[:, co:co + 1]),
                              reads=[tg, t_tab], writes=[t_e])
                        fw.op(fw.act, lambda: A.activation(out=e2[:], in_=ub[bs][:, 8 + co, cs_], func=AF.Tanh, scale=0.5),
                              reads=[t_ub[bs]], writes=[t_e])
                        fw.op(fw.dve, lambda: V.scalar_tensor_tensor(out=e1[:], in0=e1[:], scalar=1.0, in1=g2[gs][:, co, :],
                                                                     op0=ALU.add, op1=ALU.mult), reads=[t_e, t_g2[gs]], writes=[t_e])
                        fw.op(fw.dve, lambda: V.scalar_tensor_tensor(out=e2[:], in0=e2[:], scalar=1.0, in1=ub[bs][:, 8 + co, cs_],
                                                                     op0=ALU.add, op1=ALU.mult), reads=[t_e, t_ub[bs]], writes=[t_e])
                        fw.op(fw.dve, lambda: V.scalar_tensor_tensor(out=ob[bs][:, co, cs_], in0=e1[:], scalar=0.125, in1=e2[:],
                                                                     op0=ALU.mult, op1=ALU.mult), reads=[t_e], writes=[t_ob[bs]])
                fw.dma(fw.act, ydst[:, :, tok0:tok0 + NB], ob[bs][:], reads=[t_ob[bs]], writes=[t_y0], owner=t_ob[bs])
        fw.barrier()
        fw.stack = old


NH = 8
DH = 384
DBG = {}


class StopEmit(Exception):
    pass


def stage(n):
    if DBG.get("stage", 999) <= n:
        raise StopEmit()
HEAD_EPS = 1e-5


def phase_mlstm(fw, p0T, t_p0, prm, y0T, t_y0):
    nc = fw.nc
    V, A, P = nc.vector, nc.scalar, nc.tensor
    L = TS
    NT = 24
    with contextlib.ExitStack() as st:
        old = fw.stack
        fw.stack = st
        Wq = fw.sb("Wq", [128, NT, 128], BF16)
        Wk = fw.sb("Wk", [128, NT, 128], BF16)
        Wv = fw.sb("Wv", [128, NT, 128], BF16)
        Gqk = fw.sb("Gqk", [128, NT, 16], BF16)
        Gv = fw.sb("Gv", [128, NT, 16], BF16)
        cw = fw.sb("cw", [128, 4, NT], F32)
        cb = fw.sb("cbias", [128, NT], F32)
        nw = fw.sb("nw", [128, NT], F32)
        sk = fw.sb("skp", [128, NT], F32)
        bg = fw.sb("bg", [1, 16], BF16)
        ones_b = fw.sb("ones_b", [128, 128], BF16)
        ones_f = fw.sb("ones_f", [128, 128], F32)
        identb = fw.sb("identb", [128, 128], BF16)
        identf = fw.sb("identf", [128, 128], F32)
        trif = fw.sb("trif", [128, L], F32)
        Cst = fw.sb("Cst", [128, NH, 3, DH + 1], F32)
        Cbf = fw.sb("Cbf", [128, NH, 3, DH + 1], BF16)
        m_b = fw.sb("m_b", [128, NH], F32)
        m_c = fw.sb("m_c", [NH, 1], F32)
        t_tab = T("mltab")
        t_C = [T("C%d" % h) for h in range(NH)]
        t_Cb = [T("Cb%d" % h) for h in range(NH)]
        t_m = T("m")
        ps = [fw.ps("mlps%d" % i, [128, 512], F32) for i in range(7)]
        t_ps = [T("mlps%d" % i) for i in range(7)]
        pTb = fw.ps("mlpT", [128, 1024], BF16)
        t_pTb = [T("mlpT%d" % i) for i in range(3)]
        t_psk = T("mlpsk")

        with contextlib.ExitStack() as st2:
            fw.stack = st2
            S = [T("mlsetup")]
            o = lambda eng, fn: fw.op(eng, fn, reads=S, writes=S)
            bd4 = fw.sb("bd4", [128, 128], F32)
            wexp = [fw.sb("wexp%d" % i, [128, NT, 4], F32) for i in range(3)]
            Wraw = fw.sb("Wraw", [128, 128], BF16)
            wgf = fw.sb("wgf", [128, 3, NT, 16], F32)
            wgb = fw.sb("wgb", [128, 3, NT, 16], BF16)
            WT = [fw.sb("WT%d" % i, [128, 128], BF16) for i in range(3)]
            bgf = fw.sb("bgf", [1, 16], F32)
            skf = fw.sb("skf", [128, NT], F32)
            fw.dma(fw.sp, bd4[:], fw.consts["bd4"], writes=S)
            fw.dma(fw.sp, identf[:], fw.consts["ident"], writes=S)
            fw.dma(fw.sp, trif[:L, :], fw.consts["tri"], writes=S)
            for i, nm in enumerate(("ml_wq", "ml_wk", "ml_wv")):
                fw.dma(fw.sp, wexp[i][:], prm[nm].rearrange("n i o -> (n i) o").rearrange("(t p) o -> p t o", p=128), writes=S)
            fw.dma(fw.sp, wgf[:], prm["ml_w_gate"].rearrange("(a t p) g -> p a t g", a=3, p=128), writes=S)
            for k in range(4):
                fw.dma(fw.sp, cw[:, k, :], prm["ml_conv_w"][k, :].rearrange("(t p) -> p t", p=128), writes=S, allow_slow_non_contiguous=True)
            fw.dma(fw.sp, cb[:], prm["ml_conv_b"].rearrange("(t p) -> p t", p=128), writes=S, allow_slow_non_contiguous=True)
            fw.dma(fw.sp, nw[:], prm["ml_norm"].rearrange("(t p) -> p t", p=128), writes=S, allow_slow_non_contiguous=True)
            fw.dma(fw.sp, skf[:], prm["ml_skip"].rearrange("(t p) -> p t", p=128), writes=S, allow_slow_non_contiguous=True)
            fw.dma(fw.sp, bgf[:], prm["ml_b_gate"].rearrange("(a g) -> a g", a=1), writes=S)
            o(fw.dve, lambda: V.tensor_copy(out=bg[:], in_=bgf[:]))
            o(fw.dve, lambda: V.tensor_scalar(out=sk[:], in0=skf[:], scalar1=0.5, scalar2=None, op0=ALU.mult))
            o(fw.dve, lambda: V.tensor_copy(out=wgb[:], in_=wgf[:]))
            o(fw.dve, lambda: V.tensor_copy(out=identb[:], in_=identf[:]))
            o(fw.dve, lambda: V.memset(ones_b[:], 1.0))
            o(fw.dve, lambda: V.memset(ones_f[:], 1.0))
            bdv = bd4[:].rearrange("p (n o) -> p n o", o=4)
            scl = (0.5 * DH ** -0.5, 0.5, 1.0)
            for t in range(NT):
                for i, Wd in enumerate((Wq, Wk, Wv)):
                    o(fw.dve, lambda: V.scalar_tensor_tensor(out=Wd[:, t, :].rearrange("p (n o) -> p n o", o=4),
                                                             in0=wexp[i][:, t, :].unsqueeze(1).to_broadcast([128, 32, 4]),
                                                             scalar=scl[i], in1=bdv, op0=ALU.mult, op1=ALU.mult))
                    o(fw.dve, lambda: V.tensor_tensor(out=Wraw[:].rearrange("p (n o) -> p n o", o=4),
                                                      in0=wexp[i][:, t, :].unsqueeze(1).to_broadcast([128, 32, 4]),
                                                      in1=bdv, op=ALU.mult))
                    o(fw.pe, lambda: P.transpose(pTb[:, 0:128], Wraw[:], identb[:]))
                    o(fw.act, lambda: A.copy(out=WT[i][:], in_=pTb[:, 0:128]))
                pq = ps[t % 4]
                o(fw.pe, lambda: P.matmul(pq[:, 0:16], lhsT=WT[0][:], rhs=wgb[:, 0, t, :], start=True, stop=False))
                o(fw.pe, lambda: P.matmul(pq[:, 0:16], lhsT=WT[1][:], rhs=wgb[:, 1, t, :], start=False, stop=True))
                o(fw.pe, lambda: P.matmul(pq[:, 16:32], lhsT=WT[2][:], rhs=wgb[:, 2, t, :], start=True, stop=True))
                o(fw.act, lambda: A.mul(out=Gqk[:, t, :], in_=pq[:, 0:16], mul=0.5))
                o(fw.act, lambda: A.copy(out=Gv[:, t, :], in_=pq[:, 16:32]))
            fw.op(fw.dve, lambda: V.memset(m_b[:], 0.0), reads=S, writes=S + [t_tab, t_m])
            fw.barrier()
        fw.stack = st

        xblk = [fw.sb("xblk%d" % i, [128, NT, NB + 3], BF16) for i in range(2)]
        zblk = [fw.sb("zblk%d" % i, [128, NT, NB], BF16) for i in range(2)]
        t_xblk = [T("xblk%d" % i) for i in range(2)]
        t_zblk = [T("zblk%d" % i) for i in range(2)]
        accw = fw.sb("caccw", [128, 12, L], F32); t_acc = T("cacc")
        tmpw = fw.sb("ctmpw", [128, 12, L], F32); t_th = T("cth")
        xc2 = fw.sb("xc2", [128, NT, L], BF16); t_xc2 = T("xc2")
        qT = fw.sb("qT", [128, NT, L], BF16); t_qT = T("qT")
        kT = fw.sb("kT", [128, NT, L], BF16); t_kT = T("kT")
        ktok = fw.sb("ktok", [128, NH, DH], BF16); t_ktok = [T("ktok%d" % h) for h in range(NH)]
        vext = fw.sb("vext", [128, NH, DH + 1], BF16); t_vext = [T("vext%d" % h) for h in range(NH)]
        gts = fw.sb("gts", [128, 64], F32); t_g = T("gts")
        g8 = fw.sb("g8", [8, 256], F32); t_g8 = T("g8")
        Mcb = fw.sb("Mcb", [128, 8], F32); dlt = fw.sb("dlt", [128, 8], F32); t_Mc = T("Mcb")
        Sp = [fw.sb("Sp%d" % i, [128, L], BF16) for i in range(2)]; t_Sp = [T("Sp%d" % i) for i in range(2)]
        hh = fw.sb("hh", [128, DH], F32); t_hh = T("hh")
        hst = fw.sb("hst", [128, 16], F32); t_hst = T("hst")
        hn = [fw.sb("hn%d" % i, [128, DH], BF16) for i in range(2)]; t_hn = [T("hn%d" % i) for i in range(2)]
        hjunk = fw.sb("hjunk", [128, DH], BF16); t_hjunk = T("hjunk")
        e1 = fw.sb("mle1", [128, 3, L], F32); e2 = fw.sb("mle2", [128, 3, L], F32); e3 = fw.sb("mle3", [128, 3, L], F32)
        t_e = T("mle")
        ob = [fw.sb("mlob%d" % i, [128, NT, NB], BF16) for i in range(2)]; t_ob = [T("mlob%d" % i) for i in range(2)]
        fw.op(fw.dve, lambda: V.memset(vext[:, :, DH:DH + 1], 1.0), writes=t_vext)
        xsrc = p0T[2048:5120, :].rearrange("(j c) t -> c j t", c=128)
        zsrc = p0T[5120:8192, :].rearrange("(j c) t -> c j t", c=128)
        ydst = y0T[1024:4096, :].rearrange("(j c) t -> c j t", c=128)
        try:
            for seq in range(NSEQ):
                if DBG.get("stage", 999) < 999 and seq > 0:
                    break
                fw.op(fw.dve, lambda: V.memset(Cst[:], 0.0), writes=t_C)
                fw.op(fw.dve, lambda: V.memset(m_b[:], 0.0), writes=[t_m])
                fw.op(fw.dve, lambda: V.memset(m_c[:], 0.0), writes=[t_m])
                for tb in range(NTB):
                    if seq * NTB + tb >= DBG.get("ml_max_blocks", 99):
                        break
                    bs = (seq * NTB + tb) % 2
                    tok0 = seq * TSEQ + tb * NB
                    if tb == 0:
                        fw.op(fw.dve, lambda: V.memset(xblk[bs][:, :, 0:3], 0.0), writes=[t_xblk[bs]])
                        fw.dma(fw.sp, xblk[bs][:, :, 3:NB + 3], xsrc[:, :, tok0:tok0 + NB], reads=[t_p0], writes=[t_xblk[bs]])
                    else:
                        fw.dma(fw.sp, xblk[bs][:, :, :], xsrc[:, :, tok0 - 3:tok0 + NB], reads=[t_p0], writes=[t_xblk[bs]])
                    fw.dma(fw.sp, zblk[bs][:], zsrc[:, :, tok0:tok0 + NB], reads=[t_p0], writes=[t_zblk[bs]])
                    for j in range(3):
                        c0 = j * L
                        XB, ZB = xblk[bs], zblk[bs]
                        tXB, tZB = t_xblk[bs], t_zblk[bs]
                        for hf in range(2):
                            ts_ = slice(hf * 12, hf * 12 + 12)
                            bw = lambda k: cw[:, k, ts_].unsqueeze(2).to_broadcast([128, 12, L])
                            fw.op(fw.dve, lambda: V.tensor_tensor(out=accw[:], in0=XB[:, ts_, c0:c0 + L], in1=bw(0), op=ALU.mult),
                                  reads=[tXB, t_tab], writes=[t_acc])
                            for k in range(1, 4):
                                fw.op(fw.dve, lambda: V.tensor_tensor(out=tmpw[:], in0=XB[:, ts_, c0 + k:c0 + k + L], in1=bw(k), op=ALU.mult),
                                      reads=[tXB, t_tab], writes=[t_th])
                                fw.op(fw.dve, lambda: V.tensor_tensor(out=accw[:], in0=accw[:], in1=tmpw[:], op=ALU.add),
                                      reads=[t_th, t_acc], writes=[t_acc])
                            fw.op(fw.dve, lambda: V.tensor_tensor(out=accw[:], in0=accw[:], in1=cb[:, ts_].unsqueeze(2).to_broadcast([128, 12, L]), op=ALU.add),
                                  reads=[t_acc, t_tab], writes=[t_acc])
                            fw.op(fw.act, lambda: A.activation(out=tmpw[:], in_=accw[:], func=AF.Tanh, scale=0.5), reads=[t_acc], writes=[t_th])
                            fw.op(fw.dve, lambda: V.scalar_tensor_tensor(out=xc2[:, ts_, :], in0=tmpw[:], scalar=1.0, in1=accw[:], op0=ALU.add, op1=ALU.mult),
                                  reads=[t_th, t_acc], writes=[t_xc2])
                        try_stage = stage(1)
                        for t4 in range(NT // 4):
                            for i4 in range(4):
                                t = 4 * t4 + i4
                                fw.op(fw.pe, lambda: P.matmul(ps[0][:, i4 * L:(i4 + 1) * L], lhsT=Wq[:, t, :], rhs=xc2[:, t, :], start=True, stop=True),
                                      reads=[t_xc2, t_tab], writes=[t_ps[0]], signal=(i4 == 3))
                            fw.op(fw.act, lambda: A.copy(out=qT[:, 4 * t4:4 * t4 + 4, :], in_=ps[0][:, 0:4 * L].rearrange("p (a t) -> p a t", a=4)),
                                  reads=[t_ps[0]], writes=[t_qT])
                            for i4 in range(4):
                                t = 4 * t4 + i4
                                fw.op(fw.pe, lambda: P.matmul(ps[1][:, i4 * L:(i4 + 1) * L], lhsT=Wk[:, t, :], rhs=xc2[:, t, :], start=True, stop=True),
                                      reads=[t_xc2, t_tab], writes=[t_ps[1]], signal=(i4 == 3))
                            fw.op(fw.dve, lambda: V.tensor_copy(out=kT[:, 4 * t4:4 * t4 + 4, :], in_=ps[1][:, 0:4 * L].rearrange("p (a t) -> p a t", a=4)),
                                  reads=[t_ps[1]], writes=[t_kT])
                        try_stage = stage(2)
                        pg = ps[3]
                        tg = t_ps[3]
                        for t in range(NT):
                            fw.op(fw.pe, lambda: P.matmul(pg[:L, 0:16], lhsT=xc2[:, t, :], rhs=Gqk[:, t, :], start=(t == 0), stop=False),
                                  reads=[t_xc2, t_tab], writes=[tg], signal=False)
                            fw.op(fw.pe, lambda: P.matmul(pg[:L, 0:16], lhsT=XB[:, t, c0 + 3:c0 + 3 + L], rhs=Gv[:, t, :], start=False, stop=False),
                                  reads=[tXB, t_tab], writes=[tg], signal=False)
                        fw.op(fw.pe, lambda: P.matmul(pg[:L, 0:16], lhsT=ones_b[0:1, :L], rhs=bg[0:1, :], start=False, stop=True),
                              reads=[t_tab], writes=[tg])
                        try_stage = stage(3)
                        G = [t_g]
                        fw.op(fw.act, lambda: A.activation(out=gts[:L, 8:16], in_=pg[:L, 8:16], func=AF.Exp, scale=-1.0), reads=[tg], writes=G)
                        fw.op(fw.act, lambda: A.activation(out=gts[:L, 8:16], in_=gts[:L, 8:16], func=AF.Ln, bias=1.0), reads=G, writes=G)
                        fw.op(fw.dve, lambda: V.tensor_scalar(out=gts[:L, 8:16], in0=gts[:L, 8:16], scalar1=-1.0, scalar2=None, op0=ALU.mult), reads=G, writes=G)
                        fw.op(fw.dve, lambda: V.tensor_copy(out=gts[:L, 0:8], in_=pg[:L, 0:8]), reads=[tg], writes=G)
                        try_stage = stage(4)
                        fw.op(fw.pe, lambda: P.matmul(pg[:L, 16:24], lhsT=trif[:L, :L], rhs=gts[:L, 8:16], start=True, stop=True), reads=G + [t_tab], writes=[tg])
                        fw.op(fw.pe, lambda: P.matmul(pg[:, 24:32], lhsT=ones_f[:L, :], rhs=gts[:L, 8:16], start=True, stop=True), reads=G + [t_tab], writes=[tg])
                        fw.op(fw.pe, lambda: P.matmul(pg[0:8, 32:33], lhsT=gts[:L, 8:16], rhs=ones_f[:L, 0:1], start=True, stop=True), reads=G + [t_tab], writes=[tg])
                        fw.op(fw.dve, lambda: V.tensor_copy(out=gts[:L, 40:48], in_=pg[:L, 16:24]), reads=[tg], writes=G)
                        fw.op(fw.dve, lambda: V.tensor_tensor(out=gts[:L, 16:24], in0=gts[:L, 0:8], in1=gts[:L, 40:48], op=ALU.subtract), reads=G, writes=G)
                        try_stage = stage(5)
                        fw.op(fw.pe, lambda: P.transpose(pg[0:8, 64:64 + L], gts[:L, 16:24], identf[:L, :L]), reads=G + [t_tab], writes=[tg])
                        G8 = [t_g8]
                        fw.op(fw.dve, lambda: V.reduce_max(out=g8[:, 0:1], in_=pg[0:8, 64:64 + L], axis=AX.X), reads=[tg], writes=G8)
                        fw.op(fw.dve, lambda: V.tensor_tensor(out=g8[:, 1:2], in0=g8[:, 0:1], in1=m_c[:, 0:1], op=ALU.max), reads=G8 + [t_m], writes=G8)
                        fw.op(fw.dve, lambda: V.tensor_scalar(out=g8[:, 8:16], in0=identf[0:8, 0:8], scalar1=g8[:, 1:2], scalar2=None, op0=ALU.mult), reads=G8 + [t_tab], writes=G8)
                        fw.op(fw.pe, lambda: P.matmul(pg[:, 40:48], lhsT=ones_f[0:8, :], rhs=g8[:, 8:16], start=True, stop=True), reads=G8 + [t_tab], writes=[tg])
                        try_stage = stage(6)
                        MC = [t_Mc]
                        fw.op(fw.dve, lambda: V.tensor_copy(out=Mcb[:], in_=pg[:, 40:48]), reads=[tg], writes=MC)
                        fw.op(fw.dve, lambda: V.tensor_tensor(out=dlt[:], in0=m_b[:], in1=Mcb[:], op=ALU.subtract), reads=MC + [t_m], writes=MC)
                        fw.op(fw.act, lambda: A.activation(out=dlt[:], in_=dlt[:], func=AF.Exp), reads=MC, writes=MC)
                        fw.op(fw.dve, lambda: V.tensor_tensor(out=m_b[:], in0=pg[:, 24:32], in1=Mcb[:], op=ALU.add), reads=MC + [tg], writes=[t_m])
                        fw.op(fw.dve, lambda: V.tensor_tensor(out=m_c[:, 0:1], in0=pg[0:8, 32:33], in1=g8[:, 1:2], op=ALU.add), reads=G8 + [tg], writes=[t_m])
                        fw.op(fw.dve, lambda: V.tensor_tensor(out=gts[:L, 24:32], in0=gts[:L, 16:24], in1=Mcb[:L, :], op=ALU.subtract), reads=G + MC, writes=G)
                        fw.op(fw.act, lambda: A.activation(out=gts[:L, 24:32], in_=gts[:L, 24:32], func=AF.Exp), reads=G, writes=G)
                        fw.op(fw.dve, lambda: V.tensor_tensor(out=gts[:L, 32:40], in0=gts[:L, 40:48], in1=Mcb[:L, :], op=ALU.add), reads=G + MC, writes=G)
                        fw.op(fw.act, lambda: A.activation(out=gts[:L, 32:40], in_=gts[:L, 32:40], func=AF.Exp, scale=-1.0), reads=G, writes=G)
                        try_stage = stage(7)
                        for h in range(NH):
                            pkv = ps[1]
                            tkv = t_ps[1]
                            for jj in range(3):
                                t = 3 * h + jj
                                fw.op(fw.pe, lambda: P.matmul(pkv[:L, jj * 128:(jj + 1) * 128], lhsT=xc2[:, t, :], rhs=Wk[:, t, :], start=True, stop=True),
                                      reads=[t_xc2, t_tab], writes=[tkv], signal=(jj == 2))
                            fw.op(fw.act, lambda: A.activation(out=ktok[:L, h, :], in_=pkv[:L, 0:DH], func=AF.Copy, scale=gts[:L, 24 + h:25 + h]),
                                  reads=[tkv] + G, writes=[t_ktok[h]])
                            pv = ps[2]
                            tv = t_ps[2]
                            for jj in range(3):
                                t = 3 * h + jj
                                fw.op(fw.pe, lambda: P.matmul(pv[:L, jj * 128:(jj + 1) * 128], lhsT=XB[:, t, c0 + 3:c0 + 3 + L], rhs=Wv[:, t, :], start=True, stop=True),
                                      reads=[tXB, t_tab], writes=[tv], signal=(jj == 2))
                            fw.op(fw.dve, lambda: V.tensor_copy(out=vext[:L, h, 0:DH], in_=pv[:L, 0:DH]), reads=[tv], writes=[t_vext[h]])
                            try_stage = stage(8)
                            pS = ps[3]
                            tS = t_ps[3]
                            for jj in range(3):
                                t = 3 * h + jj
                                fw.op(fw.pe, lambda: P.matmul(pS[:L, 0:L], lhsT=kT[:, t, :], rhs=qT[:, t, :], start=(jj == 0), stop=(jj == 2)),
                                      reads=[t_kT, t_qT], writes=[tS], signal=(jj == 2))
                            sp = h % 2
                            fw.op(fw.dve, lambda: V.scalar_tensor_tensor(out=Sp[sp][:L, :], in0=pS[:L, 0:L], scalar=gts[:L, 24 + h:25 + h],
                                                                         in1=trif[:L, :], op0=ALU.mult, op1=ALU.mult),
                                  reads=[tS, t_tab] + G, writes=[t_Sp[sp]])
                            try_stage = stage(9)
                            for jj in range(3):
                                fw.op(fw.act, lambda: A.activation(out=Cbf[:, h, jj, :], in_=Cst[:, h, jj, :], func=AF.Copy, scale=dlt[:, h:h + 1]),
                                      reads=[t_C[h]] + MC, writes=[t_Cb[h]])
                            pX = ps[4]
                            tX = t_ps[4]
                            fw.op(fw.pe, lambda: P.matmul(pX[:L, 0:DH + 1], lhsT=Sp[sp][:L, :], rhs=vext[:L, h, :], start=True, stop=False),
                                  reads=[t_Sp[sp], t_vext[h]], writes=[tX], signal=False)
                            for jj in range(3):
                                fw.op(fw.pe, lambda: P.matmul(pX[:L, 0:DH + 1], lhsT=qT[:, 3 * h + jj, :], rhs=Cbf[:, h, jj, :], start=False, stop=(jj == 2)),
                                      reads=[t_qT, t_Cb[h]], writes=[tX], signal=(jj == 2))
                            try_stage = stage(10)
                            for jj in range(3):
                                pc = ps[5 + jj % 2]
                                tc = t_ps[5 + jj % 2]
                                fw.op(fw.pe, lambda: P.matmul(pc[:, 0:DH + 1], lhsT=ktok[:L, h, jj * 128:(jj + 1) * 128], rhs=vext[:L, h, :], start=True, stop=True),
                                      reads=[t_ktok[h], t_vext[h]], writes=[tc])
                                fw.op(fw.dve, lambda: V.scalar_tensor_tensor(out=Cst[:, h, jj, :], in0=Cst[:, h, jj, :], scalar=dlt[:, h:h + 1],
                                                                             in1=pc[:, 0:DH + 1], op0=ALU.mult, op1=ALU.add),
                                      reads=[tc, t_C[h], t_Cb[h]] + MC, writes=[t_C[h]])
                            try_stage = stage(11)
                            H = [t_hst]
                            fw.op(fw.act, lambda: A.activation(out=hst[:L, 0:1], in_=pX[:L, DH:DH + 1], func=AF.Abs), reads=[tX], writes=H)
                            fw.op(fw.dve, lambda: V.tensor_tensor(out=hst[:L, 0:1], in0=hst[:L, 0:1], in1=gts[:L, 32 + h:33 + h], op=ALU.max),
                                  reads=H + G, writes=H)
                            fw.op(fw.dve, lambda: V.reciprocal(out=hst[:L, 1:2], in_=hst[:L, 0:1]), reads=H, writes=H)
                            fw.op(fw.dve, lambda: V.tensor_scalar(out=hh[:L, :], in0=pX[:L, 0:DH], scalar1=hst[:L, 1:2], scalar2=None, op0=ALU.mult),
                                  reads=[tX] + H, writes=[t_hh])
                            try_stage = stage(12)
                            fw.op(fw.dve, lambda: V.reduce_sum(out=hst[:L, 2:3], in_=hh[:L, :], axis=AX.X), reads=[t_hh], writes=H)
                            fw.op(fw.act, lambda: A.activation(out=hjunk[:L, :], in_=hh[:L, :], func=AF.Square,
                                                              accum_out=hst[:L, 3:4]), reads=[t_hh], writes=H + [t_hjunk])
                            fw.op(fw.dve, lambda: V.tensor_scalar(out=hst[:L, 4:5], in0=hst[:L, 2:3], scalar1=1.0 / DH, scalar2=None, op0=ALU.mult), reads=H, writes=H)
                            fw.op(fw.dve, lambda: V.tensor_tensor(out=hst[:L, 5:6], in0=hst[:L, 4:5], in1=hst[:L, 4:5], op=ALU.mult), reads=H, writes=H)
                            fw.op(fw.dve, lambda: V.scalar_tensor_tensor(out=hst[:L, 6:7], in0=hst[:L, 3:4], scalar=1.0 / DH, in1=hst[:L, 5:6],
                                                                         op0=ALU.mult, op1=ALU.subtract), reads=H, writes=H)
                            fw.op(fw.act, lambda: A.activation(out=hst[:L, 7:8], in_=hst[:L, 6:7], func=AF.Ln, bias=HEAD_EPS), reads=H, writes=H)
                            fw.op(fw.act, lambda: A.activation(out=hst[:L, 8:9], in_=hst[:L, 7:8], func=AF.Exp, scale=-0.5), reads=H, writes=H)
                            fw.op(fw.dve, lambda: V.tensor_scalar(out=hn[h % 2][:L, :], in0=hh[:L, :], scalar1=hst[:L, 4:5], scalar2=hst[:L, 8:9],
                                                                  op0=ALU.subtract, op1=ALU.mult), reads=[t_hh] + H, writes=[t_hn[h % 2]])
                            try_stage = stage(13)
                            t3 = slice(3 * h, 3 * h + 3)
                            for jj in range(3):
                                fw.op(fw.pe, lambda: P.transpose(pTb[:, jj * L:(jj + 1) * L], hn[h % 2][:L, jj * 128:(jj + 1) * 128], identb[:L, :L]),
                                      reads=[t_hn[h % 2], t_tab], writes=[t_pTb[0]], signal=(jj == 2))
                            E = [t_e]
                            pT3 = pTb[:, 0:3 * L].rearrange("p (a t) -> p a t", a=3)
                            fw.op(fw.dve, lambda: V.tensor_tensor(out=e1[:], in0=xc2[:, t3, :], in1=sk[:, t3].unsqueeze(2).to_broadcast([128, 3, L]), op=ALU.mult),
                                  reads=[t_xc2, t_tab], writes=E)
                            fw.op(fw.dve, lambda: V.tensor_tensor(out=e3[:], in0=pT3, in1=nw[:, t3].unsqueeze(2).to_broadcast([128, 3, L]), op=ALU.mult),
                                  reads=[t_pTb[0], t_tab], writes=E)
                            fw.op(fw.dve, lambda: V.tensor_tensor(out=e1[:], in0=e1[:], in1=e3[:], op=ALU.add), reads=E, writes=E)
                            fw.op(fw.act, lambda: A.activation(out=e2[:], in_=ZB[:, t3, c0:c0 + L], func=AF.Tanh, scale=0.5), reads=[tZB], writes=E)
                            fw.op(fw.dve, lambda: V.scalar_tensor_tensor(out=e2[:], in0=e2[:], scalar=1.0, in1=ZB[:, t3, c0:c0 + L],
                                                                         op0=ALU.add, op1=ALU.mult), reads=E + [tZB], writes=E)
                            fw.op(fw.dve, lambda: V.scalar_tensor_tensor(out=ob[bs][:, t3, c0:c0 + L], in0=e1[:], scalar=0.5, in1=e2[:],
                                                                         op0=ALU.mult, op1=ALU.mult), reads=E, writes=[t_ob[bs]])
                    fw.dma(fw.act, ydst[:, :, tok0:tok0 + NB], ob[bs][:], reads=[t_ob[bs]], writes=[t_y0], owner=t_ob[bs])
        except StopEmit:
            pass
        fw.barrier()
        fw.stack = old


def phase_ssd(fw, p1T, t_p1, dtT_d, prm, y1T, t_y1):
    nc = fw.nc
    V, A, P = nc.vector, nc.scalar, nc.tensor
    L = TS
    NX = 48
    HP = 64
    with contextlib.ExitStack() as st:
        old = fw.stack
        fw.stack = st
        cw = fw.sb("scw", [128, 4, NX], F32)
        cb = fw.sb("scb", [128, NX], F32)
        dtb = fw.sb("dtb", [64, 1], F32)
        aneg = fw.sb("aneg", [64, 1], F32)
        D_b = fw.sb("D_b", [128, 64], F32)
        gn = fw.sb("gn", [128, 32], F32)
        Sel = fw.sb("Sel", [64, 64, L], F32)
        maskb = fw.sb("maskb", [128, L], F32)
        trif = fw.sb("trif", [128, L], F32)
        identf = fw.sb("identf", [128, 128], F32)
        identb = fw.sb("identb", [128, 128], BF16)
        ones_f = fw.sb("ones_f", [128, 128], F32)
        stT = fw.sb("stT", [128, 64, HP], F32)
        stb = fw.sb("stb", [128, 64, HP], BF16)
        t_tab = T("ssdtab")
        t_st = [T("st%d" % g) for g in range(8)]
        t_stb = [T("stb%d" % g) for g in range(8)]
        pf = [fw.ps("sdps%d" % i, [128, 512], F32) for i in range(6)]
        t_pf = [T("sdps%d" % i) for i in range(6)]
        pb = [fw.ps("sdpb%d" % i, [128, 1024], BF16) for i in range(2)]
        t_pb = [T("sdpb%d" % i) for i in range(2)]
        S = [T("ssdsetup")]
        o = lambda eng, fn: fw.op(eng, fn, reads=S, writes=S)
        with contextlib.ExitStack() as st2:
            fw.stack = st2
            tmpf = fw.sb("stmpf", [128, NX], F32)
            alog = fw.sb("alog", [64, 1], F32)
            fw.dma(fw.sp, identf[:], fw.consts["ident"], writes=S)
            fw.dma(fw.sp, trif[:L, :], fw.consts["tri"], writes=S)
            for k in range(4):
                fw.dma(fw.sp, cw[:, k, :], prm["ssd_conv_w"][k, :].rearrange("(t p) -> p t", p=128), writes=S, allow_slow_non_contiguous=True)
            fw.dma(fw.sp, cb[:], prm["ssd_conv_b"].rearrange("(t p) -> p t", p=128), writes=S, allow_slow_non_contiguous=True)
            fw.dma(fw.sp, gn[:], prm["ssd_gnorm"].rearrange("(t p) -> p t", p=128), writes=S, allow_slow_non_contiguous=True)
            fw.dma(fw.sp, dtb[:], prm["ssd_dt_bias"].rearrange("(h a) -> h a", a=1), writes=S)
            fw.dma(fw.sp, alog[:], prm["ssd_a_log"].rearrange("(h a) -> h a", a=1), writes=S)
            fw.dma(fw.sp, D_b[:], prm["ssd_d"].partition_broadcast(128), writes=S)
            o(fw.dve, lambda: V.tensor_scalar(out=cw[:], in0=cw[:], scalar1=0.5, scalar2=None, op0=ALU.mult))
            o(fw.dve, lambda: V.tensor_scalar(out=cb[:], in0=cb[:], scalar1=0.5, scalar2=None, op0=ALU.mult))
            o(fw.act, lambda: A.activation(out=aneg[:], in_=alog[:], func=AF.Exp))
            o(fw.dve, lambda: V.tensor_scalar(out=aneg[:], in0=aneg[:], scalar1=-1.0, scalar2=None, op0=ALU.mult))
            o(fw.dve, lambda: V.tensor_copy(out=identb[:], in_=identf[:]))
            o(fw.dve, lambda: V.memset(ones_f[:], 1.0))
            o(fw.dve, lambda: V.tensor_copy(out=Sel[:], in_=identf[0:64, 0:64].unsqueeze(2).to_broadcast([64, 64, L])))
            o(fw.dve, lambda: V.tensor_scalar(out=maskb[:L, :], in0=trif[:L, :], scalar1=-1.0, scalar2=30000.0, op0=ALU.add, op1=ALU.mult))
            fw.op(fw.dve, lambda: V.memset(stT[:], 0.0), reads=S, writes=S + [t_tab] + t_st)
            fw.barrier()
        fw.stack = st

        xblk = fw.sb("sxblk", [128, NX, NB + 3], BF16); t_xblk = T("sxblk")
        zblk = fw.sb("szblk", [128, 32, NB], BF16); t_zblk = T("szblk")
        dblk = fw.sb("sdblk", [64, NB], F32); t_dblk = T("sdblk")
        ob = fw.sb("sob", [128, 32, NB], BF16); t_ob = T("sob")
        accw = fw.sb("saccw", [128, 24, L], F32); t_acc = T("sacc")
        tmpw = fw.sb("stmpw", [128, 24, L], F32); t_th = T("sth")
        maskb4 = fw.sb("maskb4", [128, 4, L], F32)
        xc = fw.sb("sxc", [128, NX, L], BF16); t_xc = T("sxc")
        xtok = fw.sb("xtok", [128, 40 * 128], BF16); t_xtok = T("xtok")
        fT = fw.sb("fT", [64, 4, L], F32); t_fT = T("fT")
        tk = fw.sb("tk", [128, 5, 64], F32); t_tk = T("tk")
        dg = fw.sb("sdg", [64, 64], F32); t_dg = T("sdg")
        elb = fw.sb("elb", [128, 64], F32); lastb = fw.sb("lastb", [128, 64], F32); t_el = T("elb")
        cbT = fw.sb("cbT", [128, L], F32); t_cbT = T("cbT")
        Eh = [fw.sb("Eh%d" % i, [128, 4 * L], F32) for i in range(2)]; t_Eh = [T("Eh%d" % i) for i in range(2)]
        Wh = [fw.sb("Wh%d" % i, [128, 4 * L], BF16) for i in range(2)]; t_Wh = [T("Wh%d" % i) for i in range(2)]
        ys = fw.sb("sys", [128, 512], F32); xd = fw.sb("sxd", [128, 512], F32); t_ys = T("sys")
        xsd = fw.sb("xsd", [128, 512], BF16); t_xsd = T("xsd")
        gz = fw.sb("sgz", [128, 512], F32); t_gz = T("sgz")
        yz = fw.sb("syz", [128, 8, 512], BF16); t_yz = T("syz")
        sq = fw.sb("ssq", [128, 512], BF16); t_sq = T("ssq")
        nst = fw.sb("snst", [128, 32], F32); t_nst = T("snst")
        yn = fw.sb("syn", [128, 512], BF16); t_yn = T("syn")
        xsrc = p1T[4096:10240, :].rearrange("(j c) t -> c j t", c=128)
        zsrc = p1T[0:4096, :].rearrange("(j c) t -> c j t", c=128)
        ydst = y1T.rearrange("(j c) t -> c j t", c=128)
        fw.op(fw.dve, lambda: V.tensor_copy(out=maskb4[:L, :, :], in_=maskb[:L, :].unsqueeze(1).to_broadcast([L, 4, L])), reads=[t_tab], writes=[t_tab])
        try:
            for seq in range(NSEQ):
                if seq > 0:
                    fw.op(fw.dve, lambda: V.memset(stT[:], 0.0), writes=t_st)
                fw.op(fw.dve, lambda: V.memset(stb[:], 0.0), writes=t_stb)
                for tb in range(NTB):
                    if seq * NTB + tb >= DBG.get("ssd_max_blocks", 99):
                        raise StopEmit()
                    tok0 = seq * TSEQ + tb * NB
                    if tb == 0:
                        fw.op(fw.dve, lambda: V.memset(xblk[:, :, 0:3], 0.0), writes=[t_xblk])
                        fw.dma(fw.sp, xblk[:, 0:24, 3:NB + 3], xsrc[:, 0:24, tok0:tok0 + NB], reads=[t_p1], writes=[t_xblk])
                        fw.dma(fw.sp, xblk[:, 24:48, 3:NB + 3], xsrc[:, 24:48, tok0:tok0 + NB], reads=[t_p1], writes=[t_xblk])
                    else:
                        fw.dma(fw.sp, xblk[:, 0:24, :], xsrc[:, 0:24, tok0 - 3:tok0 + NB], reads=[t_p1], writes=[t_xblk])
                        fw.dma(fw.sp, xblk[:, 24:48, :], xsrc[:, 24:48, tok0 - 3:tok0 + NB], reads=[t_p1], writes=[t_xblk])
                    fw.dma(fw.sp, zblk[:], zsrc[:, :, tok0:tok0 + NB], reads=[t_p1], writes=[t_zblk])
                    fw.dma(fw.sp, dblk[:], dtT_d[:, tok0:tok0 + NB], reads=[t_p1], writes=[t_dblk])
                    for j in range(3):
                        c0 = j * L
                        for hf in range(2):
                            ts_ = slice(hf * 24, hf * 24 + 24)
                            bw = lambda k: cw[:, k, ts_].unsqueeze(2).to_broadcast([128, 24, L])
                            fw.op(fw.dve, lambda: V.tensor_tensor(out=accw[:], in0=xblk[:, ts_, c0:c0 + L], in1=bw(0), op=ALU.mult),
                                  reads=[t_xblk, t_tab], writes=[t_acc])
                            for k in range(1, 4):
                                fw.op(fw.dve, lambda: V.tensor_tensor(out=tmpw[:], in0=xblk[:, ts_, c0 + k:c0 + k + L], in1=bw(k), op=ALU.mult),
                                      reads=[t_xblk, t_tab], writes=[t_th])
                                fw.op(fw.dve, lambda: V.tensor_tensor(out=accw[:], in0=accw[:], in1=tmpw[:], op=ALU.add),
                                      reads=[t_th, t_acc], writes=[t_acc])
                            fw.op(fw.dve, lambda: V.tensor_tensor(out=accw[:], in0=accw[:], in1=cb[:, ts_].unsqueeze(2).to_broadcast([128, 24, L]), op=ALU.add),
                                  reads=[t_acc, t_tab], writes=[t_acc])
                            fw.op(fw.act, lambda: A.activation(out=tmpw[:], in_=accw[:], func=AF.Tanh), reads=[t_acc], writes=[t_th])
                            fw.op(fw.dve, lambda: V.scalar_tensor_tensor(out=xc[:, ts_, :], in0=tmpw[:], scalar=1.0, in1=accw[:], op0=ALU.add, op1=ALU.mult),
                                  reads=[t_th, t_acc], writes=[t_xc])
                        stage(21)
                        F_ = [t_fT]
                        fw.op(fw.act, lambda: A.activation(out=fT[:, 0, :], in_=dblk[:, c0:c0 + L], func=AF.Exp, bias=dtb[:, 0:1]), reads=[t_dblk, t_tab], writes=F_)
                        fw.op(fw.act, lambda: A.activation(out=fT[:, 0, :], in_=fT[:, 0, :], func=AF.Ln, bias=1.0), reads=F_, writes=F_)
                        fw.op(fw.dve, lambda: V.tensor_scalar(out=fT[:, 1, :], in0=fT[:, 0, :], scalar1=aneg[:, 0:1], scalar2=None, op0=ALU.mult), reads=F_ + [t_tab], writes=F_)
                        fw.op(fw.dve, lambda: V.tensor_tensor_scan(out=fT[:, 2, :], data0=ones_f[0:64, 0:L], data1=fT[:, 1, :], initial=0.0, op0=ALU.mult, op1=ALU.add),
                              reads=F_ + [t_tab], writes=F_)
                        fw.op(fw.dve, lambda: V.tensor_scalar(out=fT[:, 3, :], in0=fT[:, 2, :], scalar1=-1.0, scalar2=None, op0=ALU.mult), reads=F_, writes=F_)
                        pm = pf[5]; tm = t_pf[5]
                        fw.op(fw.pe, lambda: P.transpose(pm[:L, 0:64], fT[:, 0, :], identf[0:64, 0:64]), reads=F_ + [t_tab], writes=[tm])
                        fw.op(fw.pe, lambda: P.transpose(pm[:L, 64:128], fT[:, 2, :], identf[0:64, 0:64]), reads=F_ + [t_tab], writes=[tm])
                        fw.op(fw.dve, lambda: V.tensor_scalar(out=dg[:], in0=identf[0:64, 0:64], scalar1=fT[:, 2, L - 1:L], scalar2=None, op0=ALU.mult), reads=F_ + [t_tab], writes=[t_dg])
                        fw.op(fw.pe, lambda: P.matmul(pm[:, 128:192], lhsT=ones_f[0:64, :], rhs=dg[:], start=True, stop=True), reads=[t_dg, t_tab], writes=[tm])
                        K_ = [t_tk]
                        fw.op(fw.dve, lambda: V.tensor_copy(out=tk[:L, 0, :], in_=pm[:L, 0:64]), reads=[tm], writes=K_)
                        fw.op(fw.dve, lambda: V.tensor_copy(out=tk[:L, 4, :], in_=pm[:L, 64:128]), reads=[tm], writes=K_)
                        fw.op(fw.dve, lambda: V.tensor_scalar(out=tk[:L, 1, :], in0=pm[:L, 64:128], scalar1=-1.0, scalar2=None, op0=ALU.mult), reads=[tm], writes=K_)
                        fw.op(fw.act, lambda: A.activation(out=tk[:L, 2, :], in_=pm[:L, 64:128], func=AF.Exp), reads=[tm], writes=K_)
                        fw.op(fw.dve, lambda: V.tensor_copy(out=lastb[:], in_=pm[:, 128:192]), reads=[tm] + K_, writes=[t_el])
                        fw.op(fw.act, lambda: A.activation(out=elb[:], in_=lastb[:], func=AF.Exp), reads=[t_el], writes=[t_el])
                        fw.op(fw.dve, lambda: V.tensor_tensor(out=tk[:L, 3, :], in0=lastb[:L, :], in1=tk[:L, 4, :], op=ALU.subtract), reads=[t_el] + K_, writes=K_)
                        fw.op(fw.act, lambda: A.activation(out=tk[:L, 3, :], in_=tk[:L, 3, :], func=AF.Exp), reads=K_, writes=K_)
                        fw.op(fw.dve, lambda: V.tensor_tensor(out=tk[:L, 3, :], in0=tk[:L, 3, :], in1=tk[:L, 0, :], op=ALU.mult), reads=K_, writes=K_)
                        stage(22)
                        for r5 in range(5):
                            pbb = pb[r5 % 2]; tpb = t_pb[r5 % 2]
                            for i8 in range(8):
                                t = r5 * 8 + i8
                                fw.op(fw.pe, lambda: P.transpose(pbb[:L, i8 * 128:(i8 + 1) * 128], xc[:, t, :], identb[:, :]),
                                      reads=[t_xc, t_tab], writes=[tpb], signal=(i8 == 7))
                            if r5 % 2 == 0:
                                fw.op(fw.act, lambda: A.copy(out=xtok[:L, r5 * 1024:(r5 + 1) * 1024], in_=pbb[:L, :]), reads=[tpb], writes=[t_xtok])
                            else:
                                fw.op(fw.dve, lambda: V.tensor_copy(out=xtok[:L, r5 * 1024:(r5 + 1) * 1024], in_=pbb[:L, :]), reads=[tpb], writes=[t_xtok])
                        stage(23)
                        for g in range(8):
                            bmf = xc[:, 32 + g, :]
                            cmf = xc[:, 40 + g, :]
                            fw.op(fw.pe, lambda: P.matmul(pf[0][:L, 0:L], lhsT=bmf, rhs=cmf, start=True, stop=True), reads=[t_xc], writes=[t_pf[0]])
                            fw.op(fw.act, lambda: A.copy(out=cbT[:L, :], in_=pf[0][:L, 0:L]), reads=[t_pf[0]], writes=[t_cbT])
                            fw.op(fw.pe, lambda: P.matmul(pf[3][:L, :], lhsT=cmf, rhs=stb[:, 8 * g:8 * g + 8, :].rearrange("n h p -> n (h p)"), start=True, stop=True),
                                  reads=[t_xc, t_stb[g]], writes=[t_pf[3]])
                            for q4 in range(2):
                                h0_ = 8 * g + 4 * q4
                                e = q4 % 2
                                fw.op(fw.pe, lambda: P.matmul(pf[1][:L, 0:4 * L], lhsT=fT[:, 3, :], rhs=Sel[:, h0_:h0_ + 4, :].rearrange("k h t -> k (h t)"),
                                                              start=True, stop=False), reads=F_ + [t_tab], writes=[t_pf[1]], signal=False)
                                fw.op(fw.pe, lambda: P.matmul(pf[1][:L, 0:4 * L], lhsT=identf[:L, :L], rhs=maskb4[:L, :, :].rearrange("s h t -> s (h t)"),
                                                              start=False, stop=False), reads=[t_tab], writes=[t_pf[1]], signal=False)
                                for i4 in range(4):
                                    fw.op(fw.pe, lambda: P.matmul(pf[1][:L, i4 * L:(i4 + 1) * L], lhsT=Sel[:, h0_ + i4, :], rhs=fT[:, 2, :], start=False, stop=(i4 == 3)),
                                          reads=F_ + [t_tab], writes=[t_pf[1]], signal=(i4 == 3))
                                fw.op(fw.act, lambda: A.activation(out=Eh[e][:L, :], in_=pf[1][:L, 0:4 * L], func=AF.Exp), reads=[t_pf[1]], writes=[t_Eh[e]])
                                Ev = Eh[e][:L, :].rearrange("s (h t) -> s h t", h=4)
                                fw.op(fw.dve, lambda: V.tensor_tensor(out=Ev, in0=Ev, in1=tk[:L, 0, h0_:h0_ + 4].unsqueeze(2).to_broadcast([L, 4, L]), op=ALU.mult),
                                      reads=[t_Eh[e]] + K_, writes=[t_Eh[e]])
                                fw.op(fw.dve, lambda: V.tensor_tensor(out=Wh[e][:L, :].rearrange("s (h t) -> s h t", h=4), in0=Ev,
                                                                      in1=cbT[:L, :].unsqueeze(1).to_broadcast([L, 4, L]), op=ALU.mult),
                                      reads=[t_Eh[e], t_cbT], writes=[t_Wh[e]])
                                for i4 in range(4):
                                    hh_ = h0_ + i4
                                    hl = 4 * q4 + i4
                                    fw.op(fw.pe, lambda: P.matmul(pf[2][:L, hl * HP:(hl + 1) * HP], lhsT=Wh[e][:L, i4 * L:(i4 + 1) * L], rhs=xtok[:L, hh_ * HP:(hh_ + 1) * HP], start=True, stop=True),
                                          reads=[t_Wh[e], t_xtok], writes=[t_pf[2]], signal=(hl == 7))
                            g8 = slice(8 * g, 8 * g + 8)
                            xg = xtok[:L, g * 512:(g + 1) * 512].rearrange("s (h p) -> s h p", p=HP)
                            Y = [t_ys]
                            fw.op(fw.dve, lambda: V.tensor_tensor(out=ys[:L, :].rearrange("s (h p) -> s h p", p=HP), in0=pf[3][:L, :].rearrange("s (h p) -> s h p", p=HP),
                                                                  in1=tk[:L, 2, g8].unsqueeze(2).to_broadcast([L, 8, HP]), op=ALU.mult), reads=[t_pf[3]] + K_, writes=Y)
                            fw.op(fw.dve, lambda: V.tensor_tensor(out=xd[:L, :].rearrange("s (h p) -> s h p", p=HP), in0=xg,
                                                                  in1=D_b[:L, g8].unsqueeze(2).to_broadcast([L, 8, HP]), op=ALU.mult), reads=[t_xtok, t_tab], writes=Y)
                            fw.op(fw.dve, lambda: V.tensor_tensor(out=ys[:L, :], in0=ys[:L, :], in1=xd[:L, :], op=ALU.add), reads=Y, writes=Y)
                            fw.op(fw.dve, lambda: V.tensor_tensor(out=ys[:L, :], in0=pf[2][:L, :], in1=ys[:L, :], op=ALU.add), reads=Y + [t_pf[2]], writes=Y)
                            fw.op(fw.dve, lambda: V.tensor_tensor(out=xsd[:L, :].rearrange("s (h p) -> s h p", p=HP), in0=xg,
                                                                  in1=tk[:L, 3, g8].unsqueeze(2).to_broadcast([L, 8, HP]), op=ALU.mult), reads=[t_xtok] + K_, writes=[t_xsd])
                            fw.op(fw.pe, lambda: P.matmul(pf[4][:, :], lhsT=xtok[:L, 4096 + g * 128:4096 + (g + 1) * 128], rhs=xsd[:L, :], start=True, stop=True),
                                  reads=[t_xtok, t_xsd], writes=[t_pf[4]])
                            fw.op(fw.dve, lambda: V.tensor_tensor(out=stT[:, g8, :], in0=stT[:, g8, :], in1=elb[:, g8].unsqueeze(2).to_broadcast([128, 8, HP]), op=ALU.mult),
                                  reads=[t_st[g], t_el], writes=[t_st[g]])
                            fw.op(fw.dve, lambda: V.tensor_tensor(out=stT[:, g8, :].rearrange("n h p -> n (h p)"), in0=stT[:, g8, :].rearrange("n h p -> n (h p)"), in1=pf[4][:, :], op=ALU.add),
                                  reads=[t_st[g], t_pf[4]], writes=[t_st[g]])
                            fw.op(fw.act, lambda: A.copy(out=stb[:, g8, :], in_=stT[:, g8, :]), reads=[t_st[g], t_pf[3]], writes=[t_stb[g]])
                            pz = pb[g % 2]; tpz = t_pb[g % 2]
                            for i4 in range(4):
                                fw.op(fw.pe, lambda: P.transpose(pz[:L, i4 * 128:(i4 + 1) * 128], zblk[:, 4 * g + i4, c0:c0 + L], identb[:, :]),
                                      reads=[t_zblk, t_tab], writes=[tpz], signal=(i4 == 3))
                            fw.op(fw.act, lambda: A.activation(out=gz[:L, :], in_=pz[:L, 0:512], func=AF.Tanh, scale=0.5), reads=[tpz], writes=[t_gz])
                            fw.op(fw.dve, lambda: V.scalar_tensor_tensor(out=gz[:L, :], in0=gz[:L, :], scalar=1.0, in1=pz[:L, 0:512], op0=ALU.add, op1=ALU.mult),
                                  reads=[t_gz, tpz], writes=[t_gz])
                            fw.op(fw.dve, lambda: V.scalar_tensor_tensor(out=yz[:L, g, :], in0=ys[:L, :], scalar=0.5, in1=gz[:L, :], op0=ALU.mult, op1=ALU.mult),
                                  reads=Y + [t_gz], writes=[t_yz])
                            fw.op(fw.act, lambda: A.activation(out=sq[:L, :], in_=yz[:L, g, :], func=AF.Square, accum_out=nst[:L, g:g + 1]),
                                  reads=[t_yz], writes=[t_sq, t_nst])
                        stage(24)
                        N_ = [t_nst]
                        fw.op(fw.act, lambda: A.activation(out=nst[:L, 8:16], in_=nst[:L, 0:8], func=AF.Ln, scale=1.0 / 512, bias=EPS), reads=N_, writes=N_)
                        fw.op(fw.act, lambda: A.activation(out=nst[:L, 16:24], in_=nst[:L, 8:16], func=AF.Exp, scale=-0.5), reads=N_, writes=N_)
                        for g in range(8):
                            fw.op(fw.dve, lambda: V.tensor_scalar(out=yn[:L, :], in0=yz[:L, g, :], scalar1=nst[:L, 16 + g:17 + g], scalar2=None, op0=ALU.mult),
                                  reads=[t_yz] + N_, writes=[t_yn])
                            py = pb[g % 2]; tpy = t_pb[g % 2]
                            for i4 in range(4):
                                fw.op(fw.pe, lambda: P.transpose(py[:, i4 * L:(i4 + 1) * L], yn[:L, i4 * 128:(i4 + 1) * 128], identb[:L, :L]),
                                      reads=[t_yn, t_tab], writes=[tpy], signal=(i4 == 3))
                            for i4 in range(4):
                                t = 4 * g + i4
                                fw.op(fw.act, lambda: A.activation(out=ob[:, t, c0:c0 + L], in_=py[:, i4 * L:(i4 + 1) * L], func=AF.Copy, scale=gn[:, t:t + 1]),
                                      reads=[tpy, t_tab], writes=[t_ob])
                    fw.dma(fw.act, ydst[:, :, tok0:tok0 + NB], ob[:], reads=[t_ob], writes=[t_y1], owner=t_ob)
        except StopEmit:
            pass
        fw.barrier()
        fw.stack = old


PARAM_SHAPES = {
    "ab_norm": [D], "ab_w_in": [D, 8192],
    "s5_lambda_re": [64, 64], "s5_lambda_im": [64, 64], "s5_log_dt": [64], "s5_b_re": [64, 64, 16], "s5_b_im": [64, 64, 16],
    "s5_c_re": [64, 16, 64], "s5_c_im": [64, 16, 64], "s5_d": [1024], "s5_glu_w": [1024, 1024], "s5_glu_b": [1024],
    "ml_conv_w": [4, 3072], "ml_conv_b": [3072], "ml_wq": [768, 4, 4], "ml_wk": [768, 4, 4], "ml_wv": [768, 4, 4],
    "ml_w_gate": [9216, 16], "ml_b_gate": [16], "ml_norm": [3072], "ml_skip": [3072], "ab_w_out": [4096, D],
    "ssd_norm": [D], "ssd_w_in": [D, 10304], "ssd_conv_w": [4, 6144], "ssd_conv_b": [6144], "ssd_dt_bias": [64],
    "ssd_a_log": [64], "ssd_d": [64], "ssd_gnorm": [4096], "ssd_w_out": [4096, D], "final_norm": [D],
}


def build_program():
    nc = bass.Bass("TRN2", target_bir_lowering=False)
    dI = lambda n, s, dt=F32: nc.dram_tensor(n, list(s), dt, kind="ExternalInput").ap()
    dS = lambda n, s, dt: nc.dram_tensor(n, list(s), dt, kind="Internal").ap()
    x = dI("x", [NSEQ, SEQ, D])
    meta = dI("meta_tokens", [NMETA, D])
    prm = {k: dI(k, v) for k, v in PARAM_SHAPES.items()}
    cst = {k: dI("c_" + k, v) for k, v in CONST_SHAPES.items()}
    out = nc.dram_tensor("out", [NSEQ, SEQ, D], F32, kind="ExternalOutput").ap()
    wb0 = dS("wb0", [D, 8192], BF16); glub = dS("glub", [1024, 1024], BF16); wo0 = dS("wo0", [4096, D], BF16)
    wb1 = dS("wb1", [D, 10304], BF16); wo1 = dS("wo1", [4096, D], BF16)
    p0T = dS("p0T", [8192, TTOT], BF16); y0T = dS("y0T", [4096, TTOT], BF16)
    h1 = dS("h1", [TTOT, D], F32)
    p1T = dS("p1T", [10304, TTOT], BF16); dtT = dS("dtT", [64, TTOT], F32); y1T = dS("y1T", [4096, TTOT], BF16)
    with contextlib.ExitStack() as st:
        fw = FW(nc, st)
        fw.consts = cst
        fw.ident_dram = cst["ident"]
        t_wb0, t_glub, t_wo0, t_wb1, t_wo1 = T("wb0"), T("glub"), T("wo0"), T("wb1"), T("wo1")
        t_p0, t_y0, t_h1, t_p1, t_y1, t_out = T("p0T"), T("y0T"), T("h1"), T("p1T"), T("y1T"), T("out")
        cast_weights(fw, wb0, prm["ab_w_in"], D, t_wb0)
        cast_weights(fw, glub, prm["s5_glu_w"], 1024, t_glub)
        cast_weights(fw, wo0, prm["ab_w_out"], 4096, t_wo0)
        cast_weights(fw, wb1, prm["ssd_w_in"], D, t_wb1)
        cast_weights(fw, wo1, prm["ssd_w_out"], 4096, t_wo1)

        def load_h0(seq, tt, dst, tr):
            t0 = tt * TS
            if tt == 0:
                fw.dma(fw.sp, dst[0:NMETA, :], meta, writes=[tr])
                fw.dma(fw.sp, dst[NMETA:TS, :], x[seq, 0:TS - NMETA, :], writes=[tr])
            else:
                fw.dma(fw.sp, dst[0:TS, :], x[seq, t0 - NMETA:t0 - NMETA + TS, :], writes=[tr])

        def load_h1(seq, tt, dst, tr):
            r0 = seq * TSEQ + tt * TS
            fw.dma(fw.sp, dst[0:TS, :], h1[r0:r0 + TS, :], reads=[t_h1], writes=[tr])

        def store_h1(seq, tt, src, tr):
            r0 = seq * TSEQ + tt * TS
            fw.dma(fw.act, h1[r0:r0 + TS, :], src[0:TS, :], reads=[tr], writes=[t_h1], owner=tr)

        def store_out(seq, tt, src, tr):
            t0 = tt * TS
            if tt == 0:
                fw.dma(fw.act, out[seq, 0:TS - NMETA, :], src[NMETA:TS, :], reads=[tr], writes=[t_out], owner=tr)
            else:
                fw.dma(fw.act, out[seq, t0 - NMETA:t0 - NMETA + TS, :], src[0:TS, :], reads=[tr], writes=[t_out], owner=tr)

        phases = [
            lambda: phase_in_proj(fw, load_h0, prm["ab_norm"], wb0, t_wb0, 8192, p0T, t_p0),
            lambda: phase_s5(fw, p0T, t_p0, prm, glub, t_glub, y0T, t_y0),
            lambda: phase_mlstm(fw, p0T, t_p0, prm, y0T, t_y0),
            lambda: phase_out_proj(fw, y0T, t_y0, wo0, t_wo0, load_h0, store_h1),
            lambda: phase_in_proj(fw, load_h1, prm["ssd_norm"], wb1, t_wb1, 10304, p1T, t_p1, dt_out=(dtT, 10240)),
            lambda: phase_ssd(fw, p1T, t_p1, dtT, prm, y1T, t_y1),
            lambda: phase_out_proj(fw, y1T, t_y1, wo1, t_wo1, load_h1, store_out, final_g=prm["final_norm"]),
        ]
        for i, ph in enumerate(phases):
            if i in DBG.get("skip_phases", ()):
                continue
            ph()
        fw.barrier()
        fw.finish(fw.sp, [t_out])
    return nc


_NC_CACHE = {}


def kernel(**inputs):
    n_cores = 8
    if "nc" not in _NC_CACHE:
        _NC_CACHE["nc"] = build_program()
    nc = _NC_CACHE["nc"]
    consts = host_consts()
    shared = {}
    for k in PARAM_SHAPES:
        a = np.asarray(inputs[k], dtype=np.float32)
        if k != "final_norm":
            a = a[0]
        shared[k] = np.ascontiguousarray(a)
    shared["meta_tokens"] = np.ascontiguousarray(np.asarray(inputs["meta_tokens"], dtype=np.float32))
    for k, v in consts.items():
        shared["c_" + k] = v
    xin = np.asarray(inputs["x"], dtype=np.float32)
    in_maps = []
    for c in range(n_cores):
        m = dict(shared)
        m["x"] = np.ascontiguousarray(xin[c * NSEQ:(c + 1) * NSEQ])
        in_maps.append(m)
    res = run_bass_kernel_spmd(nc, in_maps, core_ids=list(range(n_cores)))
    return np.concatenate([np.asarray(r["out"], dtype=np.float32) for r in res.results], axis=0)
```
